# Optimizing a Trainium2 kernel written in Bass

```python
import math
import jax, jax.numpy as jnp
from jax import lax
import numpy as np

D_MODEL = 1024
BATCH = 8
SEQ = 2048
DEPTH = 4

HEAD_DIM = 64
BLOCK = 128
RMS_EPS = 1e-6
SWA_Q_HEADS = 8
SWA_KV_HEADS = 2
SWA_WINDOW = 128
FOX_HEADS = 8
SSM_EXPAND = 2
SSM_D_INNER = SSM_EXPAND * D_MODEL
SSM_HEAD_DIM = 64
SSM_HEADS = SSM_D_INNER // SSM_HEAD_DIM
SSM_GROUPS = 4
SSM_STATE = 128
SSM_CONV = 4
SSM_CHUNK = 128
D_FF = 2816

N_ATTN_LAYERS = (DEPTH + 1) // 2
N_SSM_LAYERS = DEPTH // 2

SWA_Q_W = SWA_Q_HEADS * HEAD_DIM
SWA_KV_W = SWA_KV_HEADS * HEAD_DIM
FOX_W = FOX_HEADS * HEAD_DIM
ATTN_SPLITS = np.cumsum([SWA_Q_W, SWA_KV_W, SWA_KV_W, FOX_W, FOX_W, FOX_W]).tolist()
ATTN_IN_W = SWA_Q_W + 2 * SWA_KV_W + 3 * FOX_W + FOX_HEADS
ATTN_OUT_W = SWA_Q_W + FOX_W
SSM_BC_W = SSM_GROUPS * SSM_STATE
SSM_CONV_W = SSM_D_INNER + 2 * SSM_BC_W
SSM_IN_W = SSM_D_INNER + SSM_CONV_W + SSM_HEADS

kernel_name = "hybrid_swa_fox_mamba2_macaron"


def rmsnorm(x, g):
    xf = x.astype(jnp.float32)
    y = xf * lax.rsqrt(jnp.mean(xf * xf, axis=-1, keepdims=True) + RMS_EPS)
    return (y * g.astype(jnp.float32)).astype(x.dtype)


def swiglu(x, w_gate, w_up, w_down):
    return (jax.nn.silu(x @ w_gate) * (x @ w_up)) @ w_down


def sliding_window_sink_attention(q, k, v, sinks):
    b, s, hq, d = q.shape
    nb = s // BLOCK
    g = hq // SWA_KV_HEADS
    f32 = jnp.float32
    qb = q.astype(f32).reshape(b, nb, BLOCK, SWA_KV_HEADS, g, d)

    def with_prev(t):
        tb = t.astype(f32).reshape(b, nb, BLOCK, SWA_KV_HEADS, d)
        prev = jnp.pad(tb, ((0, 0), (1, 0), (0, 0), (0, 0), (0, 0)))[:, :-1]
        return jnp.concatenate([prev, tb], axis=2)

    kc, vc = with_prev(k), with_prev(v)
    scores = jnp.einsum('bnqkgd,bnskd->bnkgqs', qb, kc) * (d ** -0.5)
    qpos = jnp.arange(BLOCK)[:, None] + BLOCK
    kpos = jnp.arange(2 * BLOCK)[None, :]
    rel = qpos - kpos
    in_window = (rel >= 0) & (rel < SWA_WINDOW)
    blk = jnp.arange(nb)[:, None, None]
    valid_key = (blk * BLOCK + kpos[None] - BLOCK) >= 0
    mask = in_window[None] & valid_key
    scores = jnp.where(mask[None, :, None, None], scores, -jnp.inf)
    sink = sinks.astype(f32).reshape(1, 1, SWA_KV_HEADS, g, 1, 1)
    m = jnp.maximum(jnp.max(scores, axis=-1, keepdims=True), sink)
    p = jnp.exp(scores - m)
    denom = jnp.sum(p, axis=-1, keepdims=True) + jnp.exp(sink - m)
    o = jnp.einsum('bnkgqs,bnskd->bnkgqd', p, vc) / denom
    o = o.transpose(0, 1, 4, 2, 3, 5).reshape(b, s, hq * d)
    return o.astype(q.dtype)


def forgetting_attention(q, k, v, log_f):
    b, s, h, d = q.shape
    nb = s // BLOCK
    f32 = jnp.float32
    cum_f = jnp.cumsum(log_f, axis=1)
    kh = k.astype(f32).transpose(0, 2, 1, 3)
    vh = v.astype(f32).transpose(0, 2, 1, 3)
    f_k = cum_f.transpose(0, 2, 1)
    qb = q.astype(f32).reshape(b, nb, BLOCK, h, d).transpose(1, 0, 3, 2, 4)
    f_q = cum_f.reshape(b, nb, BLOCK, h).transpose(1, 0, 3, 2)
    kpos = jnp.arange(s)
    scale = d ** -0.5

    def one_block(args):
        qi, fqi, i = args
        sc = jnp.einsum('bhqd,bhsd->bhqs', qi, kh) * scale + fqi[..., None] - f_k[:, :, None, :]
        qpos = i * BLOCK + jnp.arange(BLOCK)
        sc = jnp.where(kpos[None, :] <= qpos[:, None], sc, -jnp.inf)
        p = jax.nn.softmax(sc, axis=-1)
        return jnp.einsum('bhqs,bhsd->bhqd', p, vh)

    o = lax.map(one_block, (qb, f_q, jnp.arange(nb)))
    return o.transpose(1, 0, 3, 2, 4).reshape(b, s, h * d).astype(q.dtype)


def attention_mixer(u, w_in, forget_bias, swa_q_norm, swa_k_norm, swa_sinks,
                    fox_q_norm, fox_k_norm, w_out):
    b, s, _ = u.shape
    proj = u @ w_in
    qa, ka, va, qf, kf, vf, fl = jnp.split(proj, ATTN_SPLITS, axis=-1)
    qa = rmsnorm(qa.reshape(b, s, SWA_Q_HEADS, HEAD_DIM), swa_q_norm)
    ka = rmsnorm(ka.reshape(b, s, SWA_KV_HEADS, HEAD_DIM), swa_k_norm)
    va = va.reshape(b, s, SWA_KV_HEADS, HEAD_DIM)
    o_a = sliding_window_sink_attention(qa, ka, va, swa_sinks)
    qf = rmsnorm(qf.reshape(b, s, FOX_HEADS, HEAD_DIM), fox_q_norm)
    kf = rmsnorm(kf.reshape(b, s, FOX_HEADS, HEAD_DIM), fox_k_norm)
    vf = vf.reshape(b, s, FOX_HEADS, HEAD_DIM)
    log_f = jax.nn.log_sigmoid(fl.astype(jnp.float32) + forget_bias.astype(jnp.float32))
    o_b = forgetting_attention(qf, kf, vf, log_f)
    return jnp.concatenate([o_a, o_b], axis=-1) @ w_out


def causal_depthwise_conv(x, w, bias):
    k, c = w.shape
    out = lax.conv_general_dilated(
        x, w[:, None, :].astype(x.dtype), window_strides=(1,), padding=((k - 1, 0),),
        dimension_numbers=('NWC', 'WIO', 'NWC'), feature_group_count=c)
    return out + bias


def ssd_chunked(xs, dt, a, b_in, c_in):
    bsz, s, h, p = xs.shape
    g, n = b_in.shape[2], b_in.shape[3]
    r = h // g
    nc, l = s // SSM_CHUNK, SSM_CHUNK
    f32 = jnp.float32
    xd = (xs.astype(f32) * dt[..., None]).reshape(bsz, nc, l, g, r, p)
    ad = (dt * a).reshape(bsz, nc, l, g, r).transpose(0, 3, 4, 1, 2)
    bc = b_in.astype(f32).reshape(bsz, nc, l, g, n)
    cc = c_in.astype(f32).reshape(bsz, nc, l, g, n)
    a_cs = jnp.cumsum(ad, axis=-1)
    causal = jnp.tril(jnp.ones((l, l), dtype=bool))
    seg = a_cs[..., :, None] - a_cs[..., None, :]
    decay_in = jnp.exp(jnp.where(causal, seg, -jnp.inf))
    cb = jnp.einsum('bclgn,bcsgn->bgcls', cc, bc)
    y_diag = jnp.einsum('bgrcls,bcsgrp->bclgrp', cb[:, :, None] * decay_in, xd)
    decay_to_end = jnp.exp(a_cs[..., -1:] - a_cs).transpose(0, 3, 4, 1, 2)
    states = jnp.einsum('bclgn,bclgrp->cbgrpn', bc, xd * decay_to_end[..., None])
    chunk_decay = jnp.exp(a_cs[..., -1]).transpose(3, 0, 1, 2)

    def step(hstate, inp):
        st, dec = inp
        return hstate * dec[..., None, None] + st, hstate

    h0 = jnp.zeros((bsz, g, r, p, n), f32)
    _, h_prev = lax.scan(step, h0, (states, chunk_decay))
    decay_from_start = jnp.exp(a_cs).transpose(0, 3, 4, 1, 2)
    y_off = jnp.einsum('bclgn,cbgrpn->bclgrp', cc, h_prev) * decay_from_start[..., None]
    return (y_diag + y_off).reshape(bsz, s, h, p)


def mamba2_mixer(u, w_in, conv_w, conv_b, dt_bias, a_log, d_skip, norm_g, w_out):
    b, s, _ = u.shape
    f32 = jnp.float32
    proj = u @ w_in
    z, xbc, dt = jnp.split(proj, [SSM_D_INNER, SSM_D_INNER + SSM_CONV_W], axis=-1)
    xbc = jax.nn.silu(causal_depthwise_conv(xbc, conv_w, conv_b))
    xs, b_in, c_in = jnp.split(xbc, [SSM_D_INNER, SSM_D_INNER + SSM_BC_W], axis=-1)
    dt = jax.nn.softplus(dt.astype(f32) + dt_bias.astype(f32))
    a = -jnp.exp(a_log.astype(f32))
    xs = xs.reshape(b, s, SSM_HEADS, SSM_HEAD_DIM)
    y = ssd_chunked(xs, dt, a,
                    b_in.reshape(b, s, SSM_GROUPS, SSM_STATE),
                    c_in.reshape(b, s, SSM_GROUPS, SSM_STATE))
    y = y + d_skip.astype(f32)[:, None] * xs.astype(f32)
    y = y.reshape(b, s, SSM_D_INNER) * jax.nn.silu(z.astype(f32))
    yg = y.reshape(b, s, SSM_GROUPS, SSM_D_INNER // SSM_GROUPS)
    yg = yg * lax.rsqrt(jnp.mean(yg * yg, axis=-1, keepdims=True) + RMS_EPS)
    y = yg.reshape(b, s, SSM_D_INNER) * norm_g.astype(f32)
    return y.astype(u.dtype) @ w_out


def setup_inputs(seed: int = 0) -> dict:
    key = jax.random.key(seed)
    ks = iter(jax.random.split(key, 40))
    f32 = jnp.float32

    def nrm(shape, scale):
        return jax.random.normal(next(ks), shape, f32) * scale

    def gain(shape):
        return 1.0 + nrm(shape, 0.02)

    def uni(shape, lo, hi):
        return jax.random.uniform(next(ks), shape, f32, minval=lo, maxval=hi)

    L, NA, NS = DEPTH, N_ATTN_LAYERS, N_SSM_LAYERS
    dt0 = jnp.exp(uni((NS, SSM_HEADS), math.log(1e-3), math.log(1e-1)))
    return {
        "x": nrm((BATCH, SEQ, D_MODEL), 1.0),
        "ffn1_norm": gain((L, D_MODEL)),
        "ffn1_w_gate": nrm((L, D_MODEL, D_FF), D_MODEL ** -0.5),
        "ffn1_w_up": nrm((L, D_MODEL, D_FF), D_MODEL ** -0.5),
        "ffn1_w_down": nrm((L, D_FF, D_MODEL), D_FF ** -0.5),
        "mix_norm": gain((L, D_MODEL)),
        "ffn2_norm": gain((L, D_MODEL)),
        "ffn2_w_gate": nrm((L, D_MODEL, D_FF), D_MODEL ** -0.5),
        "ffn2_w_up": nrm((L, D_MODEL, D_FF), D_MODEL ** -0.5),
        "ffn2_w_down": nrm((L, D_FF, D_MODEL), D_FF ** -0.5),
        "attn_w_in": nrm((NA, D_MODEL, ATTN_IN_W), D_MODEL ** -0.5),
        "attn_forget_bias": uni((NA, FOX_HEADS), 2.0, 5.0),
        "swa_q_norm": gain((NA, HEAD_DIM)),
        "swa_k_norm": gain((NA, HEAD_DIM)),
        "swa_sinks": nrm((NA, SWA_Q_HEADS), 0.5),
        "fox_q_norm": gain((NA, HEAD_DIM)),
        "fox_k_norm": gain((NA, HEAD_DIM)),
        "attn_w_out": nrm((NA, ATTN_OUT_W, D_MODEL), ATTN_OUT_W ** -0.5),
        "ssm_w_in": nrm((NS, D_MODEL, SSM_IN_W), D_MODEL ** -0.5),
        "ssm_conv_w": nrm((NS, SSM_CONV, SSM_CONV_W), SSM_CONV ** -0.5),
        "ssm_conv_b": nrm((NS, SSM_CONV_W), 0.02),
        "ssm_dt_bias": dt0 + jnp.log(-jnp.expm1(-dt0)),
        "ssm_a_log": jnp.log(uni((NS, SSM_HEADS), 1.0, 16.0)),
        "ssm_d_skip": gain((NS, SSM_HEADS)),
        "ssm_norm": gain((NS, SSM_D_INNER)),
        "ssm_w_out": nrm((NS, SSM_D_INNER, D_MODEL), SSM_D_INNER ** -0.5),
    }


def reference(x, ffn1_norm, ffn1_w_gate, ffn1_w_up, ffn1_w_down, mix_norm,
              ffn2_norm, ffn2_w_gate, ffn2_w_up, ffn2_w_down,
              attn_w_in, attn_forget_bias, swa_q_norm, swa_k_norm, swa_sinks,
              fox_q_norm, fox_k_norm, attn_w_out,
              ssm_w_in, ssm_conv_w, ssm_conv_b, ssm_dt_bias, ssm_a_log,
              ssm_d_skip, ssm_norm, ssm_w_out):
    h = x
    for layer in range(DEPTH):
        h = h + 0.5 * swiglu(rmsnorm(h, ffn1_norm[layer]),
                             ffn1_w_gate[layer], ffn1_w_up[layer], ffn1_w_down[layer])
        u = rmsnorm(h, mix_norm[layer])
        i = layer // 2
        if layer % 2 == 0:
            h = h + attention_mixer(u, attn_w_in[i], attn_forget_bias[i],
                                    swa_q_norm[i], swa_k_norm[i], swa_sinks[i],
                                    fox_q_norm[i], fox_k_norm[i], attn_w_out[i])
        else:
            h = h + mamba2_mixer(u, ssm_w_in[i], ssm_conv_w[i], ssm_conv_b[i],
                                 ssm_dt_bias[i], ssm_a_log[i], ssm_d_skip[i],
                                 ssm_norm[i], ssm_w_out[i])
        h = h + 0.5 * swiglu(rmsnorm(h, ffn2_norm[layer]),
                             ffn2_w_gate[layer], ffn2_w_up[layer], ffn2_w_down[layer])
    return h
```

```python
from contextlib import ExitStack
import numpy as np
import concourse.bass as bass
import concourse.mybir as mybir
from concourse.bass_utils import run_bass_kernel_spmd

F32 = mybir.dt.float32
BF16 = mybir.dt.bfloat16
AF = mybir.ActivationFunctionType
ALU = mybir.AluOpType
AX = mybir.AxisListType

D = 1024
S = 2048
DFF = 2816
NF = DFF // 128
EPS = 1e-6
ATT_W = 2312
SSM_W = 5152
NEG = -30000.0

INPUT_SHAPES = {
    "ffn1_norm": [4, 1024], "ffn1_w_gate": [4, 1024, 2816], "ffn1_w_up": [4, 1024, 2816],
    "ffn1_w_down": [4, 2816, 1024], "mix_norm": [4, 1024], "ffn2_norm": [4, 1024],
    "ffn2_w_gate": [4, 1024, 2816], "ffn2_w_up": [4, 1024, 2816], "ffn2_w_down": [4, 2816, 1024],
    "attn_w_in": [2, 1024, 2312], "attn_forget_bias": [2, 8], "swa_q_norm": [2, 64],
    "swa_k_norm": [2, 64], "swa_sinks": [2, 8], "fox_q_norm": [2, 64], "fox_k_norm": [2, 64],
    "attn_w_out": [2, 1024, 1024], "ssm_w_in": [2, 1024, 5152], "ssm_conv_w": [2, 4, 3072],
    "ssm_conv_b": [2, 3072], "ssm_dt_bias": [2, 32], "ssm_a_log": [2, 32], "ssm_d_skip": [2, 32],
    "ssm_norm": [2, 2048], "ssm_w_out": [2, 2048, 1024],
}


class Dep:
    __slots__ = ("w", "r")

    def __init__(self):
        self.w = None
        self.r = []


class Queue:
    def __init__(self, name, sem, is_pe=False):
        self.name = name
        self.sem = sem
        self.count = 0
        self.ops = []
        self.waited = {}
        self.is_pe = is_pe


class Sched:
    def __init__(self, nc, stack):
        self.nc = nc
        self.stack = stack
        self.q = {}
        for name in ("pe", "act", "dve", "pool", "sp"):
            sem = stack.enter_context(nc.semaphore("s_" + name))
            self.q[name] = Queue(name, sem, name == "pe")
        self.dma_sems = {}

    def dma_sem(self, name):
        if name not in self.dma_sems:
            sem = self.stack.enter_context(self.nc.semaphore("d_" + name))
            self.dma_sems[name] = [sem, 0]
        return self.dma_sems[name]

    def _collect(self, q, reads, writes, dma_group=None):
        need = {}

        def add(tok):
            if tok is None:
                return
            sem, val, owner, grp = tok
            if owner == q.name and q.is_pe:
                return
            if dma_group is not None and grp is dma_group:
                return
            k = id(sem)
            if q.waited.get(k, 0) >= val:
                return
            if k not in need or need[k][1] < val:
                need[k] = (sem, val)

        for d in reads:
            add(d.w)
        for d in writes:
            add(d.w)
            for t in d.r:
                add(t)
        waits = list(need.values())
        for sem, val in waits:
            q.waited[id(sem)] = val
        return waits

    def op(self, qname, fn, reads=(), writes=(), signal=True):
        q = self.q[qname]
        waits = self._collect(q, reads, writes)
        tok = (q.sem, q.count + 1, q.name, None)
        if signal:
            q.count += 1
        q.ops.append((waits, fn, (q.sem, 1) if signal else None))
        for d in reads:
            d.r.append(tok)
        for d in writes:
            d.w = tok
            d.r = []
        return tok

    def dma(self, semname, out, in_, reads=(), writes=(), qname="sp"):
        q = self.q[qname]
        ent = self.dma_sem(semname)
        waits = self._collect(q, reads, writes, dma_group=ent)
        ent[1] += 16
        tok = (ent[0], ent[1], None, ent)
        q.ops.append((waits, lambda e: e.dma_start(out=out, in_=in_), (ent[0], 16)))
        for d in reads:
            d.r.append(tok)
        for d in writes:
            d.w = tok
            d.r = []
        return tok

    def barrier(self):
        toks = [(q.sem, q.count) for q in self.q.values() if q.count > 0]
        toks += [(ent[0], ent[1]) for ent in self.dma_sems.values() if ent[1] > 0]
        for q in self.q.values():
            waits = []
            for sem, val in toks:
                if sem is q.sem and q.is_pe:
                    continue
                if q.waited.get(id(sem), 0) >= val:
                    continue
                q.waited[id(sem)] = val
                waits.append((sem, val))
            q.ops.append((waits, None, None))

    def finish(self, final_tokens):
        nc = self.nc
        fin = {}
        for sem, val, owner, grp in final_tokens:
            k = id(sem)
            if k not in fin or fin[k][1] < val:
                fin[k] = (sem, val)
        engs = {"pe": "tensor", "act": "scalar", "dve": "vector", "pool": "gpsimd", "sp": "sync"}
        with nc.Block() as block:
            for name, attr in engs.items():
                q = self.q[name]

                def body(eng, q=q, name=name):
                    for waits, fn, inc in q.ops:
                        for sem, val in waits:
                            eng.wait_ge(sem, val)
                        if fn is None:
                            continue
                        ins = fn(eng)
                        if inc is not None:
                            ins.then_inc(inc[0], inc[1])
                    if name == "sp":
                        for sem, val in fin.values():
                            eng.wait_ge(sem, val)

                getattr(block, attr)(body)


class Builder:
    def __init__(self, phases):
        self.phases = phases
        self.nc = bass.Bass("TRN2", target_bir_lowering=False)

    def build(self):
        nc = self.nc
        self.din = {}
        self.x = nc.dram_tensor("x", [S, D], F32, kind="ExternalInput").ap()
        for k, shp in INPUT_SHAPES.items():
            self.din[k] = nc.dram_tensor(k, shp, F32, kind="ExternalInput").ap()
        self.y = nc.dram_tensor("y", [S, D], F32, kind="ExternalOutput").ap()
        with ExitStack() as st:
            self.st = st
            self.S = Sched(nc, st)
            self.alloc()
            self.consts()
            self.load_x()
            for ph in self.phases:
                kind, layer = ph
                self.S.barrier()
                if kind == "ffn1":
                    self.ffn(layer, "ffn1")
                elif kind == "ffn2":
                    self.ffn(layer, "ffn2")
                elif kind == "attn":
                    self.attn(layer // 2, layer)
                elif kind == "ssm":
                    self.ssm(layer // 2, layer)
            self.S.barrier()
            toks = self.store_y()
            self.S.finish(toks)
        return nc

    def sb(self, name, shape, dtype):
        return self.st.enter_context(self.nc.sbuf_tensor(name, shape, dtype))

    def alloc(self):
        nc = self.nc
        self.hT = self.sb("hT", [128, 8, S], F32)
        self.hdep = [[Dep() for _ in range(4)] for _ in range(8)]
        self.xn = self.sb("xn", [128, 8, S], BF16)
        self.xdep = [Dep() for _ in range(4)]
        self.bank = [self.st.enter_context(nc.psum_tensor("pb%d" % i, [128, 512], F32)) for i in range(8)]
        self.bdep = [Dep() for _ in range(8)]
        self.SCR = 23 * 1024 + 512
        self.scr = self.sb("scr", [128, self.SCR], F32)
        self.ident_f = self.sb("ident_f", [128, 128], F32)
        self.ident_b = self.sb("ident_b", [128, 128], BF16)
        self.ones_b = self.sb("ones_b", [128, 128], BF16)
        self.cdep = Dep()
        self.sq = self.sb("sq", [128, 8, 512], BF16)
        self.sqdep = Dep()
        self.rstd = self.sb("rstd", [128, 512], F32)
        self.rstdep = Dep()
        self.gains = self.sb("gains", [128, 12, 8], F32)
        self.gdep = Dep()
        self.sg = [self.sb("sg%d" % i, [128, 512], F32) for i in range(2)]
        self.sgdep = [Dep(), Dep()]
        self.io = [self.scr[:, i * 1024:(i + 1) * 1024] for i in range(2)]
        self.iodep = [Dep(), Dep()]

    def view(self, off_words, shape, dtype):
        n = 1
        for s in shape[1:]:
            n *= s
        if dtype == BF16:
            words = (n + 1) // 2
            ap = self.scr[:, off_words:off_words + words].bitcast(BF16)
        else:
            words = n
            ap = self.scr[:, off_words:off_words + words]
        assert off_words + words <= self.SCR, (off_words, words)
        if len(shape) == 3:
            ap = ap.rearrange("p (a b) -> p a b", a=shape[1])
        elif len(shape) == 4:
            ap = ap.rearrange("p (a b c) -> p a b c", a=shape[1], b=shape[2])
        if shape[0] != 128:
            ap = ap[0:shape[0]]
        return ap, off_words + words

    def consts(self):
        Sx = self.S
        nc = self.nc
        idf, idb, ones = self.ident_f, self.ident_b, self.ones_b
        Sx.op("pool", lambda e: e.memset(idf[:], 0.0), writes=[self.cdep])
        Sx.op("pool", lambda e: e.affine_select(out=idf[:], in_=idf[:], pattern=[[-1, 128]],
                                                compare_op=ALU.not_equal, fill=1.0, base=0,
                                                channel_multiplier=1), writes=[self.cdep])
        Sx.op("dve", lambda e: e.tensor_copy(out=idb[:], in_=idf[:]), reads=[self.cdep], writes=[self.cdep])
        Sx.op("dve", lambda e: e.memset(ones[:], 1.0), writes=[self.cdep])

    def _small_dma(self, out, in_):
        nc = self.nc

        def fn(e):
            with nc.allow_non_contiguous_dma(reason="tiny vectors"):
                return e.dma_start(out=out, in_=in_)
        return fn

    def small_dma(self, semname, out, in_, writes):
        Sx = self.S
        q = Sx.q["sp"]
        ent = Sx.dma_sem(semname)
        waits = Sx._collect(q, (), writes, dma_group=ent)
        ent[1] += 16
        tok = (ent[0], ent[1], None, ent)
        q.ops.append((waits, self._small_dma(out, in_), (ent[0], 16)))
        for d in writes:
            d.w = tok
            d.r = []
        return tok

    def load_x(self):
        Sx = self.S
        for n, name in enumerate(("ffn1_norm", "mix_norm", "ffn2_norm")):
            src = self.din[name].rearrange("l (c p) -> p l c", p=128)
            self.small_dma("gains", self.gains[:, n * 4:(n + 1) * 4, :], src, [self.gdep])
        for tt in range(16):
            io, iod = self.io[tt % 2], self.iodep[tt % 2]
            Sx.dma("xin%d" % (tt % 2), io[:], self.x[tt * 128:(tt + 1) * 128, :], writes=[iod])
            for half in range(2):
                b = 6 + half
                ps, pd = self.bank[b], self.bdep[b]
                for c4 in range(4):
                    c = half * 4 + c4
                    Sx.op("pe", lambda e, ps=ps, io=io, c=c, c4=c4: e.transpose(
                        out=ps[:, c4 * 128:(c4 + 1) * 128], in_=io[:, c * 128:(c + 1) * 128],
                        identity=self.ident_f[:]), reads=[iod, self.cdep], writes=[pd], signal=(c4 == 3))
                tb = tt // 4
                dst = self.hT[:, half * 4:half * 4 + 4, tt * 128:(tt + 1) * 128]
                src = ps[:].rearrange("p (a b) -> p a b", a=4)
                eng = "act" if half == 0 else "dve"
                if eng == "act":
                    fn = lambda e, dst=dst, src=src: e.activation(out=dst, in_=src, func=AF.Copy)
                else:
                    fn = lambda e, dst=dst, src=src: e.tensor_copy(out=dst, in_=src)
                Sx.op(eng, fn, reads=[pd], writes=[self.hdep[half * 4 + c4][tb] for c4 in range(4)])

    def store_y(self):
        Sx = self.S
        toks = []
        for tt in range(16):
            io, iod = self.io[tt % 2], self.iodep[tt % 2]
            tb = tt // 4
            for half in range(2):
                b = 6 + half
                ps, pd = self.bank[b], self.bdep[b]
                for c4 in range(4):
                    c = half * 4 + c4
                    Sx.op("pe", lambda e, ps=ps, c=c, c4=c4, tt=tt: e.transpose(
                        out=ps[:, c4 * 128:(c4 + 1) * 128], in_=self.hT[:, c, tt * 128:(tt + 1) * 128],
                        identity=self.ident_f[:]), reads=[self.hdep[c][tb], self.cdep], writes=[pd],
                        signal=(c4 == 3))
                dst = io[:, half * 512:(half + 1) * 512]
                if half == 0:
                    fn = lambda e, dst=dst, ps=ps: e.activation(out=dst, in_=ps[:], func=AF.Copy)
                    Sx.op("act", fn, reads=[pd], writes=[iod])
                else:
                    fn = lambda e, dst=dst, ps=ps: e.tensor_copy(out=dst, in_=ps[:])
                    Sx.op("dve", fn, reads=[pd], writes=[iod])
            toks.append(Sx.dma("yout%d" % (tt % 2), self.y[tt * 128:(tt + 1) * 128, :], io[:], reads=[iod]))
        return toks

    def rmsnorm(self, gidx):
        Sx = self.S
        nb = 5
        for tb in range(4):
            ts = slice(tb * 512, (tb + 1) * 512)
            for c in range(8):
                Sx.op("act", lambda e, c=c, ts=ts: e.activation(out=self.sq[:, c, :], in_=self.hT[:, c, ts],
                                                                func=AF.Square),
                      reads=[self.hdep[c][tb]], writes=[self.sqdep])
            ps, pd = self.bank[nb], self.bdep[nb]
            for c in range(8):
                Sx.op("pe", lambda e, c=c, ps=ps: e.matmul(ps[:], self.ones_b[:], self.sq[:, c, :],
                                                            start=(c == 0), stop=(c == 7)),
                      reads=[self.sqdep, self.cdep], writes=[pd], signal=(c == 7))
            Sx.op("act", lambda e, ps=ps: e.activation(out=self.rstd[:], in_=ps[:], func=AF.Sqrt,
                                                        bias=EPS, scale=1.0 / D),
                  reads=[pd], writes=[self.rstdep])
            Sx.op("dve", lambda e: e.reciprocal(out=self.rstd[:], in_=self.rstd[:]),
                  reads=[self.rstdep], writes=[self.rstdep])
            for c in range(8):
                Sx.op("dve", lambda e, c=c, ts=ts: e.scalar_tensor_tensor(
                    out=self.xn[:, c, ts], in0=self.hT[:, c, ts], scalar=self.gains[:, gidx, c:c + 1],
                    in1=self.rstd[:], op0=ALU.mult, op1=ALU.mult),
                    reads=[self.hdep[c][tb], self.rstdep, self.gdep], writes=[self.xdep[tb]])

    def ffn(self, layer, which):
        Sx = self.S
        gidx = (0 if which == "ffn1" else 2) * 4 + layer
        self.rmsnorm(gidx)
        wg = self.din[which + "_w_gate"][layer].rearrange("(kc p) f -> p kc f", p=128)
        wu = self.din[which + "_w_up"][layer].rearrange("(kc p) f -> p kc f", p=128)
        wd = self.din[which + "_w_down"][layer].rearrange("(fc p) d -> p fc d", p=128)
        off = 0
        hff, off = self.view(off, [128, NF, 1024], BF16)
        hfdep = [[Dep() for _ in range(2)] for _ in range(NF)]
        gst, ust, gud = [], [], []
        for i in range(3):
            a, off = self.view(off, [128, 8, 256], BF16)
            b, off = self.view(off, [128, 8, 256], BF16)
            gst.append(a)
            ust.append(b)
            gud.append(Dep())
        dst, ddd = [], []
        for i in range(2):
            a, off = self.view(off, [128, NF, 256], BF16)
            dst.append(a)
            ddd.append(Dep())
        step = 0
        for blk in range(2):
            for grp in range(NF // 2):
                si = grp % 3
                cs = slice(grp * 256, (grp + 1) * 256)
                Sx.dma("wgu%d" % si, gst[si], wg[:, :, cs], writes=[gud[si]], qname="pool")
                Sx.dma("wgu%d" % si, ust[si], wu[:, :, cs], writes=[gud[si]], qname="pool")
                for cc in range(2):
                    j = grp * 2 + cc
                    for half in range(2):
                        tb = blk * 2 + half
                        ts = slice(tb * 512, (tb + 1) * 512)
                        gb, ub = step % 2, 2 + step % 2
                        gps, ups = self.bank[gb], self.bank[ub]
                        for kc in range(8):
                            Sx.op("pe", lambda e, gps=gps, si=si, kc=kc, cc=cc, ts=ts: e.matmul(
                                gps[:], gst[si][:, kc, cc * 128:(cc + 1) * 128], self.xn[:, kc, ts],
                                start=(kc == 0), stop=(kc == 7)),
                                reads=[gud[si], self.xdep[tb]], writes=[self.bdep[gb]], signal=(kc == 7))
                        for kc in range(8):
                            Sx.op("pe", lambda e, ups=ups, si=si, kc=kc, cc=cc, ts=ts: e.matmul(
                                ups[:], ust[si][:, kc, cc * 128:(cc + 1) * 128], self.xn[:, kc, ts],
                                start=(kc == 0), stop=(kc == 7)),
                                reads=[gud[si], self.xdep[tb]], writes=[self.bdep[ub]], signal=(kc == 7))
                        sg, sgd = self.sg[step % 2], self.sgdep[step % 2]
                        Sx.op("act", lambda e, sg=sg, gps=gps: e.activation(out=sg[:], in_=gps[:], func=AF.Silu),
                              reads=[self.bdep[gb]], writes=[sgd])
                        Sx.op("dve", lambda e, sg=sg, ups=ups, j=j, half=half: e.tensor_tensor(
                            out=hff[:, j, half * 512:(half + 1) * 512], in0=ups[:], in1=sg[:], op=ALU.mult),
                            reads=[self.bdep[ub], sgd], writes=[hfdep[j][half]])
                        step += 1
            for dg in range(4):
                si = dg % 2
                Sx.dma("wd%d" % si, dst[si], wd[:, :, dg * 256:(dg + 1) * 256], writes=[ddd[si]], qname="pool")
                for cc in range(2):
                    dc = dg * 2 + cc
                    for half in range(2):
                        tb = blk * 2 + half
                        ts = slice(tb * 512, (tb + 1) * 512)
                        b = 4 + step % 2
                        ps = self.bank[b]
                        for f in range(NF):
                            Sx.op("pe", lambda e, ps=ps, si=si, f=f, cc=cc, half=half: e.matmul(
                                ps[:], dst[si][:, f, cc * 128:(cc + 1) * 128],
                                hff[:, f, half * 512:(half + 1) * 512],
                                start=(f == 0), stop=(f == NF - 1)),
                                reads=[ddd[si], hfdep[f][half]], writes=[self.bdep[b]], signal=(f == NF - 1))
                        Sx.op("dve", lambda e, ps=ps, dc=dc, ts=ts: e.scalar_tensor_tensor(
                            out=self.hT[:, dc, ts], in0=ps[:], scalar=0.5, in1=self.hT[:, dc, ts],
                            op0=ALU.mult, op1=ALU.add),
                            reads=[self.bdep[b], self.hdep[dc][tb]], writes=[self.hdep[dc][tb]])
                        step += 1

    def attn(self, i, layer):
        Sx = self.S
        nc = self.nc
        self.rmsnorm(4 + layer)
        win = self.din["attn_w_in"][i].rearrange("(kc p) f -> p kc f", p=128)
        wout = self.din["attn_w_out"][i]
        off = 0
        og, off = self.view(off, [128, 4, S], BF16)
        ogd = [[Dep() for _ in range(4)] for _ in range(4)]
        qh, off = self.view(off, [128, S], BF16)
        kh, off = self.view(off, [128, S], BF16)
        qd, kd = Dep(), Dep()
        vaug, off = self.view(off, [128, 16, 66], BF16)
        vd = Dep()
        wq, off = self.view(off, [128, 8, 64], BF16)
        wk, off = self.view(off, [128, 8, 64], BF16)
        wv, off = self.view(off, [128, 8, 64], BF16)
        wfl, off = self.view(off, [128, 8, 8], BF16)
        wqd, wkd, wvd, wfd = Dep(), Dep(), Dep(), Dep()
        nl, off = self.view(off, [128, S], F32)
        onesrow, off = self.view(off, [128, S], F32)
        ghi, off = self.view(off, [128, S], BF16)
        glo, off = self.view(off, [128, S], BF16)
        gtok, off = self.view(off, [128, 16], F32)
        nld, ghd, gtd = Dep(), Dep(), Dep()
        pb, pbd = [], []
        for _ in range(2):
            a, off = self.view(off, [128, 512], BF16)
            pb.append(a)
            pbd.append(Dep())
        otsb, off = self.view(off, [128, 512], F32)
        rc, off = self.view(off, [128, 512], F32)
        otd, rcd = Dep(), Dep()
        mtmp, off = self.view(off, [128, 512], F32)
        mfox, off = self.view(off, [128, 4, 512], BF16)
        mswa, off = self.view(off, [128, 5, 512], BF16)
        md = Dep()
        wo, off = self.view(off, [128, 4, 1024], BF16)
        wod = Dep()
        sm, off = self.view(off, [128, 32], F32)
        smd = Dep()
        sel, off = self.view(off, [128, 64], F32)
        hg = sm[:, 0:4]
        esink = sm[:, 4:12]
        negfb = sm[:, 12:20]
        onef = sm[:, 20:21]

        Sx.op("pool", lambda e: e.memset(vaug[:, :, 64:66], 1.0), writes=[vd])
        Sx.op("pool", lambda e: e.memset(onesrow[0:1, :], 1.0), writes=[nld])
        Sx.op("pool", lambda e: e.memset(sm[:, 20:21], 1.0), writes=[smd])
        Sx.op("pool", lambda e: e.memset(sel[:], 0.0), writes=[smd])
        Sx.op("pool", lambda e: e.memset(sel[64:65, :], 1.0), writes=[smd])
        for j in range(4):
            Sx.op("pool", lambda e: e.memset(mtmp[:], 0.0), writes=[md])
            Sx.op("pool", lambda e, j=j: e.affine_select(out=mtmp[:], in_=mtmp[:], pattern=[[1, 512]],
                                                         compare_op=ALU.is_ge, fill=NEG, base=-128 * j,
                                                         channel_multiplier=-1), writes=[md])
            Sx.op("pool", lambda e, j=j: e.tensor_copy(out=mfox[:, j, :], in_=mtmp[:]), writes=[md])
        for jj in range(5):
            j = jj - 1
            Sx.op("pool", lambda e: e.memset(mtmp[:], 0.0), writes=[md])
            Sx.op("pool", lambda e, j=j: e.affine_select(out=mtmp[:], in_=mtmp[:], pattern=[[1, 512]],
                                                         compare_op=ALU.is_ge, fill=NEG, base=-128 * j,
                                                         channel_multiplier=-1), writes=[md])
            Sx.op("pool", lambda e, j=j: e.affine_select(out=mtmp[:], in_=mtmp[:], pattern=[[-1, 512]],
                                                         compare_op=ALU.is_ge, fill=NEG, base=127 + 128 * j,
                                                         channel_multiplier=1), writes=[md])
            Sx.op("pool", lambda e, jj=jj: e.tensor_copy(out=mswa[:, jj, :], in_=mtmp[:]), writes=[md])
        for n, name in enumerate(("swa_q_norm", "swa_k_norm", "fox_q_norm", "fox_k_norm")):
            self.small_dma("asm", sm[0:64, n:n + 1], self.din[name][i].rearrange("(d o) -> d o", o=1), [smd])
        self.small_dma("asm", sm[0:64, 4:12], self.din["swa_sinks"][i:i + 1, :].to_broadcast([64, 8]), [smd])
        self.small_dma("asm", sm[0:1, 12:20], self.din["attn_forget_bias"][i:i + 1, :], [smd])
        Sx.op("act", lambda e: e.activation(out=sm[0:64, 4:12], in_=sm[0:64, 4:12], func=AF.Exp),
              reads=[smd], writes=[smd])
        Sx.op("dve", lambda e: e.tensor_scalar(out=sm[0:1, 12:20], in0=sm[0:1, 12:20], scalar1=-1.0, scalar2=None,
                                               op0=ALU.mult), reads=[smd], writes=[smd])
        Sx.dma("wfl", wfl, win[:, :, 2304:2312], writes=[wfd], qname="pool")

        B_ST, B_OT, B_PJ, B_SS, B_MS, B_OP = (0, 1), (2, 3), 4, 5, 6, 7
        cnt = {"st": 0, "ot": 0}

        def proj_norm(wst, wdep, gcol, dst, ddep):
            for tb in range(4):
                ts = slice(tb * 512, (tb + 1) * 512)
                ps = self.bank[B_PJ]
                for kc in range(8):
                    Sx.op("pe", lambda e, kc=kc, ts=ts, ps=ps: e.matmul(ps[0:64, :], wst[:, kc, :], self.xn[:, kc, ts],
                                                                         start=(kc == 0), stop=(kc == 7)),
                          reads=[wdep, self.xdep[tb]], writes=[self.bdep[B_PJ]], signal=(kc == 7))
                Sx.op("act", lambda e, ps=ps: e.activation(out=self.sq[0:64, 0, :], in_=ps[0:64, :], func=AF.Square),
                      reads=[self.bdep[B_PJ]], writes=[self.sqdep])
                p2 = self.bank[B_SS]
                Sx.op("pe", lambda e, p2=p2: e.matmul(p2[0:64, :], self.ones_b[0:64, 0:64], self.sq[0:64, 0, :],
                                                       start=True, stop=True),
                      reads=[self.sqdep, self.cdep], writes=[self.bdep[B_SS]])
                Sx.op("act", lambda e, p2=p2: e.activation(out=self.rstd[0:64, :], in_=p2[0:64, :], func=AF.Sqrt,
                                                            bias=EPS, scale=1.0 / 64),
                      reads=[self.bdep[B_SS]], writes=[self.rstdep])
                Sx.op("dve", lambda e: e.reciprocal(out=self.rstd[0:64, :], in_=self.rstd[0:64, :]),
                      reads=[self.rstdep], writes=[self.rstdep])
                Sx.op("dve", lambda e, ps=ps, ts=ts: e.scalar_tensor_tensor(
                    out=dst[0:64, ts], in0=ps[0:64, :], scalar=sm[0:64, gcol:gcol + 1], in1=self.rstd[0:64, :],
                    op0=ALU.mult, op1=ALU.mult),
                    reads=[self.bdep[B_PJ], self.rstdep, smd], writes=[ddep])

        def proj_v():
            for t4 in range(4):
                ps = self.bank[B_MS]
                for tl in range(4):
                    t = t4 * 4 + tl
                    for kc in range(8):
                        Sx.op("pe", lambda e, kc=kc, t=t, tl=tl, ps=ps: e.matmul(
                            ps[:, tl * 64:(tl + 1) * 64], self.xn[:, kc, t * 128:(t + 1) * 128], wv[:, kc, :],
                            start=(kc == 0), stop=(kc == 7)),
                            reads=[wvd, self.xdep[t // 4]], writes=[self.bdep[B_MS]],
                            signal=(kc == 7 and tl == 3))
                Sx.op("act", lambda e, ps=ps, t4=t4: e.activation(
                    out=vaug[:, t4 * 4:(t4 + 1) * 4, 0:64], in_=ps[:, 0:256].rearrange("p (a b) -> p a b", a=4),
                    func=AF.Copy), reads=[self.bdep[B_MS]], writes=[vd])

        def fox_gates(h):
            for tb in range(4):
                ts = slice(tb * 512, (tb + 1) * 512)
                ps = self.bank[B_MS]
                for kc in range(8):
                    Sx.op("pe", lambda e, kc=kc, ts=ts, ps=ps: e.matmul(ps[0:1, :], wfl[:, kc, h:h + 1], self.xn[:, kc, ts],
                                                                         start=(kc == 0), stop=(kc == 7)),
                          reads=[wfd, self.xdep[tb]], writes=[self.bdep[B_MS]], signal=(kc == 7))
                Sx.op("act", lambda e, ps=ps, ts=ts: e.activation(out=nl[0:1, ts], in_=ps[0:1, :], func=AF.Exp,
                                                                  bias=sm[0:1, 12 + h:13 + h], scale=-1.0),
                      reads=[self.bdep[B_MS], smd], writes=[nld])
            Sx.op("act", lambda e: e.activation(out=nl[0:1, :], in_=nl[0:1, :], func=AF.Ln, bias=1.0, scale=1.0),
                  reads=[nld], writes=[nld])
            Sx.op("dve", lambda e: e.tensor_tensor_scan(out=nl[0:1, :], data0=onesrow[0:1, :], data1=nl[0:1, :],
                                                        initial=0.0, op0=ALU.mult, op1=ALU.add),
                  reads=[nld], writes=[nld])
            Sx.op("dve", lambda e: e.tensor_scalar(out=ghi[0:1, :], in0=nl[0:1, :], scalar1=-8.0, scalar2=None,
                                                   op0=ALU.mult), reads=[nld], writes=[ghd])
            Sx.op("dve", lambda e: e.scalar_tensor_tensor(out=glo[0:1, :], in0=nl[0:1, :], scalar=-8.0,
                                                          in1=ghi[0:1, :], op0=ALU.mult, op1=ALU.subtract),
                  reads=[nld, ghd], writes=[ghd])
            ps = self.bank[B_MS]
            for t in range(16):
                Sx.op("pe", lambda e, t=t, ps=ps: e.matmul(ps[:, t:t + 1], nl[0:1, t * 128:(t + 1) * 128], sm[0:1, 20:21],
                                                           start=True, stop=True),
                      reads=[nld, smd], writes=[self.bdep[B_MS]], signal=(t == 15))
            Sx.op("dve", lambda e, ps=ps: e.tensor_copy(out=gtok[:], in_=ps[:, 0:16]),
                  reads=[self.bdep[B_MS]], writes=[gtd])

        def attention(pairs, fox, hh, grp, h):
            for qb in range(4):
                do_qb(pairs, fox, hh, grp, h, qb)

        def do_qb(pairs, fox, hh, grp, h, qb):
            if True:
                qs = slice(qb * 512, (qb + 1) * 512)
                plist = pairs[qb]
                ob = B_OT[cnt["ot"] % 2]
                cnt["ot"] += 1
                ot = self.bank[ob]
                pend = None

                def emit_pv(p):
                    idx, kt, pi = p
                    Sx.op("pe", lambda e, kt=kt, pi=pi: e.matmul(ot[0:65, :], vaug[:, kt, 0:65], pb[pi][:],
                                                                  start=(idx == 0), stop=(idx == len(plist) - 1)),
                          reads=[vd, pbd[pi]], writes=[self.bdep[ob]], signal=True)

                for idx, (kt, mask) in enumerate(plist):
                    sb_ = B_ST[cnt["st"] % 2]
                    pi = cnt["st"] % 2
                    cnt["st"] += 1
                    st_ = self.bank[sb_]
                    last = "qk"
                    if fox:
                        last = "glo"
                    if mask is not None:
                        last = "mask"
                    Sx.op("pe", lambda e, kt=kt, st_=st_: e.matmul(st_[:], kh[0:64, kt * 128:(kt + 1) * 128], qh[0:64, qs],
                                                                    start=True, stop=(last == "qk")),
                          reads=[kd, qd], writes=[self.bdep[sb_]], signal=(last == "qk"))
                    if fox:
                        Sx.op("pe", lambda e, st_=st_: e.matmul(st_[:], self.ones_b[0:1, :], ghi[0:1, qs],
                                                                 start=False, stop=False),
                              reads=[ghd, self.cdep], writes=[self.bdep[sb_]], signal=False)
                        Sx.op("pe", lambda e, st_=st_: e.matmul(st_[:], self.ones_b[0:1, :], glo[0:1, qs],
                                                                 start=False, stop=(last == "glo")),
                              reads=[ghd, self.cdep], writes=[self.bdep[sb_]], signal=(last == "glo"))
                    if mask is not None:
                        Sx.op("pe", lambda e, st_=st_, mask=mask: e.matmul(st_[:], self.ident_b[:], mask,
                                                                            start=False, stop=True),
                              reads=[md, self.cdep], writes=[self.bdep[sb_]], signal=True)
                    if fox:
                        Sx.op("act", lambda e, st_=st_, pi=pi, kt=kt: e.activation(
                            out=pb[pi][:], in_=st_[:], func=AF.Exp, bias=gtok[:, kt:kt + 1], scale=0.125),
                            reads=[self.bdep[sb_], gtd], writes=[pbd[pi]])
                    else:
                        Sx.op("act", lambda e, st_=st_, pi=pi: e.activation(
                            out=pb[pi][:], in_=st_[:], func=AF.Exp, scale=0.125),
                            reads=[self.bdep[sb_]], writes=[pbd[pi]])
                    if pend is not None:
                        emit_pv(pend)
                    pend = (idx, kt, pi)
                emit_pv(pend)
                Sx.op("act", lambda e: e.activation(out=otsb[0:65, :], in_=ot[0:65, :], func=AF.Copy),
                      reads=[self.bdep[ob]], writes=[otd])
                p2 = self.bank[B_SS]
                Sx.op("pe", lambda e, p2=p2: e.matmul(p2[0:64, :], sel[0:65, 0:64], otsb[0:65, :], start=True, stop=True),
                      reads=[otd, smd], writes=[self.bdep[B_SS]])
                if fox:
                    Sx.op("dve", lambda e, p2=p2: e.reciprocal(out=rc[0:64, :], in_=p2[0:64, :]),
                          reads=[self.bdep[B_SS]], writes=[rcd])
                else:
                    Sx.op("dve", lambda e, p2=p2: e.tensor_scalar(out=rc[0:64, :], in0=p2[0:64, :],
                                                                   scalar1=sm[0:64, 4 + h:5 + h], scalar2=None,
                                                                   op0=ALU.add),
                          reads=[self.bdep[B_SS], smd], writes=[rcd])
                    Sx.op("dve", lambda e: e.reciprocal(out=rc[0:64, :], in_=rc[0:64, :]), reads=[rcd], writes=[rcd])
                Sx.op("dve", lambda e, hh=hh: e.tensor_tensor(out=og[0:64, hh, qs], in0=otsb[0:64, :], in1=rc[0:64, :],
                                                              op=ALU.mult),
                      reads=[otd, rcd], writes=[ogd[hh][qb]])

        fox_pairs, swa_pairs = [], []
        for qb in range(4):
            fp = [(kt, None) for kt in range(4 * qb)] + [(4 * qb + j, mfox[:, j, :]) for j in range(4)]
            fox_pairs.append(fp)
            sp_ = [(4 * qb + j, mswa[:, j + 1, :]) for j in range(-1, 4) if 4 * qb + j >= 0]
            swa_pairs.append(sp_)

        for grp in range(4):
            fox = grp >= 2
            Sx.dma("wo", wo[0:64], wout[grp * 256:(grp + 1) * 256, :].rearrange("(h p) d -> p h d", p=64),
                   writes=[wod], qname="pool")
            for hh in range(4):
                h = (grp % 2) * 4 + hh
                if fox:
                    qc, kc_, vc = 768 + h * 64, 1280 + h * 64, 1792 + h * 64
                else:
                    g = h // 4
                    qc, kc_, vc = h * 64, 512 + g * 64, 640 + g * 64
                Sx.dma("wq", wq, win[:, :, qc:qc + 64], writes=[wqd], qname="pool")
                Sx.dma("wk", wk, win[:, :, kc_:kc_ + 64], writes=[wkd], qname="pool")
                Sx.dma("wv", wv, win[:, :, vc:vc + 64], writes=[wvd], qname="pool")
                proj_norm(wq, wqd, 2 if fox else 0, qh, qd)
                proj_norm(wk, wkd, 3 if fox else 1, kh, kd)
                proj_v()
                if fox:
                    fox_gates(h)
                attention(fox_pairs if fox else swa_pairs, fox, hh, grp, h)
            for dc in range(8):
                for tb in range(4):
                    ts = slice(tb * 512, (tb + 1) * 512)
                    ps = self.bank[B_OP]
                    for hh in range(4):
                        Sx.op("pe", lambda e, hh=hh, dc=dc, ts=ts, ps=ps: e.matmul(
                            ps[:], wo[0:64, hh, dc * 128:(dc + 1) * 128], og[0:64, hh, ts],
                            start=(hh == 0), stop=(hh == 3)),
                            reads=[wod, ogd[hh][tb]], writes=[self.bdep[B_OP]], signal=(hh == 3))
                    Sx.op("dve", lambda e, dc=dc, ts=ts, ps=ps: e.tensor_tensor(
                        out=self.hT[:, dc, ts], in0=ps[:], in1=self.hT[:, dc, ts], op=ALU.add),
                        reads=[self.bdep[B_OP], self.hdep[dc][tb]], writes=[self.hdep[dc][tb]])

    def ssm(self, i, layer):
        Sx = self.S
        self.rmsnorm(4 + layer)
        win = self.din["ssm_w_in"][i].rearrange("(kc p) f -> p kc f", p=128)
        wout = self.din["ssm_w_out"][i]
        V = self.view
        off = 0
        hst, off = V(off, [128, 32, 64], F32)
        hstb, off = V(off, [128, 32, 64], BF16)
        hsd = [Dep() for _ in range(32)]
        yg, off = V(off, [128, 8, 512], BF16)
        ygd = [Dep() for _ in range(8)]
        xraw, off = V(off, [128, 516], F32)
        bcraw, off = V(off, [128, 516], F32)
        cacc, off = V(off, [128, 512], F32)
        xsb, off = V(off, [128, 512], BF16)
        zs, off = V(off, [128, 512], BF16)
        t1, off = V(off, [128, 512], F32)
        t2, off = V(off, [128, 512], F32)
        xdt, off = V(off, [128, 4, 64], BF16)
        xddt, off = V(off, [128, 4, 64], BF16)
        Bf, off = V(off, [128, 512], BF16)
        Cf, off = V(off, [128, 512], BF16)
        Btok, off = V(off, [128, 4, 128], BF16)
        CBT, off = V(off, [128, 512], F32)
        dtt, off = V(off, [128, 512], F32)
        ad, off = V(off, [128, 512], F32)
        nacs, off = V(off, [128, 512], F32)
        acs, off = V(off, [128, 512], F32)
        dtd, off = V(off, [128, 512], F32)
        cdb, off = V(off, [128, 512], F32)
        wx, off = V(off, [128, 8, 64], BF16)
        wz, off = V(off, [128, 8, 64], BF16)
        wB, off = V(off, [128, 8, 128], BF16)
        wC, off = V(off, [128, 8, 128], BF16)
        wdt, off = V(off, [128, 8, 32], BF16)
        wo, off = V(off, [128, 8, 1024], BF16)
        R, off = V(off, [128, 128], F32)
        arg, off = V(off, [128, 128], F32)
        dec, off = V(off, [128, 128], F32)
        dfs, off = V(off, [128, 128], F32)
        MT, off = V(off, [128, 128], BF16)
        Cs, off = V(off, [128, 128], BF16)
        trif, off = V(off, [128, 128], F32)
        onesf, off = V(off, [128, 128], F32)
        sel127, off = V(off, [128, 128], F32)
        cwx, off = V(off, [128, 32, 4], F32)
        cbx, off = V(off, [128, 32], F32)
        cwbc, off = V(off, [128, 8, 4], F32)
        cbbc, off = V(off, [128, 8], F32)
        dsk, off = V(off, [128, 32], F32)
        ngn, off = V(off, [128, 32], F32)
        ab, off = V(off, [128, 32], F32)
        dtb, off = V(off, [128, 32], F32)
        carx, off = V(off, [128, 32, 4], F32)
        carbc, off = V(off, [128, 8, 4], F32)
        rsb, off = V(off, [128, 512], F32)
        d = {k: Dep() for k in ("xraw", "bcraw", "cacc", "xsb", "zs", "t1", "t2", "xdt", "xddt", "Bf", "Cf", "Btok",
                                "CBT", "lay", "wx", "wz", "wB", "wC", "wdt", "wo", "R", "arg", "dec", "dfs", "MT",
                                "Cs", "cst", "par", "carx", "carbc", "rsb")}
        cw_src = self.din["ssm_conv_w"][i]
        cb_src = self.din["ssm_conv_b"][i]
        P = lambda fn, w: Sx.op("pool", fn, writes=w)
        P(lambda e: e.memset(hst[:], 0.0), hsd)
        P(lambda e: e.memset(hstb[:], 0.0), hsd)
        P(lambda e: e.memset(carx[:], 0.0), [d["carx"]])
        P(lambda e: e.memset(carbc[:], 0.0), [d["carbc"]])
        P(lambda e: e.memset(onesf[:], 1.0), [d["cst"]])
        P(lambda e: e.memset(trif[:], 1.0), [d["cst"]])
        P(lambda e: e.affine_select(out=trif[:], in_=trif[:], pattern=[[1, 128]], compare_op=ALU.is_ge, fill=0.0,
                                    base=0, channel_multiplier=-1), [d["cst"]])
        P(lambda e: e.memset(sel127[:], 0.0), [d["cst"]])
        P(lambda e: e.affine_select(out=sel127[:], in_=sel127[:], pattern=[[0, 128]], compare_op=ALU.not_equal,
                                    fill=1.0, base=-127, channel_multiplier=1), [d["cst"]])
        sd = lambda out, in_: self.small_dma("ssmp", out, in_, [d["par"]])
        for k in range(4):
            sd(cwx[0:64, :, k], cw_src[k, 0:2048].rearrange("(h p) -> p h", p=64))
            sd(cwbc[:, :, k], cw_src[k, 2048:3072].rearrange("(j p) -> p j", p=128))
        sd(cbx[0:64], cb_src[0:2048].rearrange("(h p) -> p h", p=64))
        sd(cbbc[:], cb_src[2048:3072].rearrange("(j p) -> p j", p=128))
        sd(dsk[:], self.din["ssm_d_skip"][i:i + 1, :].to_broadcast([128, 32]))
        sd(ngn[0:64], self.din["ssm_norm"][i].rearrange("(h p) -> p h", p=64))
        sd(ab[:], self.din["ssm_a_log"][i:i + 1, :].to_broadcast([128, 32]))
        sd(dtb[:], self.din["ssm_dt_bias"][i:i + 1, :].to_broadcast([128, 32]))
        Sx.op("act", lambda e: e.activation(out=ab[:], in_=ab[:], func=AF.Exp), reads=[d["par"]], writes=[d["par"]])
        Sx.op("dve", lambda e: e.tensor_scalar(out=ab[:], in0=ab[:], scalar1=-1.0, scalar2=None, op0=ALU.mult),
              reads=[d["par"]], writes=[d["par"]])
        Sx.dma("wdt", wdt, win[:, :, 5120:5152], writes=[d["wdt"]], qname="pool")
        B_PJ, B_D, B_Y, B_ST, B_SS, B_TR, B_MS, B_OP = range(8)
        bk, bd = self.bank, self.bdep
        ps = bk[B_MS]
        for C in range(16):
            for kc in range(8):
                Sx.op("pe", lambda e, C=C, kc=kc: e.matmul(ps[:, C * 32:(C + 1) * 32], self.xn[:, kc, C * 128:(C + 1) * 128],
                                                           wdt[:, kc, :], start=(kc == 0), stop=(kc == 7)),
                      reads=[d["wdt"], self.xdep[C // 4]], writes=[bd[B_MS]], signal=(kc == 7 and C == 15))
        for C in range(16):
            Sx.op("dve", lambda e, C=C: e.tensor_tensor(out=dtt[:, C * 32:(C + 1) * 32], in0=ps[:, C * 32:(C + 1) * 32],
                                                        in1=dtb[:], op=ALU.add),
                  reads=[bd[B_MS], d["par"]], writes=[d["lay"]])
        Sx.op("act", lambda e: e.activation(out=dtt[:], in_=dtt[:], func=AF.Exp), reads=[d["lay"]], writes=[d["lay"]])
        Sx.op("act", lambda e: e.activation(out=dtt[:], in_=dtt[:], func=AF.Ln, bias=1.0, scale=1.0),
              reads=[d["lay"]], writes=[d["lay"]])
        for C in range(16):
            Sx.op("dve", lambda e, C=C: e.tensor_tensor(out=ad[:, C * 32:(C + 1) * 32], in0=dtt[:, C * 32:(C + 1) * 32],
                                                        in1=ab[:], op=ALU.mult),
                  reads=[d["lay"], d["par"]], writes=[d["lay"]])
        Sx.op("pe", lambda e: e.matmul(ps[:], trif[:], ad[:], start=True, stop=True),
              reads=[d["lay"], d["cst"]], writes=[bd[B_MS]])
        Sx.op("act", lambda e: e.activation(out=acs[:], in_=ps[:], func=AF.Copy), reads=[bd[B_MS]], writes=[d["lay"]])
        Sx.op("dve", lambda e: e.tensor_scalar(out=nacs[:], in0=ps[:], scalar1=-1.0, scalar2=None, op0=ALU.mult),
              reads=[bd[B_MS]], writes=[d["lay"]])
        Sx.op("pe", lambda e: e.matmul(ps[:], sel127[:], acs[:], start=True, stop=True),
              reads=[d["lay"], d["cst"]], writes=[bd[B_MS]])
        Sx.op("act", lambda e: e.activation(out=cdb[:], in_=ps[:], func=AF.Exp), reads=[bd[B_MS]], writes=[d["lay"]])
        Sx.op("dve", lambda e: e.tensor_tensor(out=dtd[:], in0=ps[:], in1=acs[:], op=ALU.subtract),
              reads=[bd[B_MS], d["lay"]], writes=[d["lay"]])
        Sx.op("act", lambda e: e.activation(out=dtd[:], in_=dtd[:], func=AF.Exp), reads=[d["lay"]], writes=[d["lay"]])
        Sx.op("dve", lambda e: e.tensor_tensor(out=dtd[:], in0=dtd[:], in1=dtt[:], op=ALU.mult),
              reads=[d["lay"]], writes=[d["lay"]])

        def conv_silu(raw, rdep, np_, wsl, bsl, car, cdep, out, odep, ps_):
            Sx.op("act", lambda e: e.activation(out=raw[0:np_, 4:516], in_=ps_[0:np_, :], func=AF.Copy),
                  reads=[bd[B_PJ]], writes=[rdep])
            Sx.op("dve", lambda e: e.tensor_copy(out=raw[0:np_, 1:4], in_=car[:, 0:3]), reads=[cdep], writes=[rdep])
            Sx.op("act", lambda e: e.activation(out=cacc[0:np_, :], in_=raw[0:np_, 1:513], func=AF.Identity,
                                                bias=bsl, scale=wsl[:, 0:1]),
                  reads=[rdep, d["par"]], writes=[d["cacc"]])
            for k in range(1, 4):
                Sx.op("dve", lambda e, k=k: e.scalar_tensor_tensor(out=cacc[0:np_, :], in0=raw[0:np_, 1 + k:513 + k],
                                                                   scalar=wsl[:, k:k + 1], in1=cacc[0:np_, :],
                                                                   op0=ALU.mult, op1=ALU.add),
                      reads=[rdep, d["cacc"], d["par"]], writes=[d["cacc"]])
            Sx.op("dve", lambda e: e.tensor_copy(out=car[:, 0:3], in_=raw[0:np_, 513:516]), reads=[rdep], writes=[cdep])
            Sx.op("act", lambda e: e.activation(out=out[0:np_, :], in_=cacc[0:np_, :], func=AF.Silu),
                  reads=[d["cacc"]], writes=[odep])

        def proj(wst, wdep, np_, tb):
            ts = slice(tb * 512, (tb + 1) * 512)
            for kc in range(8):
                Sx.op("pe", lambda e, kc=kc: e.matmul(bk[B_PJ][0:np_, :], wst[:, kc, :], self.xn[:, kc, ts],
                                                      start=(kc == 0), stop=(kc == 7)),
                      reads=[wdep, self.xdep[tb]], writes=[bd[B_PJ]], signal=(kc == 7))

        def head(tb, g, r):
            h = g * 8 + r
            ts = slice(tb * 512, (tb + 1) * 512)
            Sx.dma("wz", wz, win[:, :, h * 64:(h + 1) * 64], writes=[d["wz"]], qname="pool")
            Sx.dma("wx", wx, win[:, :, 2048 + h * 64:2048 + (h + 1) * 64], writes=[d["wx"]], qname="pool")
            proj(wz, d["wz"], 64, tb)
            Sx.op("act", lambda e: e.activation(out=zs[0:64, :], in_=bk[B_PJ][0:64, :], func=AF.Silu),
                  reads=[bd[B_PJ]], writes=[d["zs"]])
            proj(wx, d["wx"], 64, tb)
            conv_silu(xraw, d["xraw"], 64, cwx[0:64, h, :], cbx[0:64, h:h + 1], carx[0:64, h, :], d["carx"],
                      xsb, d["xsb"], bk[B_PJ])
            for cl in range(4):
                Sx.op("pe", lambda e, cl=cl: e.matmul(bk[B_TR][:, cl * 64:(cl + 1) * 64], xsb[0:64, cl * 128:(cl + 1) * 128],
                                                      self.ident_b[0:64, 0:64], start=True, stop=True),
                      reads=[d["xsb"], self.cdep], writes=[bd[B_TR]], signal=(cl == 3))
            for cl in range(4):
                col = (tb * 4 + cl) * 32 + h
                Sx.op("dve", lambda e, cl=cl, col=col: e.tensor_scalar(
                    out=xdt[:, cl, :], in0=bk[B_TR][:, cl * 64:(cl + 1) * 64], scalar1=dtt[:, col:col + 1], scalar2=None,
                    op0=ALU.mult), reads=[bd[B_TR], d["lay"]], writes=[d["xdt"]])
                Sx.op("dve", lambda e, cl=cl, col=col: e.tensor_scalar(
                    out=xddt[:, cl, :], in0=bk[B_TR][:, cl * 64:(cl + 1) * 64], scalar1=dtd[:, col:col + 1], scalar2=None,
                    op0=ALU.mult), reads=[bd[B_TR], d["lay"]], writes=[d["xddt"]])
            for cl in range(4):
                col = (tb * 4 + cl) * 32 + h
                cs_ = slice(cl * 128, (cl + 1) * 128)
                Sx.op("dve", lambda e, col=col: e.tensor_scalar(out=R[:], in0=trif[:], scalar1=ad[:, col:col + 1],
                                                                scalar2=None, op0=ALU.mult),
                      reads=[d["cst"], d["lay"]], writes=[d["R"]])
                Sx.op("pe", lambda e: e.matmul(bk[B_D][:, 0:128], onesf[:], R[:], start=True, stop=True),
                      reads=[d["R"], d["cst"]], writes=[bd[B_D]])
                Sx.op("dve", lambda e, col=col: e.tensor_scalar(out=arg[:], in0=bk[B_D][:, 0:128],
                                                                scalar1=nacs[:, col:col + 1], scalar2=0.0,
                                                                op0=ALU.add, op1=ALU.min),
                      reads=[bd[B_D], d["lay"]], writes=[d["arg"]])
                Sx.op("act", lambda e: e.activation(out=dec[:], in_=arg[:], func=AF.Exp), reads=[d["arg"]], writes=[d["dec"]])
                Sx.op("act", lambda e: e.activation(out=dfs[:], in_=bk[B_D][:, 0:128], func=AF.Exp),
                      reads=[bd[B_D]], writes=[d["dfs"]])
                Sx.op("dve", lambda e, cs_=cs_: e.tensor_tensor(out=MT[:], in0=CBT[:, cs_], in1=dec[:], op=ALU.mult),
                      reads=[d["CBT"], d["dec"]], writes=[d["MT"]])
                Sx.op("dve", lambda e, cs_=cs_: e.tensor_tensor(out=Cs[:], in0=Cf[:, cs_], in1=dfs[:], op=ALU.mult),
                      reads=[d["Cf"], d["dfs"]], writes=[d["Cs"]])
                Sx.op("pe", lambda e, cl=cl, cs_=cs_: e.matmul(bk[B_Y][0:64, cs_], xdt[:, cl, :], MT[:], start=True, stop=False),
                      reads=[d["xdt"], d["MT"]], writes=[bd[B_Y]], signal=False)
                Sx.op("pe", lambda e, cs_=cs_: e.matmul(bk[B_Y][0:64, cs_], hstb[:, h, :], Cs[:], start=False, stop=True),
                      reads=[hsd[h], d["Cs"]], writes=[bd[B_Y]])
                Sx.op("pe", lambda e, cl=cl: e.matmul(bk[B_ST][:, 0:64], Btok[:, cl, :], xddt[:, cl, :], start=True, stop=True),
                      reads=[d["Btok"], d["xddt"]], writes=[bd[B_ST]])
                Sx.op("dve", lambda e, col=col: e.scalar_tensor_tensor(out=hst[:, h, :], in0=hst[:, h, :],
                                                                       scalar=cdb[:, col:col + 1], in1=bk[B_ST][:, 0:64],
                                                                       op0=ALU.mult, op1=ALU.add),
                      reads=[bd[B_ST], d["lay"], hsd[h]], writes=[hsd[h]])
                Sx.op("act", lambda e: e.activation(out=hstb[:, h, :], in_=hst[:, h, :], func=AF.Copy),
                      reads=[hsd[h]], writes=[hsd[h]])
            Sx.op("dve", lambda e: e.scalar_tensor_tensor(out=t1[0:64, :], in0=xsb[0:64, :], scalar=dsk[0:64, h:h + 1],
                                                          in1=bk[B_Y][0:64, :], op0=ALU.mult, op1=ALU.add),
                  reads=[bd[B_Y], d["xsb"], d["par"]], writes=[d["t1"]])
            Sx.op("dve", lambda e: e.tensor_tensor(out=t2[0:64, :], in0=t1[0:64, :], in1=zs[0:64, :], op=ALU.mult),
                  reads=[d["t1"], d["zs"]], writes=[d["t2"]])
            Sx.op("act", lambda e: e.activation(out=self.sq[0:64, 0, :], in_=t2[0:64, :], func=AF.Square),
                  reads=[d["t2"]], writes=[self.sqdep])
            Sx.op("pe", lambda e: e.matmul(bk[B_SS][:], self.ones_b[0:64, :], self.sq[0:64, 0, :],
                                           start=(r == 0), stop=(r == 7)),
                  reads=[self.sqdep, self.cdep], writes=[bd[B_SS]], signal=True)
            Sx.op("dve", lambda e: e.tensor_scalar(out=yg[0:64, r, :], in0=t2[0:64, :], scalar1=ngn[0:64, h:h + 1],
                                                   scalar2=None, op0=ALU.mult),
                  reads=[d["t2"], d["par"]], writes=[ygd[r]])

        def group(tb, g):
            ts = slice(tb * 512, (tb + 1) * 512)
            Sx.dma("wB", wB, win[:, :, 4096 + g * 128:4096 + (g + 1) * 128], writes=[d["wB"]], qname="pool")
            Sx.dma("wC", wC, win[:, :, 4608 + g * 128:4608 + (g + 1) * 128], writes=[d["wC"]], qname="pool")
            Sx.dma("wo", wo[0:64], wout[g * 512:(g + 1) * 512, :].rearrange("(h p) d -> p h d", p=64),
                   writes=[d["wo"]], qname="pool")
            proj(wB, d["wB"], 128, tb)
            conv_silu(bcraw, d["bcraw"], 128, cwbc[:, g, :], cbbc[:, g:g + 1], carbc[:, g, :], d["carbc"], Bf, d["Bf"],
                      bk[B_PJ])
            proj(wC, d["wC"], 128, tb)
            conv_silu(bcraw, d["bcraw"], 128, cwbc[:, 4 + g, :], cbbc[:, 4 + g:5 + g], carbc[:, 4 + g, :], d["carbc"],
                      Cf, d["Cf"], bk[B_PJ])
            for cl in range(4):
                cs_ = slice(cl * 128, (cl + 1) * 128)
                Sx.op("pe", lambda e, cs_=cs_: e.matmul(bk[B_TR][:, cs_], Bf[:, cs_], self.ident_b[:], start=True, stop=True),
                      reads=[d["Bf"], self.cdep], writes=[bd[B_TR]], signal=(cl == 3))
            Sx.op("act", lambda e: e.activation(out=Btok[:], in_=bk[B_TR][:].rearrange("p (a b) -> p a b", a=4),
                                                func=AF.Copy), reads=[bd[B_TR]], writes=[d["Btok"]])
            for cl in range(4):
                cs_ = slice(cl * 128, (cl + 1) * 128)
                Sx.op("pe", lambda e, cs_=cs_: e.matmul(bk[B_TR][:, cs_], Bf[:, cs_], Cf[:, cs_], start=True, stop=True),
                      reads=[d["Bf"], d["Cf"]], writes=[bd[B_TR]], signal=(cl == 3))
            for cl in range(4):
                cs_ = slice(cl * 128, (cl + 1) * 128)
                Sx.op("dve", lambda e, cs_=cs_: e.tensor_tensor(out=CBT[:, cs_], in0=bk[B_TR][:, cs_], in1=trif[:],
                                                                op=ALU.mult),
                      reads=[bd[B_TR], d["cst"]], writes=[d["CBT"]])
            for r in range(8):
                head(tb, g, r)
            Sx.op("act", lambda e: e.activation(out=rsb[:], in_=bk[B_SS][:], func=AF.Sqrt, bias=EPS, scale=1.0 / 512),
                  reads=[bd[B_SS]], writes=[d["rsb"]])
            Sx.op("dve", lambda e: e.reciprocal(out=rsb[:], in_=rsb[:]), reads=[d["rsb"]], writes=[d["rsb"]])
            for dc in range(8):
                for r in range(8):
                    Sx.op("pe", lambda e, r=r, dc=dc: e.matmul(bk[B_OP][:], wo[0:64, r, dc * 128:(dc + 1) * 128],
                                                               yg[0:64, r, :], start=(r == 0), stop=(r == 7)),
                          reads=[d["wo"], ygd[r]], writes=[bd[B_OP]], signal=(r == 7))
                Sx.op("dve", lambda e: e.tensor_tensor(out=t1[:], in0=bk[B_OP][:], in1=rsb[:], op=ALU.mult),
                      reads=[bd[B_OP], d["rsb"]], writes=[d["t1"]])
                Sx.op("dve", lambda e, dc=dc: e.tensor_tensor(out=self.hT[:, dc, ts], in0=t1[:], in1=self.hT[:, dc, ts],
                                                              op=ALU.add),
                      reads=[d["t1"], self.hdep[dc][tb]], writes=[self.hdep[dc][tb]])

        for tb in range(4):
            for g in range(4):
                group(tb, g)


ALL_PHASES = []
for _l in range(4):
    ALL_PHASES.append(("ffn1", _l))
    ALL_PHASES.append(("attn" if _l % 2 == 0 else "ssm", _l))
    ALL_PHASES.append(("ffn2", _l))


def run(inputs, phases, n_cores=8, trace=False):
    nc = Builder(phases).build()
    x = np.ascontiguousarray(inputs["x"], dtype=np.float32)
    in_maps = []
    for c in range(n_cores):
        m = {"x": x[c]}
        for k in INPUT_SHAPES:
            m[k] = np.ascontiguousarray(inputs[k], dtype=np.float32)
        in_maps.append(m)
    res = run_bass_kernel_spmd(nc, in_maps, core_ids=list(range(n_cores)), trace=trace)
    out = np.stack([r["y"] for r in res.results], axis=0)
    return out, res


def kernel(**inputs):
    out, _ = run(inputs, ALL_PHASES, 8)
    return out.astype(np.float32)
```

```python
from contextlib import ExitStack
import os
import numpy as np
import concourse.bass as bass
import concourse.mybir as mybir
from concourse.bass_utils import run_bass_kernel_spmd

F32 = mybir.dt.float32
BF16 = mybir.dt.bfloat16
AF = mybir.ActivationFunctionType
ALU = mybir.AluOpType
AX = mybir.AxisListType

D = 1024
S = 2048
DFF = 2816
NF = DFF // 128
EPS = 1e-6
ATT_W = 2312
SSM_W = 5152
NEG = -30000.0

INPUT_SHAPES = {
    "ffn1_norm": [4, 1024], "ffn1_w_gate": [4, 1024, 2816], "ffn1_w_up": [4, 1024, 2816],
    "ffn1_w_down": [4, 2816, 1024], "mix_norm": [4, 1024], "ffn2_norm": [4, 1024],
    "ffn2_w_gate": [4, 1024, 2816], "ffn2_w_up": [4, 1024, 2816], "ffn2_w_down": [4, 2816, 1024],
    "attn_w_in": [2, 1024, 2312], "attn_forget_bias": [2, 8], "swa_q_norm": [2, 64],
    "swa_k_norm": [2, 64], "swa_sinks": [2, 8], "fox_q_norm": [2, 64], "fox_k_norm": [2, 64],
    "attn_w_out": [2, 1024, 1024], "ssm_w_in": [2, 1024, 5152], "ssm_conv_w": [2, 4, 3072],
    "ssm_conv_b": [2, 3072], "ssm_dt_bias": [2, 32], "ssm_a_log": [2, 32], "ssm_d_skip": [2, 32],
    "ssm_norm": [2, 2048], "ssm_w_out": [2, 2048, 1024],
}


class Dep:
    __slots__ = ("w", "r")

    def __init__(self):
        self.w = None
        self.r = []


class Queue:
    def __init__(self, name, sem, is_pe=False):
        self.name = name
        self.sem = sem
        self.count = 0
        self.ops = []
        self.waited = {}
        self.is_pe = is_pe


class Sched:
    def __init__(self, nc, stack):
        self.nc = nc
        self.stack = stack
        self.q = {}
        for name in ("pe", "act", "dve", "pool", "sp"):
            sem = stack.enter_context(nc.semaphore("s_" + name))
            self.q[name] = Queue(name, sem, name == "pe")
        self.dma_sems = {}

    def dma_sem(self, name):
        if name not in self.dma_sems:
            sem = self.stack.enter_context(self.nc.semaphore("d_" + name))
            self.dma_sems[name] = [sem, 0]
        return self.dma_sems[name]

    def _collect(self, q, reads, writes, dma_group=None):
        need = {}

        def add(tok):
            if tok is None:
                return
            sem, val, owner, grp = tok
            if owner == q.name and q.is_pe:
                return
            if dma_group is not None and grp is dma_group:
                return
            k = id(sem)
            if q.waited.get(k, 0) >= val:
                return
            if k not in need or need[k][1] < val:
                need[k] = (sem, val)

        for d in reads:
            add(d.w)
        for d in writes:
            add(d.w)
            for t in d.r:
                add(t)
        waits = list(need.values())
        for sem, val in waits:
            q.waited[id(sem)] = val
        return waits

    def op(self, qname, fn, reads=(), writes=(), signal=True):
        q = self.q[qname]
        waits = self._collect(q, reads, writes)
        tok = (q.sem, q.count + 1, q.name, None)
        if signal:
            q.count += 1
        q.ops.append((waits, fn, (q.sem, 1) if signal else None))
        for d in reads:
            d.r.append(tok)
        for d in writes:
            d.w = tok
            d.r = []
        return tok

    def dma(self, semname, out, in_, reads=(), writes=(), qname="sp"):
        q = self.q[qname]
        ent = self.dma_sem(semname)
        waits = self._collect(q, reads, writes, dma_group=ent)
        ent[1] += 16
        tok = (ent[0], ent[1], None, ent)
        q.ops.append((waits, lambda e: e.dma_start(out=out, in_=in_), (ent[0], 16)))
        for d in reads:
            d.r.append(tok)
        for d in writes:
            d.w = tok
            d.r = []
        return tok

    def barrier(self):
        toks = [(q.sem, q.count) for q in self.q.values() if q.count > 0]
        toks += [(ent[0], ent[1]) for ent in self.dma_sems.values() if ent[1] > 0]
        for q in self.q.values():
            waits = []
            for sem, val in toks:
                if sem is q.sem and q.is_pe:
                    continue
                if q.waited.get(id(sem), 0) >= val:
                    continue
                q.waited[id(sem)] = val
                waits.append((sem, val))
            q.ops.append((waits, None, None))

    def finish(self, final_tokens):
        nc = self.nc
        fin = {}
        for sem, val, owner, grp in final_tokens:
            k = id(sem)
            if k not in fin or fin[k][1] < val:
                fin[k] = (sem, val)
        engs = {"pe": "tensor", "act": "scalar", "dve": "vector", "pool": "gpsimd", "sp": "sync"}
        with nc.Block() as block:
            for name, attr in engs.items():
                q = self.q[name]

                def body(eng, q=q, name=name):
                    for waits, fn, inc in q.ops:
                        for sem, val in waits:
                            eng.wait_ge(sem, val)
                        if fn is None:
                            continue
                        ins = fn(eng)
                        if inc is not None:
                            ins.then_inc(inc[0], inc[1])
                    if name == "sp":
                        for sem, val in fin.values():
                            eng.wait_ge(sem, val)

                getattr(block, attr)(body)


class Builder:
    def __init__(self, phases):
        self.phases = phases
        self.nc = bass.Bass("TRN2", target_bir_lowering=False)

    def build(self):
        nc = self.nc
        self.din = {}
        self.x = nc.dram_tensor("x", [S, D], F32, kind="ExternalInput").ap()
        for k, shp in INPUT_SHAPES.items():
            self.din[k] = nc.dram_tensor(k, shp, F32, kind="ExternalInput").ap()
        self.y = nc.dram_tensor("y", [S, D], F32, kind="ExternalOutput").ap()
        with ExitStack() as st:
            self.st = st
            self.S = Sched(nc, st)
            self.alloc()
            self.consts()
            self.load_x()
            for ph in self.phases:
                kind, layer = ph
                self.S.barrier()
                if kind == "ffn1":
                    self.ffn(layer, "ffn1")
                elif kind == "ffn2":
                    self.ffn(layer, "ffn2")
                elif kind == "attn":
                    self.attn(layer // 2, layer)
                elif kind == "ssm":
                    self.ssm(layer // 2, layer)
            self.S.barrier()
            toks = self.store_y()
            self.S.finish(toks)
        return nc

    def sb(self, name, shape, dtype):
        return self.st.enter_context(self.nc.sbuf_tensor(name, shape, dtype))

    def alloc(self):
        nc = self.nc
        self.hT = self.sb("hT", [128, 8, S], F32)
        self.hdep = [[Dep() for _ in range(4)] for _ in range(8)]
        self.xn = self.sb("xn", [128, 8, S], BF16)
        self.xdep = [Dep() for _ in range(4)]
        self.bank = [self.st.enter_context(nc.psum_tensor("pb%d" % i, [128, 512], F32)) for i in range(8)]
        self.bdep = [Dep() for _ in range(8)]
        self.SCR = 23 * 1024 + 512
        self.scr = self.sb("scr", [128, self.SCR], F32)
        self.ident_f = self.sb("ident_f", [128, 128], F32)
        self.ident_b = self.sb("ident_b", [128, 128], BF16)
        self.ones_b = self.sb("ones_b", [128, 128], BF16)
        self.cdep = Dep()
        self.sq = self.sb("sq", [128, 8, 512], BF16)
        self.sqdep = Dep()
        self.rstd = self.sb("rstd", [128, 512], F32)
        self.rstdep = Dep()
        self.gains = self.sb("gains", [128, 12, 8], F32)
        self.gdep = Dep()
        self.sg = [self.sb("sg%d" % i, [128, 512], F32) for i in range(2)]
        self.sgdep = [Dep(), Dep()]
        self.io = [self.scr[:, i * 1024:(i + 1) * 1024] for i in range(2)]
        self.iodep = [Dep(), Dep()]

    def view(self, off_words, shape, dtype):
        n = 1
        for s in shape[1:]:
            n *= s
        if dtype == BF16:
            words = (n + 1) // 2
            ap = self.scr[:, off_words:off_words + words].bitcast(BF16)
        else:
            words = n
            ap = self.scr[:, off_words:off_words + words]
        assert off_words + words <= self.SCR, (off_words, words)
        if len(shape) == 3:
            ap = ap.rearrange("p (a b) -> p a b", a=shape[1])
        elif len(shape) == 4:
            ap = ap.rearrange("p (a b c) -> p a b c", a=shape[1], b=shape[2])
        if shape[0] != 128:
            ap = ap[0:shape[0]]
        return ap, off_words + words

    def consts(self):
        Sx = self.S
        nc = self.nc
        idf, idb, ones = self.ident_f, self.ident_b, self.ones_b
        Sx.op("pool", lambda e: e.memset(idf[:], 0.0), writes=[self.cdep])
        Sx.op("pool", lambda e: e.affine_select(out=idf[:], in_=idf[:], pattern=[[-1, 128]],
                                                compare_op=ALU.not_equal, fill=1.0, base=0,
                                                channel_multiplier=1), writes=[self.cdep])
        Sx.op("dve", lambda e: e.tensor_copy(out=idb[:], in_=idf[:]), reads=[self.cdep], writes=[self.cdep])
        Sx.op("dve", lambda e: e.memset(ones[:], 1.0), writes=[self.cdep])

    def _small_dma(self, out, in_):
        nc = self.nc

        def fn(e):
            with nc.allow_non_contiguous_dma(reason="tiny vectors"):
                return e.dma_start(out=out, in_=in_)
        return fn

    def small_dma(self, semname, out, in_, writes):
        Sx = self.S
        q = Sx.q["sp"]
        ent = Sx.dma_sem(semname)
        waits = Sx._collect(q, (), writes, dma_group=ent)
        ent[1] += 16
        tok = (ent[0], ent[1], None, ent)
        q.ops.append((waits, self._small_dma(out, in_), (ent[0], 16)))
        for d in writes:
            d.w = tok
            d.r = []
        return tok

    def load_x(self):
        Sx = self.S
        for n, name in enumerate(("ffn1_norm", "mix_norm", "ffn2_norm")):
            src = self.din[name].rearrange("l (c p) -> p l c", p=128)
            self.small_dma("gains", self.gains[:, n * 4:(n + 1) * 4, :], src, [self.gdep])
        for tt in range(16):
            io, iod = self.io[tt % 2], self.iodep[tt % 2]
            Sx.dma("xin%d" % (tt % 2), io[:], self.x[tt * 128:(tt + 1) * 128, :], writes=[iod])
            for half in range(2):
                b = 6 + half
                ps, pd = self.bank[b], self.bdep[b]
                for c4 in range(4):
                    c = half * 4 + c4
                    Sx.op("pe", lambda e, ps=ps, io=io, c=c, c4=c4: e.transpose(
                        out=ps[:, c4 * 128:(c4 + 1) * 128], in_=io[:, c * 128:(c + 1) * 128],
                        identity=self.ident_f[:]), reads=[iod, self.cdep], writes=[pd], signal=(c4 == 3))
                tb = tt // 4
                dst = self.hT[:, half * 4:half * 4 + 4, tt * 128:(tt + 1) * 128]
                src = ps[:].rearrange("p (a b) -> p a b", a=4)
                eng = "act" if half == 0 else "dve"
                if eng == "act":
                    fn = lambda e, dst=dst, src=src: e.activation(out=dst, in_=src, func=AF.Copy)
                else:
                    fn = lambda e, dst=dst, src=src: e.tensor_copy(out=dst, in_=src)
                Sx.op(eng, fn, reads=[pd], writes=[self.hdep[half * 4 + c4][tb] for c4 in range(4)])

    def store_y(self):
        Sx = self.S
        toks = []
        for tt in range(16):
            io, iod = self.io[tt % 2], self.iodep[tt % 2]
            tb = tt // 4
            for half in range(2):
                b = 6 + half
                ps, pd = self.bank[b], self.bdep[b]
                for c4 in range(4):
                    c = half * 4 + c4
                    Sx.op("pe", lambda e, ps=ps, c=c, c4=c4, tt=tt: e.transpose(
                        out=ps[:, c4 * 128:(c4 + 1) * 128], in_=self.hT[:, c, tt * 128:(tt + 1) * 128],
                        identity=self.ident_f[:]), reads=[self.hdep[c][tb], self.cdep], writes=[pd],
                        signal=(c4 == 3))
                dst = io[:, half * 512:(half + 1) * 512]
                if half == 0:
                    fn = lambda e, dst=dst, ps=ps: e.activation(out=dst, in_=ps[:], func=AF.Copy)
                    Sx.op("act", fn, reads=[pd], writes=[iod])
                else:
                    fn = lambda e, dst=dst, ps=ps: e.tensor_copy(out=dst, in_=ps[:])
                    Sx.op("dve", fn, reads=[pd], writes=[iod])
            toks.append(Sx.dma("yout%d" % (tt % 2), self.y[tt * 128:(tt + 1) * 128, :], io[:], reads=[iod]))
        return toks

    def rmsnorm(self, gidx):
        Sx = self.S
        nb = 5
        for tb in range(4):
            ts = slice(tb * 512, (tb + 1) * 512)
            for c in range(8):
                Sx.op("act", lambda e, c=c, ts=ts: e.activation(out=self.sq[:, c, :], in_=self.hT[:, c, ts],
                                                                func=AF.Square),
                      reads=[self.hdep[c][tb]], writes=[self.sqdep])
            ps, pd = self.bank[nb], self.bdep[nb]
            for c in range(8):
                Sx.op("pe", lambda e, c=c, ps=ps: e.matmul(ps[:], self.ones_b[:], self.sq[:, c, :],
                                                            start=(c == 0), stop=(c == 7)),
                      reads=[self.sqdep, self.cdep], writes=[pd], signal=(c == 7))
            Sx.op("act", lambda e, ps=ps: e.activation(out=self.rstd[:], in_=ps[:], func=AF.Sqrt,
                                                        bias=EPS, scale=1.0 / D),
                  reads=[pd], writes=[self.rstdep])
            Sx.op("dve", lambda e: e.reciprocal(out=self.rstd[:], in_=self.rstd[:]),
                  reads=[self.rstdep], writes=[self.rstdep])
            for c in range(8):
                Sx.op("dve", lambda e, c=c, ts=ts: e.scalar_tensor_tensor(
                    out=self.xn[:, c, ts], in0=self.hT[:, c, ts], scalar=self.gains[:, gidx, c:c + 1],
                    in1=self.rstd[:], op0=ALU.mult, op1=ALU.mult),
                    reads=[self.hdep[c][tb], self.rstdep, self.gdep], writes=[self.xdep[tb]])

    def ffn(self, layer, which):
        Sx = self.S
        gidx = (0 if which == "ffn1" else 2) * 4 + layer
        self.rmsnorm(gidx)
        wg = self.din[which + "_w_gate"][layer].rearrange("(kc p) f -> p kc f", p=128)
        wu = self.din[which + "_w_up"][layer].rearrange("(kc p) f -> p kc f", p=128)
        wd = self.din[which + "_w_down"][layer].rearrange("(fc p) d -> p fc d", p=128)
        off = 0
        hff, off = self.view(off, [128, NF, 1024], BF16)
        hfdep = [[Dep() for _ in range(2)] for _ in range(NF)]
        gst, ust, gud = [], [], []
        for i in range(3):
            a, off = self.view(off, [128, 8, 256], BF16)
            b, off = self.view(off, [128, 8, 256], BF16)
            gst.append(a)
            ust.append(b)
            gud.append(Dep())
        dst, ddd = [], []
        for i in range(2):
            a, off = self.view(off, [128, NF, 256], BF16)
            dst.append(a)
            ddd.append(Dep())
        step = 0
        for blk in range(2):
            for grp in range(NF // 2):
                si = grp % 3
                cs = slice(grp * 256, (grp + 1) * 256)
                Sx.dma("wgu%d" % si, gst[si], wg[:, :, cs], writes=[gud[si]], qname="pool")
                Sx.dma("wgu%d" % si, ust[si], wu[:, :, cs], writes=[gud[si]], qname="pool")
                for cc in range(2):
                    j = grp * 2 + cc
                    for half in range(2):
                        tb = blk * 2 + half
                        ts = slice(tb * 512, (tb + 1) * 512)
                        gb, ub = step % 2, 2 + step % 2
                        gps, ups = self.bank[gb], self.bank[ub]
                        for kc in range(8):
                            Sx.op("pe", lambda e, gps=gps, si=si, kc=kc, cc=cc, ts=ts: e.matmul(
                                gps[:], gst[si][:, kc, cc * 128:(cc + 1) * 128], self.xn[:, kc, ts],
                                start=(kc == 0), stop=(kc == 7)),
                                reads=[gud[si], self.xdep[tb]], writes=[self.bdep[gb]], signal=(kc == 7))
                        for kc in range(8):
                            Sx.op("pe", lambda e, ups=ups, si=si, kc=kc, cc=cc, ts=ts: e.matmul(
                                ups[:], ust[si][:, kc, cc * 128:(cc + 1) * 128], self.xn[:, kc, ts],
                                start=(kc == 0), stop=(kc == 7)),
                                reads=[gud[si], self.xdep[tb]], writes=[self.bdep[ub]], signal=(kc == 7))
                        sg, sgd = self.sg[step % 2], self.sgdep[step % 2]
                        Sx.op("act", lambda e, sg=sg, gps=gps: e.activation(out=sg[:], in_=gps[:], func=AF.Silu),
                              reads=[self.bdep[gb]], writes=[sgd])
                        Sx.op("dve", lambda e, sg=sg, ups=ups, j=j, half=half: e.tensor_tensor(
                            out=hff[:, j, half * 512:(half + 1) * 512], in0=ups[:], in1=sg[:], op=ALU.mult),
                            reads=[self.bdep[ub], sgd], writes=[hfdep[j][half]])
                        step += 1
            for dg in range(4):
                si = dg % 2
                Sx.dma("wd%d" % si, dst[si], wd[:, :, dg * 256:(dg + 1) * 256], writes=[ddd[si]], qname="pool")
                for cc in range(2):
                    dc = dg * 2 + cc
                    for half in range(2):
                        tb = blk * 2 + half
                        ts = slice(tb * 512, (tb + 1) * 512)
                        b = 4 + step % 2
                        ps = self.bank[b]
                        for f in range(NF):
                            Sx.op("pe", lambda e, ps=ps, si=si, f=f, cc=cc, half=half: e.matmul(
                                ps[:], dst[si][:, f, cc * 128:(cc + 1) * 128],
                                hff[:, f, half * 512:(half + 1) * 512],
                                start=(f == 0), stop=(f == NF - 1)),
                                reads=[ddd[si], hfdep[f][half]], writes=[self.bdep[b]], signal=(f == NF - 1))
                        Sx.op("dve", lambda e, ps=ps, dc=dc, ts=ts: e.scalar_tensor_tensor(
                            out=self.hT[:, dc, ts], in0=ps[:], scalar=0.5, in1=self.hT[:, dc, ts],
                            op0=ALU.mult, op1=ALU.add),
                            reads=[self.bdep[b], self.hdep[dc][tb]], writes=[self.hdep[dc][tb]])
                        step += 1

    def attn(self, i, layer):
        Sx = self.S
        nc = self.nc
        self.rmsnorm(4 + layer)
        win = self.din["attn_w_in"][i].rearrange("(kc p) f -> p kc f", p=128)
        wout = self.din["attn_w_out"][i]
        off = 0
        og, off = self.view(off, [128, 4, S], BF16)
        ogd = [[Dep() for _ in range(4)] for _ in range(4)]
        qh, off = self.view(off, [128, S], BF16)
        kh, off = self.view(off, [128, S], BF16)
        qd, kd = Dep(), Dep()
        vaug, off = self.view(off, [128, 16, 66], BF16)
        vd = Dep()
        wq, off = self.view(off, [128, 8, 64], BF16)
        wk, off = self.view(off, [128, 8, 64], BF16)
        wv, off = self.view(off, [128, 8, 64], BF16)
        wfl, off = self.view(off, [128, 8, 8], BF16)
        wqd, wkd, wvd, wfd = Dep(), Dep(), Dep(), Dep()
        nl, off = self.view(off, [128, S], F32)
        onesrow, off = self.view(off, [128, S], F32)
        ghi, off = self.view(off, [128, S], BF16)
        glo, off = self.view(off, [128, S], BF16)
        gtok, off = self.view(off, [128, 16], F32)
        nld, ghd, gtd = Dep(), Dep(), Dep()
        pb, pbd = [], []
        for _ in range(2):
            a, off = self.view(off, [128, 512], BF16)
            pb.append(a)
            pbd.append(Dep())
        otsb, off = self.view(off, [128, 512], F32)
        rc, off = self.view(off, [128, 512], F32)
        otd, rcd = Dep(), Dep()
        mtmp, off = self.view(off, [128, 512], F32)
        mfox, off = self.view(off, [128, 4, 512], BF16)
        mswa, off = self.view(off, [128, 5, 512], BF16)
        md = Dep()
        wo, off = self.view(off, [128, 4, 1024], BF16)
        wod = Dep()
        sm, off = self.view(off, [128, 32], F32)
        smd = Dep()
        sel, off = self.view(off, [128, 64], F32)
        hg = sm[:, 0:4]
        esink = sm[:, 4:12]
        negfb = sm[:, 12:20]
        onef = sm[:, 20:21]

        Sx.op("pool", lambda e: e.memset(vaug[:, :, 64:66], 1.0), writes=[vd])
        Sx.op("pool", lambda e: e.memset(onesrow[0:1, :], 1.0), writes=[nld])
        Sx.op("pool", lambda e: e.memset(sm[:, 20:21], 1.0), writes=[smd])
        Sx.op("pool", lambda e: e.memset(sel[:], 0.0), writes=[smd])
        Sx.op("pool", lambda e: e.memset(sel[64:65, :], 1.0), writes=[smd])
        for j in range(4):
            Sx.op("pool", lambda e: e.memset(mtmp[:], 0.0), writes=[md])
            Sx.op("pool", lambda e, j=j: e.affine_select(out=mtmp[:], in_=mtmp[:], pattern=[[1, 512]],
                                                         compare_op=ALU.is_ge, fill=NEG, base=-128 * j,
                                                         channel_multiplier=-1), writes=[md])
            Sx.op("pool", lambda e, j=j: e.tensor_copy(out=mfox[:, j, :], in_=mtmp[:]), writes=[md])
        for jj in range(5):
            j = jj - 1
            Sx.op("pool", lambda e: e.memset(mtmp[:], 0.0), writes=[md])
            Sx.op("pool", lambda e, j=j: e.affine_select(out=mtmp[:], in_=mtmp[:], pattern=[[1, 512]],
                                                         compare_op=ALU.is_ge, fill=NEG, base=-128 * j,
                                                         channel_multiplier=-1), writes=[md])
            Sx.op("pool", lambda e, j=j: e.affine_select(out=mtmp[:], in_=mtmp[:], pattern=[[-1, 512]],
                                                         compare_op=ALU.is_ge, fill=NEG, base=127 + 128 * j,
                                                         channel_multiplier=1), writes=[md])
            Sx.op("pool", lambda e, jj=jj: e.tensor_copy(out=mswa[:, jj, :], in_=mtmp[:]), writes=[md])
        for n, name in enumerate(("swa_q_norm", "swa_k_norm", "fox_q_norm", "fox_k_norm")):
            self.small_dma("asm", sm[0:64, n:n + 1], self.din[name][i].rearrange("(d o) -> d o", o=1), [smd])
        self.small_dma("asm", sm[0:64, 4:12], self.din["swa_sinks"][i:i + 1, :].to_broadcast([64, 8]), [smd])
        self.small_dma("asm", sm[0:1, 12:20], self.din["attn_forget_bias"][i:i + 1, :], [smd])
        Sx.op("act", lambda e: e.activation(out=sm[0:64, 4:12], in_=sm[0:64, 4:12], func=AF.Exp),
              reads=[smd], writes=[smd])
        Sx.op("dve", lambda e: e.tensor_scalar(out=sm[0:1, 12:20], in0=sm[0:1, 12:20], scalar1=-1.0, scalar2=None,
                                               op0=ALU.mult), reads=[smd], writes=[smd])
        Sx.dma("wfl", wfl, win[:, :, 2304:2312], writes=[wfd], qname="pool")

        B_ST, B_OT, B_PJ, B_SS, B_MS, B_OP = (0, 1), (2, 3), 4, 5, 6, 7
        cnt = {"st": 0, "ot": 0}

        def proj_norm(wst, wdep, gcol, dst, ddep):
            for tb in range(4):
                ts = slice(tb * 512, (tb + 1) * 512)
                ps = self.bank[B_PJ]
                for kc in range(8):
                    Sx.op("pe", lambda e, kc=kc, ts=ts, ps=ps: e.matmul(ps[0:64, :], wst[:, kc, :], self.xn[:, kc, ts],
                                                                         start=(kc == 0), stop=(kc == 7)),
                          reads=[wdep, self.xdep[tb]], writes=[self.bdep[B_PJ]], signal=(kc == 7))
                Sx.op("act", lambda e, ps=ps: e.activation(out=self.sq[0:64, 0, :], in_=ps[0:64, :], func=AF.Square),
                      reads=[self.bdep[B_PJ]], writes=[self.sqdep])
                p2 = self.bank[B_SS]
                Sx.op("pe", lambda e, p2=p2: e.matmul(p2[0:64, :], self.ones_b[0:64, 0:64], self.sq[0:64, 0, :],
                                                       start=True, stop=True),
                      reads=[self.sqdep, self.cdep], writes=[self.bdep[B_SS]])
                Sx.op("act", lambda e, p2=p2: e.activation(out=self.rstd[0:64, :], in_=p2[0:64, :], func=AF.Sqrt,
                                                            bias=EPS, scale=1.0 / 64),
                      reads=[self.bdep[B_SS]], writes=[self.rstdep])
                Sx.op("dve", lambda e: e.reciprocal(out=self.rstd[0:64, :], in_=self.rstd[0:64, :]),
                      reads=[self.rstdep], writes=[self.rstdep])
                Sx.op("dve", lambda e, ps=ps, ts=ts: e.scalar_tensor_tensor(
                    out=dst[0:64, ts], in0=ps[0:64, :], scalar=sm[0:64, gcol:gcol + 1], in1=self.rstd[0:64, :],
                    op0=ALU.mult, op1=ALU.mult),
                    reads=[self.bdep[B_PJ], self.rstdep, smd], writes=[ddep])

        def proj_v():
            for t4 in range(4):
                ps = self.bank[B_MS]
                for tl in range(4):
                    t = t4 * 4 + tl
                    for kc in range(8):
                        Sx.op("pe", lambda e, kc=kc, t=t, tl=tl, ps=ps: e.matmul(
                            ps[:, tl * 64:(tl + 1) * 64], self.xn[:, kc, t * 128:(t + 1) * 128], wv[:, kc, :],
                            start=(kc == 0), stop=(kc == 7)),
                            reads=[wvd, self.xdep[t // 4]], writes=[self.bdep[B_MS]],
                            signal=(kc == 7 and tl == 3))
                Sx.op("act", lambda e, ps=ps, t4=t4: e.activation(
                    out=vaug[:, t4 * 4:(t4 + 1) * 4, 0:64], in_=ps[:, 0:256].rearrange("p (a b) -> p a b", a=4),
                    func=AF.Copy), reads=[self.bdep[B_MS]], writes=[vd])

        def fox_gates(h):
            for tb in range(4):
                ts = slice(tb * 512, (tb + 1) * 512)
                ps = self.bank[B_MS]
                for kc in range(8):
                    Sx.op("pe", lambda e, kc=kc, ts=ts, ps=ps: e.matmul(ps[0:1, :], wfl[:, kc, h:h + 1], self.xn[:, kc, ts],
                                                                         start=(kc == 0), stop=(kc == 7)),
                          reads=[wfd, self.xdep[tb]], writes=[self.bdep[B_MS]], signal=(kc == 7))
                Sx.op("act", lambda e, ps=ps, ts=ts: e.activation(out=nl[0:1, ts], in_=ps[0:1, :], func=AF.Exp,
                                                                  bias=sm[0:1, 12 + h:13 + h], scale=-1.0),
                      reads=[self.bdep[B_MS], smd], writes=[nld])
            Sx.op("act", lambda e: e.activation(out=nl[0:1, :], in_=nl[0:1, :], func=AF.Ln, bias=1.0, scale=1.0),
                  reads=[nld], writes=[nld])
            Sx.op("dve", lambda e: e.tensor_tensor_scan(out=nl[0:1, :], data0=onesrow[0:1, :], data1=nl[0:1, :],
                                                        initial=0.0, op0=ALU.mult, op1=ALU.add),
                  reads=[nld], writes=[nld])
            Sx.op("dve", lambda e: e.tensor_scalar(out=ghi[0:1, :], in0=nl[0:1, :], scalar1=-8.0, scalar2=None,
                                                   op0=ALU.mult), reads=[nld], writes=[ghd])
            Sx.op("dve", lambda e: e.scalar_tensor_tensor(out=glo[0:1, :], in0=nl[0:1, :], scalar=-8.0,
                                                          in1=ghi[0:1, :], op0=ALU.mult, op1=ALU.subtract),
                  reads=[nld, ghd], writes=[ghd])
            ps = self.bank[B_MS]
            for t in range(16):
                Sx.op("pe", lambda e, t=t, ps=ps: e.matmul(ps[:, t:t + 1], nl[0:1, t * 128:(t + 1) * 128], sm[0:1, 20:21],
                                                           start=True, stop=True),
                      reads=[nld, smd], writes=[self.bdep[B_MS]], signal=(t == 15))
            Sx.op("dve", lambda e, ps=ps: e.tensor_copy(out=gtok[:], in_=ps[:, 0:16]),
                  reads=[self.bdep[B_MS]], writes=[gtd])

        def attention(pairs, fox, hh, grp, h):
            for qb in range(4):
                do_qb(pairs, fox, hh, grp, h, qb)

        def do_qb(pairs, fox, hh, grp, h, qb):
            if True:
                qs = slice(qb * 512, (qb + 1) * 512)
                plist = pairs[qb]
                ob = B_OT[cnt["ot"] % 2]
                cnt["ot"] += 1
                ot = self.bank[ob]
                pend = None

                def emit_pv(p):
                    idx, kt, pi = p
                    Sx.op("pe", lambda e, kt=kt, pi=pi: e.matmul(ot[0:65, :], vaug[:, kt, 0:65], pb[pi][:],
                                                                  start=(idx == 0), stop=(idx == len(plist) - 1)),
                          reads=[vd, pbd[pi]], writes=[self.bdep[ob]], signal=True)

                for idx, (kt, mask) in enumerate(plist):
                    sb_ = B_ST[cnt["st"] % 2]
                    pi = cnt["st"] % 2
                    cnt["st"] += 1
                    st_ = self.bank[sb_]
                    last = "qk"
                    if fox:
                        last = "glo"
                    if mask is not None:
                        last = "mask"
                    Sx.op("pe", lambda e, kt=kt, st_=st_, last=last: e.matmul(
                        st_[:], kh[0:64, kt * 128:(kt + 1) * 128], qh[0:64, qs], start=True, stop=(last == "qk")),
                          reads=[kd, qd], writes=[self.bdep[sb_]], signal=(last == "qk"))
                    if fox:
                        Sx.op("pe", lambda e, st_=st_: e.matmul(st_[:], self.ones_b[0:1, :], ghi[0:1, qs],
                                                                 start=False, stop=False),
                              reads=[ghd, self.cdep], writes=[self.bdep[sb_]], signal=False)
                        Sx.op("pe", lambda e, st_=st_, last=last: e.matmul(st_[:], self.ones_b[0:1, :], glo[0:1, qs],
                                                                            start=False, stop=(last == "glo")),
                              reads=[ghd, self.cdep], writes=[self.bdep[sb_]], signal=(last == "glo"))
                    if mask is not None:
                        Sx.op("pe", lambda e, st_=st_, mask=mask: e.matmul(st_[:], self.ident_b[:], mask,
                                                                            start=False, stop=True),
                              reads=[md, self.cdep], writes=[self.bdep[sb_]], signal=True)
                    if fox:
                        Sx.op("act", lambda e, st_=st_, pi=pi, kt=kt: e.activation(
                            out=pb[pi][:], in_=st_[:], func=AF.Exp, bias=gtok[:, kt:kt + 1], scale=0.125),
                            reads=[self.bdep[sb_], gtd], writes=[pbd[pi]])
                    else:
                        Sx.op("act", lambda e, st_=st_, pi=pi: e.activation(
                            out=pb[pi][:], in_=st_[:], func=AF.Exp, scale=0.125),
                            reads=[self.bdep[sb_]], writes=[pbd[pi]])
                    if pend is not None:
                        emit_pv(pend)
                    pend = (idx, kt, pi)
                emit_pv(pend)
                Sx.op("act", lambda e: e.activation(out=otsb[0:65, :], in_=ot[0:65, :], func=AF.Copy),
                      reads=[self.bdep[ob]], writes=[otd])
                p2 = self.bank[B_SS]
                Sx.op("pe", lambda e, p2=p2: e.matmul(p2[0:64, :], sel[0:65, 0:64], otsb[0:65, :], start=True, stop=True),
                      reads=[otd, smd], writes=[self.bdep[B_SS]])
                if fox:
                    Sx.op("dve", lambda e, p2=p2: e.reciprocal(out=rc[0:64, :], in_=p2[0:64, :]),
                          reads=[self.bdep[B_SS]], writes=[rcd])
                else:
                    Sx.op("dve", lambda e, p2=p2: e.tensor_scalar(out=rc[0:64, :], in0=p2[0:64, :],
                                                                   scalar1=sm[0:64, 4 + h:5 + h], scalar2=None,
                                                                   op0=ALU.add),
                          reads=[self.bdep[B_SS], smd], writes=[rcd])
                    Sx.op("dve", lambda e: e.reciprocal(out=rc[0:64, :], in_=rc[0:64, :]), reads=[rcd], writes=[rcd])
                Sx.op("dve", lambda e, hh=hh: e.tensor_tensor(out=og[0:64, hh, qs], in0=otsb[0:64, :], in1=rc[0:64, :],
                                                              op=ALU.mult),
                      reads=[otd, rcd], writes=[ogd[hh][qb]])

        fox_pairs, swa_pairs = [], []
        for qb in range(4):
            fp = [(kt, None) for kt in range(4 * qb)] + [(4 * qb + j, mfox[:, j, :]) for j in range(4)]
            fox_pairs.append(fp)
            sp_ = [(4 * qb + j, mswa[:, j + 1, :]) for j in range(-1, 4) if 4 * qb + j >= 0]
            swa_pairs.append(sp_)

        for grp in range(4):
            fox = grp >= 2
            Sx.dma("wo", wo[0:64], wout[grp * 256:(grp + 1) * 256, :].rearrange("(h p) d -> p h d", p=64),
                   writes=[wod], qname="pool")
            for hh in range(4):
                h = (grp % 2) * 4 + hh
                if fox:
                    qc, kc_, vc = 768 + h * 64, 1280 + h * 64, 1792 + h * 64
                else:
                    g = h // 4
                    qc, kc_, vc = h * 64, 512 + g * 64, 640 + g * 64
                Sx.dma("wq", wq, win[:, :, qc:qc + 64], writes=[wqd], qname="pool")
                Sx.dma("wk", wk, win[:, :, kc_:kc_ + 64], writes=[wkd], qname="pool")
                Sx.dma("wv", wv, win[:, :, vc:vc + 64], writes=[wvd], qname="pool")
                proj_norm(wq, wqd, 2 if fox else 0, qh, qd)
                proj_norm(wk, wkd, 3 if fox else 1, kh, kd)
                proj_v()
                if fox:
                    fox_gates(h)
                attention(fox_pairs if fox else swa_pairs, fox, hh, grp, h)
            for dc in range(8):
                for tb in range(4):
                    ts = slice(tb * 512, (tb + 1) * 512)
                    ps = self.bank[B_OP]
                    for hh in range(4):
                        Sx.op("pe", lambda e, hh=hh, dc=dc, ts=ts, ps=ps: e.matmul(
                            ps[:], wo[0:64, hh, dc * 128:(dc + 1) * 128], og[0:64, hh, ts],
                            start=(hh == 0), stop=(hh == 3)),
                            reads=[wod, ogd[hh][tb]], writes=[self.bdep[B_OP]], signal=(hh == 3))
                    Sx.op("dve", lambda e, dc=dc, ts=ts, ps=ps: e.tensor_tensor(
                        out=self.hT[:, dc, ts], in0=ps[:], in1=self.hT[:, dc, ts], op=ALU.add),
                        reads=[self.bdep[B_OP], self.hdep[dc][tb]], writes=[self.hdep[dc][tb]])

    def ssm(self, i, layer):
        Sx = self.S
        self.rmsnorm(4 + layer)
        Sx.barrier()
        win = self.din["ssm_w_in"][i].rearrange("(kc p) f -> p kc f", p=128)
        wout = self.din["ssm_w_out"][i]
        V = self.view
        off = 0
        hst, off = V(off, [128, 32, 64], F32)
        hb8, off = V(off, [128, 8, 64], BF16)
        zs8, off = V(off, [128, 8, 512], BF16)
        yg, off = V(off, [128, 8, 512], BF16)
        xs8 = self.sq
        xraw = []
        for _ in range(2):
            a_, off = V(off, [128, 516], F32)
            xraw.append(a_)
        bcraw, off = V(off, [128, 516], F32)
        cacc = self.sg
        tt = []
        for _ in range(2):
            a_, off = V(off, [128, 4, 128], F32)
            tt.append(a_)
        sqs = []
        for _ in range(2):
            a_, off = V(off, [128, 4, 128], BF16)
            sqs.append(a_)
        xdt8, off = V(off, [128, 4, 8, 64], BF16)
        xddt8, off = V(off, [128, 4, 8, 64], BF16)
        Bf, off = V(off, [128, 512], BF16)
        Cf, off = V(off, [128, 512], BF16)
        Btok, off = V(off, [128, 4, 128], BF16)
        CBT, off = V(off, [128, 512], F32)
        dtt, off = V(off, [128, 512], F32)
        ad, off = V(off, [128, 512], F32)
        acs, off = V(off, [128, 512], F32)
        dtd, off = V(off, [128, 512], F32)
        cdb, off = V(off, [128, 512], F32)
        wx, wz = [], []
        for _ in range(2):
            a_, off = V(off, [128, 8, 64], BF16)
            wx.append(a_)
            a_, off = V(off, [128, 8, 64], BF16)
            wz.append(a_)
        wB, off = V(off, [128, 8, 128], BF16)
        wC, off = V(off, [128, 8, 128], BF16)
        wdt, off = V(off, [128, 8, 32], BF16)
        wo = []
        for _ in range(2):
            a_, off = V(off, [128, 8, 128], BF16)
            wo.append(a_)
        R8, off = V(off, [128, 8, 128], F32)
        A8, off = V(off, [128, 8, 128], F32)
        dec, off = V(off, [128, 4, 128], F32)
        dfs, off = V(off, [128, 4, 128], F32)
        MT8, off = V(off, [128, 8, 128], BF16)
        Cs8, off = V(off, [128, 8, 128], BF16)
        trif, off = V(off, [128, 128], F32)
        ntrif, off = V(off, [128, 128], F32)
        onesf, off = V(off, [128, 128], F32)
        sel127, off = V(off, [128, 128], F32)
        negm4, off = V(off, [128, 4, 128], BF16)
        cwx, off = V(off, [128, 32, 4], F32)
        cbx, off = V(off, [128, 32], F32)
        cwbc, off = V(off, [128, 8, 4], F32)
        cbbc, off = V(off, [128, 8], F32)
        dsk, off = V(off, [128, 32], F32)
        ngn, off = V(off, [128, 32], F32)
        ab, off = V(off, [128, 32], F32)
        dtb, off = V(off, [128, 32], F32)
        carx, off = V(off, [128, 32, 4], F32)
        carbc, off = V(off, [128, 8, 4], F32)
        rsb = self.rstd
        names = ("bcraw", "xdt", "xddt", "Bf", "Cf", "Btok", "CBT", "lay", "wB", "wC", "wdt", "R", "A", "dec", "dfs",
                 "MT", "Cs", "cst", "par", "carx", "carbc", "rsb", "hb8", "hst")
        d = {k: Dep() for k in names}
        xrd = [Dep(), Dep()]
        cad = [Dep(), Dep()]
        ttd = [Dep(), Dep()]
        sqd = [Dep(), Dep()]
        wxd = [Dep(), Dep()]
        wzd = [Dep(), Dep()]
        wod = [Dep(), Dep()]
        zsd = [Dep() for _ in range(8)]
        xsd = [Dep() for _ in range(8)]
        ygd = [Dep() for _ in range(8)]
        cw_src = self.din["ssm_conv_w"][i]
        cb_src = self.din["ssm_conv_b"][i]
        P = lambda fn, w: Sx.op("pool", fn, writes=w)
        P(lambda e: e.memset(hst[:], 0.0), [d["hst"]])
        P(lambda e: e.memset(carx[:], 0.0), [d["carx"]])
        P(lambda e: e.memset(carbc[:], 0.0), [d["carbc"]])
        P(lambda e: e.memset(onesf[:], 1.0), [d["cst"]])
        P(lambda e: e.memset(trif[:], 1.0), [d["cst"]])
        P(lambda e: e.affine_select(out=trif[:], in_=trif[:], pattern=[[1, 128]], compare_op=ALU.is_ge, fill=0.0,
                                    base=0, channel_multiplier=-1), [d["cst"]])
        P(lambda e: e.memset(ntrif[:], -1.0), [d["cst"]])
        P(lambda e: e.affine_select(out=ntrif[:], in_=ntrif[:], pattern=[[1, 128]], compare_op=ALU.is_ge, fill=0.0,
                                    base=0, channel_multiplier=-1), [d["cst"]])
        P(lambda e: e.memset(sel127[:], 0.0), [d["cst"]])
        P(lambda e: e.affine_select(out=sel127[:], in_=sel127[:], pattern=[[0, 128]], compare_op=ALU.not_equal,
                                    fill=1.0, base=-127, channel_multiplier=1), [d["cst"]])
        Sx.op("dve", lambda e: e.tensor_scalar(out=negm4[:], in0=trif[:].unsqueeze(1).to_broadcast([128, 4, 128]),
                                               scalar1=-NEG, scalar2=NEG, op0=ALU.mult, op1=ALU.add),
              reads=[d["cst"]], writes=[d["cst"]])
        sd = lambda out, in_: self.small_dma("ssmp", out, in_, [d["par"]])
        for k in range(4):
            sd(cwx[0:64, :, k], cw_src[k, 0:2048].rearrange("(h p) -> p h", p=64))
            sd(cwbc[:, :, k], cw_src[k, 2048:3072].rearrange("(j p) -> p j", p=128))
        sd(cbx[0:64], cb_src[0:2048].rearrange("(h p) -> p h", p=64))
        sd(cbbc[:], cb_src[2048:3072].rearrange("(j p) -> p j", p=128))
        sd(dsk[:], self.din["ssm_d_skip"][i:i + 1, :].to_broadcast([128, 32]))
        sd(ngn[0:64], self.din["ssm_norm"][i].rearrange("(h p) -> p h", p=64))
        sd(ab[:], self.din["ssm_a_log"][i:i + 1, :].to_broadcast([128, 32]))
        sd(dtb[:], self.din["ssm_dt_bias"][i:i + 1, :].to_broadcast([128, 32]))
        Sx.op("act", lambda e: e.activation(out=ab[:], in_=ab[:], func=AF.Exp), reads=[d["par"]], writes=[d["par"]])
        Sx.op("dve", lambda e: e.tensor_scalar(out=ab[:], in0=ab[:], scalar1=-1.0, scalar2=None, op0=ALU.mult),
              reads=[d["par"]], writes=[d["par"]])
        Sx.dma("wdt", wdt, win[:, :, 5120:5152], writes=[d["wdt"]], qname="pool")
        B_PJ, B_D, B_Y, B_ST, B_SS, B_TR = (0, 1), (2, 3), (4, 5), 6, 7, 6
        B_MS = 7
        bk, bd = self.bank, self.bdep
        ps = bk[B_MS]
        for C in range(16):
            for kc in range(8):
                Sx.op("pe", lambda e, C=C, kc=kc: e.matmul(ps[:, C * 32:(C + 1) * 32], self.xn[:, kc, C * 128:(C + 1) * 128],
                                                           wdt[:, kc, :], start=(kc == 0), stop=(kc == 7)),
                      reads=[d["wdt"], self.xdep[C // 4]], writes=[bd[B_MS]], signal=(kc == 7 and C == 15))
        for C in range(16):
            Sx.op("dve", lambda e, C=C: e.tensor_tensor(out=dtt[:, C * 32:(C + 1) * 32], in0=ps[:, C * 32:(C + 1) * 32],
                                                        in1=dtb[:], op=ALU.add),
                  reads=[bd[B_MS], d["par"]], writes=[d["lay"]])
        Sx.op("act", lambda e: e.activation(out=dtt[:], in_=dtt[:], func=AF.Exp), reads=[d["lay"]], writes=[d["lay"]])
        Sx.op("act", lambda e: e.activation(out=dtt[:], in_=dtt[:], func=AF.Ln, bias=1.0, scale=1.0),
              reads=[d["lay"]], writes=[d["lay"]])
        for C in range(16):
            Sx.op("dve", lambda e, C=C: e.tensor_tensor(out=ad[:, C * 32:(C + 1) * 32], in0=dtt[:, C * 32:(C + 1) * 32],
                                                        in1=ab[:], op=ALU.mult),
                  reads=[d["lay"], d["par"]], writes=[d["lay"]])
        Sx.op("pe", lambda e: e.matmul(ps[:], trif[:], ad[:], start=True, stop=True),
              reads=[d["lay"], d["cst"]], writes=[bd[B_MS]])
        Sx.op("act", lambda e: e.activation(out=acs[:], in_=ps[:], func=AF.Copy), reads=[bd[B_MS]], writes=[d["lay"]])
        Sx.op("pe", lambda e: e.matmul(ps[:], sel127[:], acs[:], start=True, stop=True),
              reads=[d["lay"], d["cst"]], writes=[bd[B_MS]])
        Sx.op("act", lambda e: e.activation(out=cdb[:], in_=ps[:], func=AF.Exp), reads=[bd[B_MS]], writes=[d["lay"]])
        Sx.op("dve", lambda e: e.tensor_tensor(out=dtd[:], in0=ps[:], in1=acs[:], op=ALU.subtract),
              reads=[bd[B_MS], d["lay"]], writes=[d["lay"]])
        Sx.op("act", lambda e: e.activation(out=dtd[:], in_=dtd[:], func=AF.Exp), reads=[d["lay"]], writes=[d["lay"]])
        Sx.op("dve", lambda e: e.tensor_tensor(out=dtd[:], in0=dtd[:], in1=dtt[:], op=ALU.mult),
              reads=[d["lay"]], writes=[d["lay"]])
        cnt = {"pj": 0, "x": 0, "ep": 0, "wo": 0}

        def conv_silu(raw, rdep, np_, wsl, bsl, car, cdep, out, odep, pb_):
            ca = cacc[cnt["x"] % 2]
            cd_ = cad[cnt["x"] % 2]
            cnt["x"] += 1
            Sx.op("act", lambda e: e.activation(out=raw[0:np_, 4:516], in_=bk[pb_][0:np_, :], func=AF.Copy),
                  reads=[bd[pb_]], writes=[rdep])
            Sx.op("dve", lambda e: e.tensor_copy(out=raw[0:np_, 1:4], in_=car[:, 0:3]), reads=[cdep], writes=[rdep])
            Sx.op("act", lambda e: e.activation(out=ca[0:np_, :], in_=raw[0:np_, 1:513], func=AF.Identity,
                                                bias=bsl, scale=wsl[:, 0:1]),
                  reads=[rdep, d["par"]], writes=[cd_])
            for k in range(1, 4):
                Sx.op("dve", lambda e, k=k: e.scalar_tensor_tensor(out=ca[0:np_, :], in0=raw[0:np_, 1 + k:513 + k],
                                                                   scalar=wsl[:, k:k + 1], in1=ca[0:np_, :],
                                                                   op0=ALU.mult, op1=ALU.add),
                      reads=[rdep, cd_, d["par"]], writes=[cd_])
            Sx.op("dve", lambda e: e.tensor_copy(out=car[:, 0:3], in_=raw[0:np_, 513:516]), reads=[rdep], writes=[cdep])
            Sx.op("act", lambda e: e.activation(out=out, in_=ca[0:np_, :], func=AF.Silu), reads=[cd_], writes=[odep])

        def proj(wst, wdep, np_, tb):
            ts = slice(tb * 512, (tb + 1) * 512)
            pb_ = B_PJ[cnt["pj"] % 2]
            cnt["pj"] += 1
            for kc in range(8):
                Sx.op("pe", lambda e, kc=kc: e.matmul(bk[pb_][0:np_, :], wst[:, kc, :], self.xn[:, kc, ts],
                                                      start=(kc == 0), stop=(kc == 7)),
                      reads=[wdep, self.xdep[tb]], writes=[bd[pb_]], signal=(kc == 7))
            return pb_

        def head_prep(tb, g, r):
            h = g * 8 + r
            wi = h % 2
            Sx.dma("wz%d" % wi, wz[wi], win[:, :, h * 64:(h + 1) * 64], writes=[wzd[wi]], qname="pool")
            Sx.dma("wx%d" % wi, wx[wi], win[:, :, 2048 + h * 64:2048 + (h + 1) * 64], writes=[wxd[wi]], qname="pool")
            pb_ = proj(wz[wi], wzd[wi], 64, tb)
            Sx.op("act", lambda e, pb_=pb_: e.activation(out=zs8[0:64, r, :], in_=bk[pb_][0:64, :], func=AF.Silu),
                  reads=[bd[pb_]], writes=[zsd[r]])
            pb_ = proj(wx[wi], wxd[wi], 64, tb)
            conv_silu(xraw[wi], xrd[wi], 64, cwx[0:64, h, :], cbx[0:64, h:h + 1], carx[0:64, h, :], d["carx"],
                      xs8[0:64, r, :], xsd[r], pb_)

        DBG = int(os.environ.get('SSM_DBG', '99'))

        def chunk(tb, g, cl):
            C = tb * 4 + cl
            c0 = C * 32 + g * 8
            cs_ = slice(cl * 128, (cl + 1) * 128)
            for r in range(8):
                Sx.op("pe", lambda e, r=r: e.matmul(bk[B_TR][:, r * 64:(r + 1) * 64], xs8[0:64, r, cs_],
                                                    self.ident_b[0:64, 0:64], start=True, stop=True),
                      reads=[xsd[r], self.cdep], writes=[bd[B_TR]], signal=(r == 7))
            trv = bk[B_TR][:].rearrange("p (a b) -> p a b", a=8)
            Sx.op("dve", lambda e: e.tensor_tensor(out=xdt8[:, cl, :, :], in0=trv,
                                                   in1=dtt[:, c0:c0 + 8].unsqueeze(2).to_broadcast([128, 8, 64]),
                                                   op=ALU.mult), reads=[bd[B_TR], d["lay"]], writes=[d["xdt"]])
            Sx.op("dve", lambda e: e.tensor_tensor(out=xddt8[:, cl, :, :], in0=trv,
                                                   in1=dtd[:, c0:c0 + 8].unsqueeze(2).to_broadcast([128, 8, 64]),
                                                   op=ALU.mult), reads=[bd[B_TR], d["lay"]], writes=[d["xddt"]])
            adb = ad[:, c0:c0 + 8].unsqueeze(2).to_broadcast([128, 8, 128])
            Sx.op("dve", lambda e: e.tensor_tensor(out=R8[:], in0=trif[:].unsqueeze(1).to_broadcast([128, 8, 128]),
                                                   in1=adb, op=ALU.mult),
                  reads=[d["cst"], d["lay"]], writes=[d["R"]])
            Sx.op("act", lambda e: e.activation(out=A8[:], in_=adb, func=AF.Copy), reads=[d["lay"]], writes=[d["A"]])
            def decay_half(hf):
                db = B_D[hf]
                hs = slice(hf * 4, hf * 4 + 4)
                dv = bk[db][:].rearrange("p (a b) -> p a b", a=4)
                Sx.op("pe", lambda e: e.matmul(dv, onesf[:], R8[:, hs, :], start=True, stop=True),
                      reads=[d["R"], d["cst"]], writes=[bd[db]])
                Sx.op("act", lambda e: e.activation(out=dfs[:], in_=dv, func=AF.Exp), reads=[bd[db]], writes=[d["dfs"]])
                Sx.op("pe", lambda e: e.matmul(dv, ntrif[:], A8[:, hs, :], start=False, stop=False,
                                               skip_group_check=True),
                      reads=[d["A"], d["cst"]], writes=[bd[db]], signal=False)
                Sx.op("pe", lambda e: e.matmul(dv, self.ident_b[:], negm4[:], start=False, stop=True,
                                               skip_group_check=True),
                      reads=[d["cst"], self.cdep], writes=[bd[db]])
                Sx.op("act", lambda e: e.activation(out=dec[:], in_=dv, func=AF.Exp), reads=[bd[db]], writes=[d["dec"]])
                Sx.op("dve", lambda e: e.tensor_tensor(
                    out=MT8[:, hs, :], in0=dec[:], in1=CBT[:, cs_].unsqueeze(1).to_broadcast([128, 4, 128]),
                    op=ALU.mult), reads=[d["dec"], d["CBT"]], writes=[d["MT"]])
                Sx.op("dve", lambda e: e.tensor_tensor(out=Cs8[:, hs, :], in0=dfs[:],
                                                       in1=Cf[:, cs_].unsqueeze(1).to_broadcast([128, 4, 128]),
                                                       op=ALU.mult),
                      reads=[d["dfs"], d["Cf"]], writes=[d["Cs"]])
            for hf in range(2):
                decay_half(hf)

            def y_half(hf):
                yb = B_Y[hf]
                for rr in range(4):
                    r = hf * 4 + rr
                    Sx.op("pe", lambda e, r=r, rr=rr: e.matmul(bk[yb][0:64, rr * 128:(rr + 1) * 128], xdt8[:, cl, r, :],
                                                               MT8[:, r, :], start=True, stop=False),
                          reads=[d["xdt"], d["MT"]], writes=[bd[yb]], signal=False)
                    Sx.op("pe", lambda e, r=r, rr=rr: e.matmul(bk[yb][0:64, rr * 128:(rr + 1) * 128], hb8[:, r, :],
                                                               Cs8[:, r, :], start=False, stop=True),
                          reads=[d["hb8"], d["Cs"]], writes=[bd[yb]], signal=(rr == 3))
            for hf in range(2):
                y_half(hf)
            Sx.op("pe", lambda e: e.matmul(bk[B_ST][:], Btok[:, cl, :], xddt8[:, cl, :, :].rearrange("p a b -> p (a b)"),
                                           start=True, stop=True),
                  reads=[d["Btok"], d["xddt"]], writes=[bd[B_ST]])
            hv = hst[:, g * 8:(g + 1) * 8, :]
            Sx.op("dve", lambda e: e.tensor_tensor(out=hv, in0=hv,
                                                   in1=cdb[:, c0:c0 + 8].unsqueeze(2).to_broadcast([128, 8, 64]),
                                                   op=ALU.mult), reads=[d["lay"], d["hst"]], writes=[d["hst"]])
            Sx.op("dve", lambda e: e.tensor_tensor(out=hv, in0=bk[B_ST][:].rearrange("p (a b) -> p a b", a=8), in1=hv,
                                                   op=ALU.add), reads=[bd[B_ST], d["hst"]], writes=[d["hst"]])
            Sx.op("act", lambda e: e.activation(out=hb8[:], in_=hv, func=AF.Copy), reads=[d["hst"]], writes=[d["hb8"]])
            def epi_half(hf):
                yb = B_Y[hf]
                hs = slice(hf * 4, hf * 4 + 4)
                ei = cnt["ep"] % 2
                cnt["ep"] += 1
                t_, td_ = tt[ei], ttd[ei]
                q_, qd_ = sqs[ei], sqd[ei]
                h0 = g * 8 + hf * 4
                yv = bk[yb][0:64, :].rearrange("p (a b) -> p a b", a=4)
                Sx.op("dve", lambda e: e.tensor_tensor(out=t_[0:64], in0=xs8[0:64, hs, cs_],
                                                       in1=dsk[0:64, h0:h0 + 4].unsqueeze(2).to_broadcast([64, 4, 128]),
                                                       op=ALU.mult),
                      reads=[xsd[hf * 4 + k] for k in range(4)] + [d["par"]], writes=[td_])
                Sx.op("dve", lambda e: e.tensor_tensor(out=t_[0:64], in0=yv, in1=t_[0:64], op=ALU.add),
                      reads=[bd[yb], td_], writes=[td_])
                Sx.op("dve", lambda e: e.tensor_tensor(out=t_[0:64], in0=t_[0:64], in1=zs8[0:64, hs, cs_], op=ALU.mult),
                      reads=[td_] + [zsd[hf * 4 + k] for k in range(4)], writes=[td_])
                Sx.op("act", lambda e: e.activation(out=q_[0:64], in_=t_[0:64], func=AF.Square), reads=[td_], writes=[qd_])
                for rr in range(4):
                    first = (hf == 0 and rr == 0)
                    last = (hf == 1 and rr == 3)
                    Sx.op("pe", lambda e, rr=rr, first=first, last=last: e.matmul(
                        bk[B_SS][:, cs_], self.ones_b[0:64, :], q_[0:64, rr, :], start=first, stop=last,
                        skip_group_check=True),
                          reads=[qd_, self.cdep], writes=[bd[B_SS]], signal=(rr == 3))
                Sx.op("dve", lambda e: e.tensor_tensor(out=yg[0:64, hs, cs_], in0=t_[0:64],
                                                       in1=ngn[0:64, h0:h0 + 4].unsqueeze(2).to_broadcast([64, 4, 128]),
                                                       op=ALU.mult),
                      reads=[td_, d["par"]], writes=[ygd[hf * 4 + k] for k in range(4)])

            for hf in range(2):
                epi_half(hf)

        def group(tb, g):
            ts = slice(tb * 512, (tb + 1) * 512)
            Sx.dma("wB", wB, win[:, :, 4096 + g * 128:4096 + (g + 1) * 128], writes=[d["wB"]], qname="pool")
            Sx.dma("wC", wC, win[:, :, 4608 + g * 128:4608 + (g + 1) * 128], writes=[d["wC"]], qname="pool")
            pb_ = proj(wB, d["wB"], 128, tb)
            conv_silu(bcraw, d["bcraw"], 128, cwbc[:, g, :], cbbc[:, g:g + 1], carbc[:, g, :], d["carbc"], Bf[:], d["Bf"], pb_)
            pb_ = proj(wC, d["wC"], 128, tb)
            conv_silu(bcraw, d["bcraw"], 128, cwbc[:, 4 + g, :], cbbc[:, 4 + g:5 + g], carbc[:, 4 + g, :], d["carbc"],
                      Cf[:], d["Cf"], pb_)
            for cl in range(4):
                cs_ = slice(cl * 128, (cl + 1) * 128)
                Sx.op("pe", lambda e, cs_=cs_: e.matmul(bk[B_TR][:, cs_], Bf[:, cs_], self.ident_b[:], start=True, stop=True),
                      reads=[d["Bf"], self.cdep], writes=[bd[B_TR]], signal=(cl == 3))
            Sx.op("act", lambda e: e.activation(out=Btok[:], in_=bk[B_TR][:].rearrange("p (a b) -> p a b", a=4),
                                                func=AF.Copy), reads=[bd[B_TR]], writes=[d["Btok"]])
            for cl in range(4):
                cs_ = slice(cl * 128, (cl + 1) * 128)
                Sx.op("pe", lambda e, cs_=cs_: e.matmul(bk[B_TR][:, cs_], Bf[:, cs_], Cf[:, cs_], start=True, stop=True),
                      reads=[d["Bf"], d["Cf"]], writes=[bd[B_TR]], signal=(cl == 3))
            Sx.op("dve", lambda e: e.tensor_tensor(out=CBT[:].rearrange("p (a b) -> p a b", a=4),
                                                   in0=bk[B_TR][:].rearrange("p (a b) -> p a b", a=4),
                                                   in1=trif[:].unsqueeze(1).to_broadcast([128, 4, 128]), op=ALU.mult),
                  reads=[bd[B_TR], d["cst"]], writes=[d["CBT"]])
            Sx.op("act", lambda e: e.activation(out=hb8[:], in_=hst[:, g * 8:(g + 1) * 8, :], func=AF.Copy),
                  reads=[d["hst"]], writes=[d["hb8"]])
            for r in range(8):
                head_prep(tb, g, r)
            for cl in range(4):
                chunk(tb, g, cl)
            Sx.op("act", lambda e: e.activation(out=rsb[:], in_=bk[B_SS][:], func=AF.Sqrt, bias=EPS, scale=1.0 / 512),
                  reads=[bd[B_SS]], writes=[d["rsb"]])
            Sx.op("dve", lambda e: e.reciprocal(out=rsb[:], in_=rsb[:]), reads=[d["rsb"]], writes=[d["rsb"]])
            for dc in range(8):
                wi = cnt["wo"] % 2
                cnt["wo"] += 1
                Sx.dma("wo%d" % wi, wo[wi][0:64],
                       wout[g * 512:(g + 1) * 512, dc * 128:(dc + 1) * 128].rearrange("(h p) d -> p h d", p=64),
                       writes=[wod[wi]], qname="pool")
                ob = B_PJ[cnt["pj"] % 2]
                cnt["pj"] += 1
                for r in range(8):
                    Sx.op("pe", lambda e, r=r, wi=wi, ob=ob: e.matmul(bk[ob][:], wo[wi][0:64, r, :], yg[0:64, r, :],
                                                                      start=(r == 0), stop=(r == 7)),
                          reads=[wod[wi], ygd[r]], writes=[bd[ob]], signal=(r == 7))
                ei = cnt["ep"] % 2
                cnt["ep"] += 1
                t_ = tt[ei][:].rearrange("p a b -> p (a b)")
                td_ = ttd[ei]
                Sx.op("dve", lambda e, ob=ob, t_=t_: e.tensor_tensor(out=t_, in0=bk[ob][:], in1=rsb[:], op=ALU.mult),
                      reads=[bd[ob], d["rsb"]], writes=[td_])
                Sx.op("dve", lambda e, dc=dc, t_=t_: e.tensor_tensor(out=self.hT[:, dc, ts], in0=t_, in1=self.hT[:, dc, ts],
                                                                      op=ALU.add),
                      reads=[td_, self.hdep[dc][tb]], writes=[self.hdep[dc][tb]])

        for tb in range(4 if DBG >= 99 else 1):
            for g in range(4 if DBG >= 99 else 1):
                group(tb, g)


ALL_PHASES = []
for _l in range(4):
    ALL_PHASES.append(("ffn1", _l))
    ALL_PHASES.append(("attn" if _l % 2 == 0 else "ssm", _l))
    ALL_PHASES.append(("ffn2", _l))


def run(inputs, phases, n_cores=8, trace=False):
    nc = Builder(phases).build()
    x = np.ascontiguousarray(inputs["x"], dtype=np.float32)
    in_maps = []
    for c in range(n_cores):
        m = {"x": x[c]}
        for k in INPUT_SHAPES:
            m[k] = np.ascontiguousarray(inputs[k], dtype=np.float32)
        in_maps.append(m)
    res = run_bass_kernel_spmd(nc, in_maps, core_ids=list(range(n_cores)), trace=trace)
    out = np.stack([r["y"] for r in res.results], axis=0)
    return out, res


def kernel(**inputs):
    out, _ = run(inputs, ALL_PHASES, 8)
    return out.astype(np.float32)
```

```python
from contextlib import ExitStack
import os
import numpy as np
import concourse.bass as bass
import concourse.mybir as mybir
from concourse.bass_utils import run_bass_kernel_spmd

F32 = mybir.dt.float32
BF16 = mybir.dt.bfloat16
AF = mybir.ActivationFunctionType
ALU = mybir.AluOpType
AX = mybir.AxisListType

D = 1024
S = 2048
DFF = 2816
NF = DFF // 128
EPS = 1e-6
ATT_W = 2312
SSM_W = 5152
NEG = -30000.0

INPUT_SHAPES = {
    "ffn1_norm": [4, 1024], "ffn1_w_gate": [4, 1024, 2816], "ffn1_w_up": [4, 1024, 2816],
    "ffn1_w_down": [4, 2816, 1024], "mix_norm": [4, 1024], "ffn2_norm": [4, 1024],
    "ffn2_w_gate": [4, 1024, 2816], "ffn2_w_up": [4, 1024, 2816], "ffn2_w_down": [4, 2816, 1024],
    "attn_w_in": [2, 1024, 2312], "attn_forget_bias": [2, 8], "swa_q_norm": [2, 64],
    "swa_k_norm": [2, 64], "swa_sinks": [2, 8], "fox_q_norm": [2, 64], "fox_k_norm": [2, 64],
    "attn_w_out": [2, 1024, 1024], "ssm_w_in": [2, 1024, 5152], "ssm_conv_w": [2, 4, 3072],
    "ssm_conv_b": [2, 3072], "ssm_dt_bias": [2, 32], "ssm_a_log": [2, 32], "ssm_d_skip": [2, 32],
    "ssm_norm": [2, 2048], "ssm_w_out": [2, 2048, 1024],
}


class Dep:
    __slots__ = ("w", "r")

    def __init__(self):
        self.w = None
        self.r = []


class Queue:
    def __init__(self, name, sem, is_pe=False):
        self.name = name
        self.sem = sem
        self.count = 0
        self.ops = []
        self.waited = {}
        self.is_pe = is_pe


class Sched:
    def __init__(self, nc, stack):
        self.nc = nc
        self.stack = stack
        self.q = {}
        for name in ("pe", "act", "dve", "pool", "sp"):
            sem = stack.enter_context(nc.semaphore("s_" + name))
            self.q[name] = Queue(name, sem, name == "pe")
        self.dma_sems = {}

    def dma_sem(self, name):
        if name not in self.dma_sems:
            sem = self.stack.enter_context(self.nc.semaphore("d_" + name))
            self.dma_sems[name] = [sem, 0]
        return self.dma_sems[name]

    def _collect(self, q, reads, writes, dma_group=None):
        need = {}

        def add(tok):
            if tok is None:
                return
            sem, val, owner, grp = tok
            if owner == q.name and q.is_pe:
                return
            if dma_group is not None and grp is dma_group:
                return
            k = id(sem)
            if q.waited.get(k, 0) >= val:
                return
            if k not in need or need[k][1] < val:
                need[k] = (sem, val)

        for d in reads:
            add(d.w)
        for d in writes:
            add(d.w)
            for t in d.r:
                add(t)
        waits = list(need.values())
        for sem, val in waits:
            q.waited[id(sem)] = val
        return waits

    def op(self, qname, fn, reads=(), writes=(), signal=True):
        q = self.q[qname]
        waits = self._collect(q, reads, writes)
        tok = (q.sem, q.count + 1, q.name, None)
        if signal:
            q.count += 1
        q.ops.append((waits, fn, (q.sem, 1) if signal else None))
        for d in reads:
            d.r.append(tok)
        for d in writes:
            d.w = tok
            d.r = []
        return tok

    def dma(self, semname, out, in_, reads=(), writes=(), qname="sp"):
        q = self.q[qname]
        ent = self.dma_sem(semname)
        waits = self._collect(q, reads, writes, dma_group=ent)
        ent[1] += 16
        tok = (ent[0], ent[1], None, ent)
        q.ops.append((waits, lambda e: e.dma_start(out=out, in_=in_), (ent[0], 16)))
        for d in reads:
            d.r.append(tok)
        for d in writes:
            d.w = tok
            d.r = []
        return tok

    def barrier(self):
        toks = [(q.sem, q.count) for q in self.q.values() if q.count > 0]
        toks += [(ent[0], ent[1]) for ent in self.dma_sems.values() if ent[1] > 0]
        for q in self.q.values():
            waits = []
            for sem, val in toks:
                if sem is q.sem and q.is_pe:
                    continue
                if q.waited.get(id(sem), 0) >= val:
                    continue
                q.waited[id(sem)] = val
                waits.append((sem, val))
            q.ops.append((waits, None, None))

    def finish(self, final_tokens):
        nc = self.nc
        fin = {}
        for sem, val, owner, grp in final_tokens:
            k = id(sem)
            if k not in fin or fin[k][1] < val:
                fin[k] = (sem, val)
        engs = {"pe": "tensor", "act": "scalar", "dve": "vector", "pool": "gpsimd", "sp": "sync"}
        with nc.Block() as block:
            for name, attr in engs.items():
                q = self.q[name]

                def body(eng, q=q, name=name):
                    for waits, fn, inc in q.ops:
                        for sem, val in waits:
                            eng.wait_ge(sem, val)
                        if fn is None:
                            continue
                        ins = fn(eng)
                        if inc is not None:
                            ins.then_inc(inc[0], inc[1])
                    if name == "sp":
                        for sem, val in fin.values():
                            eng.wait_ge(sem, val)

                getattr(block, attr)(body)


class Builder:
    def __init__(self, phases):
        self.phases = phases
        self.nc = bass.Bass("TRN2", target_bir_lowering=False)

    def build(self):
        nc = self.nc
        self.din = {}
        self.x = nc.dram_tensor("x", [S, D], F32, kind="ExternalInput").ap()
        for k, shp in INPUT_SHAPES.items():
            self.din[k] = nc.dram_tensor(k, shp, F32, kind="ExternalInput").ap()
        self.y = nc.dram_tensor("y", [S, D], F32, kind="ExternalOutput").ap()
        with ExitStack() as st:
            self.st = st
            self.S = Sched(nc, st)
            self.alloc()
            self.consts()
            self.load_x()
            for ph in self.phases:
                kind, layer = ph
                self.S.barrier()
                if kind == "ffn1":
                    self.ffn(layer, "ffn1")
                elif kind == "ffn2":
                    self.ffn(layer, "ffn2")
                elif kind == "attn":
                    self.attn(layer // 2, layer)
                elif kind == "ssm":
                    self.ssm(layer // 2, layer)
            self.S.barrier()
            toks = self.store_y()
            self.S.finish(toks)
        return nc

    def sb(self, name, shape, dtype):
        return self.st.enter_context(self.nc.sbuf_tensor(name, shape, dtype))

    def alloc(self):
        nc = self.nc
        self.hT = self.sb("hT", [128, 8, S], F32)
        self.hdep = [[Dep() for _ in range(4)] for _ in range(8)]
        self.xn = self.sb("xn", [128, 8, S], BF16)
        self.xdep = [Dep() for _ in range(4)]
        self.bank = [self.st.enter_context(nc.psum_tensor("pb%d" % i, [128, 512], F32)) for i in range(8)]
        self.bdep = [Dep() for _ in range(8)]
        self.SCR = 23 * 1024 + 512
        self.scr = self.sb("scr", [128, self.SCR], F32)
        self.ident_f = self.sb("ident_f", [128, 128], F32)
        self.ident_b = self.sb("ident_b", [128, 128], BF16)
        self.ones_b = self.sb("ones_b", [128, 128], BF16)
        self.cdep = Dep()
        self.sq = self.sb("sq", [128, 8, 512], BF16)
        self.sqdep = Dep()
        self.rstd = self.sb("rstd", [128, 512], F32)
        self.rstdep = Dep()
        self.gains = self.sb("gains", [128, 12, 8], F32)
        self.gdep = Dep()
        self.sg = [self.sb("sg%d" % i, [128, 512], F32) for i in range(2)]
        self.sgdep = [Dep(), Dep()]
        self.io = [self.scr[:, i * 1024:(i + 1) * 1024] for i in range(2)]
        self.iodep = [Dep(), Dep()]

    def view(self, off_words, shape, dtype):
        n = 1
        for s in shape[1:]:
            n *= s
        if dtype == BF16:
            words = (n + 1) // 2
            ap = self.scr[:, off_words:off_words + words].bitcast(BF16)
        else:
            words = n
            ap = self.scr[:, off_words:off_words + words]
        assert off_words + words <= self.SCR, (off_words, words)
        if len(shape) == 3:
            ap = ap.rearrange("p (a b) -> p a b", a=shape[1])
        elif len(shape) == 4:
            ap = ap.rearrange("p (a b c) -> p a b c", a=shape[1], b=shape[2])
        if shape[0] != 128:
            ap = ap[0:shape[0]]
        return ap, off_words + words

    def consts(self):
        Sx = self.S
        nc = self.nc
        idf, idb, ones = self.ident_f, self.ident_b, self.ones_b
        Sx.op("pool", lambda e: e.memset(idf[:], 0.0), writes=[self.cdep])
        Sx.op("pool", lambda e: e.affine_select(out=idf[:], in_=idf[:], pattern=[[-1, 128]],
                                                compare_op=ALU.not_equal, fill=1.0, base=0,
                                                channel_multiplier=1), writes=[self.cdep])
        Sx.op("dve", lambda e: e.tensor_copy(out=idb[:], in_=idf[:]), reads=[self.cdep], writes=[self.cdep])
        Sx.op("dve", lambda e: e.memset(ones[:], 1.0), writes=[self.cdep])

    def _small_dma(self, out, in_):
        nc = self.nc

        def fn(e):
            with nc.allow_non_contiguous_dma(reason="tiny vectors"):
                return e.dma_start(out=out, in_=in_)
        return fn

    def small_dma(self, semname, out, in_, writes):
        Sx = self.S
        q = Sx.q["sp"]
        ent = Sx.dma_sem(semname)
        waits = Sx._collect(q, (), writes, dma_group=ent)
        ent[1] += 16
        tok = (ent[0], ent[1], None, ent)
        q.ops.append((waits, self._small_dma(out, in_), (ent[0], 16)))
        for d in writes:
            d.w = tok
            d.r = []
        return tok

    def load_x(self):
        Sx = self.S
        for n, name in enumerate(("ffn1_norm", "mix_norm", "ffn2_norm")):
            src = self.din[name].rearrange("l (c p) -> p l c", p=128)
            self.small_dma("gains", self.gains[:, n * 4:(n + 1) * 4, :], src, [self.gdep])
        for tt in range(16):
            io, iod = self.io[tt % 2], self.iodep[tt % 2]
            Sx.dma("xin%d" % (tt % 2), io[:], self.x[tt * 128:(tt + 1) * 128, :], writes=[iod])
            for half in range(2):
                b = 6 + half
                ps, pd = self.bank[b], self.bdep[b]
                for c4 in range(4):
                    c = half * 4 + c4
                    Sx.op("pe", lambda e, ps=ps, io=io, c=c, c4=c4: e.transpose(
                        out=ps[:, c4 * 128:(c4 + 1) * 128], in_=io[:, c * 128:(c + 1) * 128],
                        identity=self.ident_f[:]), reads=[iod, self.cdep], writes=[pd], signal=(c4 == 3))
                tb = tt // 4
                dst = self.hT[:, half * 4:half * 4 + 4, tt * 128:(tt + 1) * 128]
                src = ps[:].rearrange("p (a b) -> p a b", a=4)
                eng = "act" if half == 0 else "dve"
                if eng == "act":
                    fn = lambda e, dst=dst, src=src: e.activation(out=dst, in_=src, func=AF.Copy)
                else:
                    fn = lambda e, dst=dst, src=src: e.tensor_copy(out=dst, in_=src)
                Sx.op(eng, fn, reads=[pd], writes=[self.hdep[half * 4 + c4][tb] for c4 in range(4)])

    def store_y(self):
        Sx = self.S
        toks = []
        for tt in range(16):
            io, iod = self.io[tt % 2], self.iodep[tt % 2]
            tb = tt // 4
            for half in range(2):
                b = 6 + half
                ps, pd = self.bank[b], self.bdep[b]
                for c4 in range(4):
                    c = half * 4 + c4
                    Sx.op("pe", lambda e, ps=ps, c=c, c4=c4, tt=tt: e.transpose(
                        out=ps[:, c4 * 128:(c4 + 1) * 128], in_=self.hT[:, c, tt * 128:(tt + 1) * 128],
                        identity=self.ident_f[:]), reads=[self.hdep[c][tb], self.cdep], writes=[pd],
                        signal=(c4 == 3))
                dst = io[:, half * 512:(half + 1) * 512]
                if half == 0:
                    fn = lambda e, dst=dst, ps=ps: e.activation(out=dst, in_=ps[:], func=AF.Copy)
                    Sx.op("act", fn, reads=[pd], writes=[iod])
                else:
                    fn = lambda e, dst=dst, ps=ps: e.tensor_copy(out=dst, in_=ps[:])
                    Sx.op("dve", fn, reads=[pd], writes=[iod])
            toks.append(Sx.dma("yout%d" % (tt % 2), self.y[tt * 128:(tt + 1) * 128, :], io[:], reads=[iod]))
        return toks

    def rmsnorm(self, gidx):
        Sx = self.S
        nb = 5
        for tb in range(4):
            ts = slice(tb * 512, (tb + 1) * 512)
            for c in range(8):
                Sx.op("act", lambda e, c=c, ts=ts: e.activation(out=self.sq[:, c, :], in_=self.hT[:, c, ts],
                                                                func=AF.Square),
                      reads=[self.hdep[c][tb]], writes=[self.sqdep])
            ps, pd = self.bank[nb], self.bdep[nb]
            for c in range(8):
                Sx.op("pe", lambda e, c=c, ps=ps: e.matmul(ps[:], self.ones_b[:], self.sq[:, c, :],
                                                            start=(c == 0), stop=(c == 7)),
                      reads=[self.sqdep, self.cdep], writes=[pd], signal=(c == 7))
            Sx.op("act", lambda e, ps=ps: e.activation(out=self.rstd[:], in_=ps[:], func=AF.Sqrt,
                                                        bias=EPS, scale=1.0 / D),
                  reads=[pd], writes=[self.rstdep])
            Sx.op("dve", lambda e: e.reciprocal(out=self.rstd[:], in_=self.rstd[:]),
                  reads=[self.rstdep], writes=[self.rstdep])
            for c in range(8):
                Sx.op("dve", lambda e, c=c, ts=ts: e.scalar_tensor_tensor(
                    out=self.xn[:, c, ts], in0=self.hT[:, c, ts], scalar=self.gains[:, gidx, c:c + 1],
                    in1=self.rstd[:], op0=ALU.mult, op1=ALU.mult),
                    reads=[self.hdep[c][tb], self.rstdep, self.gdep], writes=[self.xdep[tb]])

    def ffn(self, layer, which):
        Sx = self.S
        gidx = (0 if which == "ffn1" else 2) * 4 + layer
        self.rmsnorm(gidx)
        wg = self.din[which + "_w_gate"][layer].rearrange("(kc p) f -> p kc f", p=128)
        wu = self.din[which + "_w_up"][layer].rearrange("(kc p) f -> p kc f", p=128)
        wd = self.din[which + "_w_down"][layer].rearrange("(fc p) d -> p fc d", p=128)
        off = 0
        hff, off = self.view(off, [128, NF, 1024], BF16)
        hfdep = [[Dep() for _ in range(2)] for _ in range(NF)]
        gst, ust, gud = [], [], []
        for i in range(3):
            a, off = self.view(off, [128, 8, 256], BF16)
            b, off = self.view(off, [128, 8, 256], BF16)
            gst.append(a)
            ust.append(b)
            gud.append(Dep())
        dst, ddd = [], []
        for i in range(2):
            a, off = self.view(off, [128, NF, 256], BF16)
            dst.append(a)
            ddd.append(Dep())
        step = 0
        for blk in range(2):
            for grp in range(NF // 2):
                si = grp % 3
                cs = slice(grp * 256, (grp + 1) * 256)
                Sx.dma("wgu%d" % si, gst[si], wg[:, :, cs], writes=[gud[si]], qname="pool")
                Sx.dma("wgu%d" % si, ust[si], wu[:, :, cs], writes=[gud[si]], qname="pool")
                for cc in range(2):
                    j = grp * 2 + cc
                    for half in range(2):
                        tb = blk * 2 + half
                        ts = slice(tb * 512, (tb + 1) * 512)
                        gb, ub = step % 2, 2 + step % 2
                        gps, ups = self.bank[gb], self.bank[ub]
                        for kc in range(8):
                            Sx.op("pe", lambda e, gps=gps, si=si, kc=kc, cc=cc, ts=ts: e.matmul(
                                gps[:], gst[si][:, kc, cc * 128:(cc + 1) * 128], self.xn[:, kc, ts],
                                start=(kc == 0), stop=(kc == 7)),
                                reads=[gud[si], self.xdep[tb]], writes=[self.bdep[gb]], signal=(kc == 7))
                        for kc in range(8):
                            Sx.op("pe", lambda e, ups=ups, si=si, kc=kc, cc=cc, ts=ts: e.matmul(
                                ups[:], ust[si][:, kc, cc * 128:(cc + 1) * 128], self.xn[:, kc, ts],
                                start=(kc == 0), stop=(kc == 7)),
                                reads=[gud[si], self.xdep[tb]], writes=[self.bdep[ub]], signal=(kc == 7))
                        sg, sgd = self.sg[step % 2], self.sgdep[step % 2]
                        Sx.op("act", lambda e, sg=sg, gps=gps: e.activation(out=sg[:], in_=gps[:], func=AF.Silu),
                              reads=[self.bdep[gb]], writes=[sgd])
                        Sx.op("dve", lambda e, sg=sg, ups=ups, j=j, half=half: e.tensor_tensor(
                            out=hff[:, j, half * 512:(half + 1) * 512], in0=ups[:], in1=sg[:], op=ALU.mult),
                            reads=[self.bdep[ub], sgd], writes=[hfdep[j][half]])
                        step += 1
            for dg in range(4):
                si = dg % 2
                Sx.dma("wd%d" % si, dst[si], wd[:, :, dg * 256:(dg + 1) * 256], writes=[ddd[si]], qname="pool")
                for cc in range(2):
                    dc = dg * 2 + cc
                    for half in range(2):
                        tb = blk * 2 + half
                        ts = slice(tb * 512, (tb + 1) * 512)
                        b = 4 + step % 2
                        ps = self.bank[b]
                        for f in range(NF):
                            Sx.op("pe", lambda e, ps=ps, si=si, f=f, cc=cc, half=half: e.matmul(
                                ps[:], dst[si][:, f, cc * 128:(cc + 1) * 128],
                                hff[:, f, half * 512:(half + 1) * 512],
                                start=(f == 0), stop=(f == NF - 1)),
                                reads=[ddd[si], hfdep[f][half]], writes=[self.bdep[b]], signal=(f == NF - 1))
                        Sx.op("dve", lambda e, ps=ps, dc=dc, ts=ts: e.scalar_tensor_tensor(
                            out=self.hT[:, dc, ts], in0=ps[:], scalar=0.5, in1=self.hT[:, dc, ts],
                            op0=ALU.mult, op1=ALU.add),
                            reads=[self.bdep[b], self.hdep[dc][tb]], writes=[self.hdep[dc][tb]])
                        step += 1

    def attn(self, i, layer):
        Sx = self.S
        nc = self.nc
        self.rmsnorm(4 + layer)
        Sx.barrier()
        win = self.din["attn_w_in"][i].rearrange("(kc p) f -> p kc f", p=128)
        wout = self.din["attn_w_out"][i]
        off = 0
        og, off = self.view(off, [128, 4, S], BF16)
        ogd = [[Dep() for _ in range(4)] for _ in range(4)]
        qh, off = self.view(off, [128, S], BF16)
        kh, off = self.view(off, [128, S], BF16)
        qd, kd = Dep(), Dep()
        vaug, off = self.view(off, [128, 16, 66], BF16)
        vd = Dep()
        wq, off = self.view(off, [128, 8, 64], BF16)
        wk, off = self.view(off, [128, 8, 64], BF16)
        wv, off = self.view(off, [128, 8, 64], BF16)
        wfl, off = self.view(off, [128, 8, 8], BF16)
        wqd, wkd, wvd, wfd = Dep(), Dep(), Dep(), Dep()
        nl, off = self.view(off, [128, S], F32)
        onesrow, off = self.view(off, [128, S], F32)
        ghi, off = self.view(off, [128, S], BF16)
        glo, off = self.view(off, [128, S], BF16)
        gtok, off = self.view(off, [128, 16], F32)
        nld, ghd, gtd = Dep(), Dep(), Dep()
        pb, pbd = [], []
        for _ in range(2):
            a, off = self.view(off, [128, 512], BF16)
            pb.append(a)
            pbd.append(Dep())
        otsb, off = self.view(off, [128, 512], F32)
        rc, off = self.view(off, [128, 512], F32)
        otd, rcd = Dep(), Dep()
        mtmp, off = self.view(off, [128, 512], F32)
        mfox, off = self.view(off, [128, 4, 512], BF16)
        mswa, off = self.view(off, [128, 5, 512], BF16)
        md = Dep()
        wo, off = self.view(off, [128, 4, 1024], BF16)
        wod = Dep()
        sm, off = self.view(off, [128, 32], F32)
        smd = Dep()
        sel, off = self.view(off, [128, 64], F32)
        hg = sm[:, 0:4]
        esink = sm[:, 4:12]
        negfb = sm[:, 12:20]
        onef = sm[:, 20:21]

        Sx.op("pool", lambda e: e.memset(vaug[:, :, 64:66], 1.0), writes=[vd])
        Sx.op("pool", lambda e: e.memset(onesrow[0:1, :], 1.0), writes=[nld])
        Sx.op("pool", lambda e: e.memset(sm[:, 20:21], 1.0), writes=[smd])
        Sx.op("pool", lambda e: e.memset(sel[:], 0.0), writes=[smd])
        Sx.op("pool", lambda e: e.memset(sel[64:65, :], 1.0), writes=[smd])
        for j in range(4):
            Sx.op("pool", lambda e: e.memset(mtmp[:], 0.0), writes=[md])
            Sx.op("pool", lambda e, j=j: e.affine_select(out=mtmp[:], in_=mtmp[:], pattern=[[1, 512]],
                                                         compare_op=ALU.is_ge, fill=NEG, base=-128 * j,
                                                         channel_multiplier=-1), writes=[md])
            Sx.op("pool", lambda e, j=j: e.tensor_copy(out=mfox[:, j, :], in_=mtmp[:]), writes=[md])
        for jj in range(5):
            j = jj - 1
            Sx.op("pool", lambda e: e.memset(mtmp[:], 0.0), writes=[md])
            Sx.op("pool", lambda e, j=j: e.affine_select(out=mtmp[:], in_=mtmp[:], pattern=[[1, 512]],
                                                         compare_op=ALU.is_ge, fill=NEG, base=-128 * j,
                                                         channel_multiplier=-1), writes=[md])
            Sx.op("pool", lambda e, j=j: e.affine_select(out=mtmp[:], in_=mtmp[:], pattern=[[-1, 512]],
                                                         compare_op=ALU.is_ge, fill=NEG, base=127 + 128 * j,
                                                         channel_multiplier=1), writes=[md])
            Sx.op("pool", lambda e, jj=jj: e.tensor_copy(out=mswa[:, jj, :], in_=mtmp[:]), writes=[md])
        for n, name in enumerate(("swa_q_norm", "swa_k_norm", "fox_q_norm", "fox_k_norm")):
            self.small_dma("asm", sm[0:64, n:n + 1], self.din[name][i].rearrange("(d o) -> d o", o=1), [smd])
        self.small_dma("asm", sm[0:64, 4:12], self.din["swa_sinks"][i:i + 1, :].to_broadcast([64, 8]), [smd])
        self.small_dma("asm", sm[0:1, 12:20], self.din["attn_forget_bias"][i:i + 1, :], [smd])
        Sx.op("act", lambda e: e.activation(out=sm[0:64, 4:12], in_=sm[0:64, 4:12], func=AF.Exp),
              reads=[smd], writes=[smd])
        Sx.op("dve", lambda e: e.tensor_scalar(out=sm[0:1, 12:20], in0=sm[0:1, 12:20], scalar1=-1.0, scalar2=None,
                                               op0=ALU.mult), reads=[smd], writes=[smd])
        Sx.dma("wfl", wfl, win[:, :, 2304:2312], writes=[wfd], qname="pool")

        B_ST, B_OT, B_OP = (0, 1), (2, 3), 7
        RAW, SUM = (4, 6), (5, 7)
        cnt = {"st": 0, "ot": 0, "raw": 0, "sum": 0, "sc": 0}
        sqs_ = [self.sq[0:64, 0, :], self.sq[0:64, 1, :]]
        sqd_ = [Dep(), Dep()]
        rss_ = [self.rstd[0:64, :], self.sg[0][0:64, :]]
        rsd_ = [Dep(), Dep()]

        def raw_bank():
            b_ = RAW[cnt["raw"] % 2]
            cnt["raw"] += 1
            return b_

        def sum_bank():
            b_ = SUM[cnt["sum"] % 2]
            cnt["sum"] += 1
            return b_

        def proj_norm(wst, wdep, gcol, dst, ddep):
            for tb in range(4):
                norm_tb(wst, wdep, gcol, dst, ddep, tb)

        def norm_tb(wst, wdep, gcol, dst, ddep, tb):
            ts = slice(tb * 512, (tb + 1) * 512)
            pj = raw_bank()
            ps = self.bank[pj]
            si = cnt["sc"] % 2
            cnt["sc"] += 1
            sq_, sqdd = sqs_[si], sqd_[si]
            rs_, rsdd = rss_[si], rsd_[si]
            for kc in range(8):
                Sx.op("pe", lambda e, kc=kc: e.matmul(ps[0:64, :], wst[:, kc, :], self.xn[:, kc, ts],
                                                      start=(kc == 0), stop=(kc == 7)),
                      reads=[wdep, self.xdep[tb]], writes=[self.bdep[pj]], signal=(kc == 7))
            Sx.op("act", lambda e: e.activation(out=sq_, in_=ps[0:64, :], func=AF.Square),
                  reads=[self.bdep[pj]], writes=[sqdd])
            sb_ = sum_bank()
            p2 = self.bank[sb_]
            Sx.op("pe", lambda e: e.matmul(p2[0:64, :], self.ones_b[0:64, 0:64], sq_, start=True, stop=True),
                  reads=[sqdd, self.cdep], writes=[self.bdep[sb_]])
            Sx.op("act", lambda e: e.activation(out=rs_, in_=p2[0:64, :], func=AF.Sqrt, bias=EPS, scale=1.0 / 64),
                  reads=[self.bdep[sb_]], writes=[rsdd])
            Sx.op("dve", lambda e: e.reciprocal(out=rs_, in_=rs_), reads=[rsdd], writes=[rsdd])
            Sx.op("dve", lambda e: e.scalar_tensor_tensor(
                out=dst[0:64, ts], in0=ps[0:64, :], scalar=sm[0:64, gcol:gcol + 1], in1=rs_,
                op0=ALU.mult, op1=ALU.mult),
                reads=[self.bdep[pj], rsdd, smd], writes=[ddep])

        def proj_v():
            for t4 in range(4):
                v_t4(t4)

        def v_t4(t4):
            pj = raw_bank()
            ps = self.bank[pj]
            for tl in range(4):
                t = t4 * 4 + tl
                for kc in range(8):
                    Sx.op("pe", lambda e, kc=kc, t=t, tl=tl: e.matmul(
                        ps[:, tl * 64:(tl + 1) * 64], self.xn[:, kc, t * 128:(t + 1) * 128], wv[:, kc, :],
                        start=(kc == 0), stop=(kc == 7)),
                        reads=[wvd, self.xdep[t // 4]], writes=[self.bdep[pj]],
                        signal=(kc == 7 and tl == 3))
            Sx.op("act", lambda e: e.activation(
                out=vaug[:, t4 * 4:(t4 + 1) * 4, 0:64], in_=ps[:, 0:256].rearrange("p (a b) -> p a b", a=4),
                func=AF.Copy), reads=[self.bdep[pj]], writes=[vd])

        def fox_gates(h):
            for tb in range(4):
                ts = slice(tb * 512, (tb + 1) * 512)
                pj = raw_bank()
                ps = self.bank[pj]
                for kc in range(8):
                    Sx.op("pe", lambda e, kc=kc, ts=ts, ps=ps: e.matmul(ps[0:1, :], wfl[:, kc, h:h + 1], self.xn[:, kc, ts],
                                                                         start=(kc == 0), stop=(kc == 7)),
                          reads=[wfd, self.xdep[tb]], writes=[self.bdep[pj]], signal=(kc == 7))
                Sx.op("act", lambda e, ps=ps, ts=ts: e.activation(out=nl[0:1, ts], in_=ps[0:1, :], func=AF.Exp,
                                                                  bias=sm[0:1, 12 + h:13 + h], scale=-1.0),
                      reads=[self.bdep[pj], smd], writes=[nld])
            Sx.op("act", lambda e: e.activation(out=nl[0:1, :], in_=nl[0:1, :], func=AF.Ln, bias=1.0, scale=1.0),
                  reads=[nld], writes=[nld])
            Sx.op("dve", lambda e: e.tensor_tensor_scan(out=nl[0:1, :], data0=onesrow[0:1, :], data1=nl[0:1, :],
                                                        initial=0.0, op0=ALU.mult, op1=ALU.add),
                  reads=[nld], writes=[nld])
            Sx.op("dve", lambda e: e.tensor_scalar(out=ghi[0:1, :], in0=nl[0:1, :], scalar1=-8.0, scalar2=None,
                                                   op0=ALU.mult), reads=[nld], writes=[ghd])
            Sx.op("dve", lambda e: e.scalar_tensor_tensor(out=glo[0:1, :], in0=nl[0:1, :], scalar=-8.0,
                                                          in1=ghi[0:1, :], op0=ALU.mult, op1=ALU.subtract),
                  reads=[nld, ghd], writes=[ghd])
            pj = raw_bank()
            ps = self.bank[pj]
            for t in range(16):
                Sx.op("pe", lambda e, t=t, ps=ps: e.matmul(ps[:, t:t + 1], nl[0:1, t * 128:(t + 1) * 128], sm[0:1, 20:21],
                                                           start=True, stop=True),
                      reads=[nld, smd], writes=[self.bdep[pj]], signal=(t == 15))
            Sx.op("dve", lambda e, ps=ps: e.tensor_copy(out=gtok[:], in_=ps[:, 0:16]),
                  reads=[self.bdep[pj]], writes=[gtd])

        def attention(pairs, fox, hh, grp, h):
            for qb in range(4):
                do_qb(pairs, fox, hh, grp, h, qb)

        def do_qb(pairs, fox, hh, grp, h, qb):
            if True:
                qs = slice(qb * 512, (qb + 1) * 512)
                plist = pairs[qb]
                ob = B_OT[cnt["ot"] % 2]
                cnt["ot"] += 1
                ot = self.bank[ob]
                pend = None

                def emit_pv(p):
                    idx, kt, pi = p
                    Sx.op("pe", lambda e, kt=kt, pi=pi: e.matmul(ot[0:65, :], vaug[:, kt, 0:65], pb[pi][:],
                                                                  start=(idx == 0), stop=(idx == len(plist) - 1)),
                          reads=[vd, pbd[pi]], writes=[self.bdep[ob]], signal=True)

                for idx, (kt, mask) in enumerate(plist):
                    sb_ = B_ST[cnt["st"] % 2]
                    pi = cnt["st"] % 2
                    cnt["st"] += 1
                    st_ = self.bank[sb_]
                    last = "qk"
                    if fox:
                        last = "glo"
                    if mask is not None:
                        last = "mask"
                    Sx.op("pe", lambda e, kt=kt, st_=st_, last=last: e.matmul(
                        st_[:], kh[0:64, kt * 128:(kt + 1) * 128], qh[0:64, qs], start=True, stop=(last == "qk")),
                          reads=[kd, qd], writes=[self.bdep[sb_]], signal=(last == "qk"))
                    if fox:
                        Sx.op("pe", lambda e, st_=st_: e.matmul(st_[:], self.ones_b[0:1, :], ghi[0:1, qs],
                                                                 start=False, stop=False),
                              reads=[ghd, self.cdep], writes=[self.bdep[sb_]], signal=False)
                        Sx.op("pe", lambda e, st_=st_, last=last: e.matmul(st_[:], self.ones_b[0:1, :], glo[0:1, qs],
                                                                            start=False, stop=(last == "glo")),
                              reads=[ghd, self.cdep], writes=[self.bdep[sb_]], signal=(last == "glo"))
                    if mask is not None:
                        Sx.op("pe", lambda e, st_=st_, mask=mask: e.matmul(st_[:], self.ident_b[:], mask,
                                                                            start=False, stop=True),
                              reads=[md, self.cdep], writes=[self.bdep[sb_]], signal=True)
                    if fox:
                        Sx.op("act", lambda e, st_=st_, pi=pi, kt=kt: e.activation(
                            out=pb[pi][:], in_=st_[:], func=AF.Exp, bias=gtok[:, kt:kt + 1], scale=0.125),
                            reads=[self.bdep[sb_], gtd], writes=[pbd[pi]])
                    else:
                        Sx.op("act", lambda e, st_=st_, pi=pi: e.activation(
                            out=pb[pi][:], in_=st_[:], func=AF.Exp, scale=0.125),
                            reads=[self.bdep[sb_]], writes=[pbd[pi]])
                    if pend is not None:
                        emit_pv(pend)
                    pend = (idx, kt, pi)
                emit_pv(pend)
                Sx.op("act", lambda e: e.activation(out=otsb[0:65, :], in_=ot[0:65, :], func=AF.Copy),
                      reads=[self.bdep[ob]], writes=[otd])
                B_SS = sum_bank()
                p2 = self.bank[B_SS]
                Sx.op("pe", lambda e, p2=p2: e.matmul(p2[0:64, :], sel[0:65, 0:64], otsb[0:65, :], start=True, stop=True),
                      reads=[otd, smd], writes=[self.bdep[B_SS]])
                if fox:
                    Sx.op("dve", lambda e, p2=p2: e.reciprocal(out=rc[0:64, :], in_=p2[0:64, :]),
                          reads=[self.bdep[B_SS]], writes=[rcd])
                else:
                    Sx.op("dve", lambda e, p2=p2: e.tensor_scalar(out=rc[0:64, :], in0=p2[0:64, :],
                                                                   scalar1=sm[0:64, 4 + h:5 + h], scalar2=None,
                                                                   op0=ALU.add),
                          reads=[self.bdep[B_SS], smd], writes=[rcd])
                    Sx.op("dve", lambda e: e.reciprocal(out=rc[0:64, :], in_=rc[0:64, :]), reads=[rcd], writes=[rcd])
                Sx.op("dve", lambda e, hh=hh: e.tensor_tensor(out=og[0:64, hh, qs], in0=otsb[0:64, :], in1=rc[0:64, :],
                                                              op=ALU.mult),
                      reads=[otd, rcd], writes=[ogd[hh][qb]])

        fox_pairs, swa_pairs = [], []
        for qb in range(4):
            fp = [(kt, None) for kt in range(4 * qb)] + [(4 * qb + j, mfox[:, j, :]) for j in range(4)]
            fox_pairs.append(fp)
            sp_ = [(4 * qb + j, mswa[:, j + 1, :]) for j in range(-1, 4) if 4 * qb + j >= 0]
            swa_pairs.append(sp_)

        for grp in range(4):
            fox = grp >= 2
            Sx.dma("wo", wo[0:64], wout[grp * 256:(grp + 1) * 256, :].rearrange("(h p) d -> p h d", p=64),
                   writes=[wod], qname="pool")
            for hh in range(4):
                h = (grp % 2) * 4 + hh
                if fox:
                    qc, kc_, vc = 768 + h * 64, 1280 + h * 64, 1792 + h * 64
                else:
                    g = h // 4
                    qc, kc_, vc = h * 64, 512 + g * 64, 640 + g * 64
                new_kv = fox or hh == 0
                Sx.dma("wq", wq, win[:, :, qc:qc + 64], writes=[wqd], qname="pool")
                if new_kv:
                    Sx.dma("wk", wk, win[:, :, kc_:kc_ + 64], writes=[wkd], qname="pool")
                    Sx.dma("wv", wv, win[:, :, vc:vc + 64], writes=[wvd], qname="pool")
                proj_norm(wq, wqd, 2 if fox else 0, qh, qd)
                if new_kv:
                    proj_norm(wk, wkd, 3 if fox else 1, kh, kd)
                    proj_v()
                if fox:
                    fox_gates(h)
                attention(fox_pairs if fox else swa_pairs, fox, hh, grp, h)
            for dc in range(8):
                for tb in range(4):
                    ts = slice(tb * 512, (tb + 1) * 512)
                    ps = self.bank[B_OP]
                    for hh in range(4):
                        Sx.op("pe", lambda e, hh=hh, dc=dc, ts=ts, ps=ps: e.matmul(
                            ps[:], wo[0:64, hh, dc * 128:(dc + 1) * 128], og[0:64, hh, ts],
                            start=(hh == 0), stop=(hh == 3)),
                            reads=[wod, ogd[hh][tb]], writes=[self.bdep[B_OP]], signal=(hh == 3))
                    Sx.op("dve", lambda e, dc=dc, ts=ts, ps=ps: e.tensor_tensor(
                        out=self.hT[:, dc, ts], in0=ps[:], in1=self.hT[:, dc, ts], op=ALU.add),
                        reads=[self.bdep[B_OP], self.hdep[dc][tb]], writes=[self.hdep[dc][tb]])

    def ssm(self, i, layer):
        Sx = self.S
        self.rmsnorm(4 + layer)
        Sx.barrier()
        win = self.din["ssm_w_in"][i].rearrange("(kc p) f -> p kc f", p=128)
        wout = self.din["ssm_w_out"][i]
        V = self.view
        off = 0
        hst, off = V(off, [128, 32, 64], F32)
        hb8, off = V(off, [128, 8, 64], BF16)
        zs8, off = V(off, [128, 8, 512], BF16)
        yg, off = V(off, [128, 8, 512], BF16)
        xs8 = self.sq
        xraw = []
        for _ in range(2):
            a_, off = V(off, [128, 516], F32)
            xraw.append(a_)
        bcraw = xraw[0]
        cacc = self.sg
        tt = []
        for _ in range(2):
            a_, off = V(off, [128, 4, 128], F32)
            tt.append(a_)
        sqs = []
        for _ in range(2):
            a_, off = V(off, [128, 4, 128], BF16)
            sqs.append(a_)
        xdt8, off = V(off, [128, 4, 8, 64], BF16)
        xddt8, off = V(off, [128, 4, 8, 64], BF16)
        Bf, off = V(off, [128, 512], BF16)
        Cf, off = V(off, [128, 512], BF16)
        Btok, off = V(off, [128, 4, 128], BF16)
        CBT, off = V(off, [128, 512], F32)
        dtt, off = V(off, [128, 512], F32)
        ad, off = V(off, [128, 512], F32)
        dtd, off = V(off, [128, 512], F32)
        cdb, off = V(off, [128, 512], F32)
        wx, wz = [], []
        for _ in range(2):
            a_, off = V(off, [128, 8, 64], BF16)
            wx.append(a_)
            a_, off = V(off, [128, 8, 64], BF16)
            wz.append(a_)
        wB, off = V(off, [128, 8, 128], BF16)
        wC, off = V(off, [128, 8, 128], BF16)
        wdt, off = V(off, [128, 8, 32], BF16)
        wo = []
        for _ in range(2):
            a_, off = V(off, [128, 8, 128], BF16)
            wo.append(a_)
        R8, off = V(off, [128, 8, 128], F32)
        acs = R8[:, 0:4, :].rearrange("p a b -> p (a b)")
        A8, off = V(off, [128, 8, 128], F32)
        dec2, dfs2 = [], []
        for _ in range(2):
            a_, off = V(off, [128, 4, 128], F32)
            dec2.append(a_)
            a_, off = V(off, [128, 4, 128], F32)
            dfs2.append(a_)
        hd = {k: [Dep(), Dep()] for k in ("dec", "dfs", "MT", "Cs")}
        MT8, off = V(off, [128, 8, 128], BF16)
        Cs8, off = V(off, [128, 8, 128], BF16)
        trif, off = V(off, [128, 128], F32)
        ntrif, off = V(off, [128, 128], F32)
        onesf, off = V(off, [128, 128], F32)
        sel127, off = V(off, [128, 128], F32)
        negm4, off = V(off, [128, 4, 128], BF16)
        cwx, off = V(off, [128, 32, 4], F32)
        cbx, off = V(off, [128, 32], F32)
        cwbc, off = V(off, [128, 8, 4], F32)
        cbbc, off = V(off, [128, 8], F32)
        dsk, off = V(off, [128, 32], F32)
        ngn, off = V(off, [128, 32], F32)
        ab, off = V(off, [128, 32], F32)
        dtb, off = V(off, [128, 32], F32)
        carx, off = V(off, [128, 32, 4], F32)
        carbc, off = V(off, [128, 8, 4], F32)
        rsb = self.rstd
        self._ssm_off = off
        names = ("bcraw", "xdt", "xddt", "Bf", "Cf", "Btok", "CBT", "lay", "wB", "wC", "wdt", "R", "A", "dec", "dfs",
                 "MT", "Cs", "cst", "par", "carx", "carbc", "rsb", "hb8", "hst")
        d = {k: Dep() for k in names}
        xrd = [Dep(), Dep()]
        cad = [Dep(), Dep()]
        ttd = [Dep(), Dep()]
        sqd = [Dep(), Dep()]
        wxd = [Dep(), Dep()]
        wzd = [Dep(), Dep()]
        wod = [Dep(), Dep()]
        zsd = [Dep() for _ in range(8)]
        xsd = [Dep() for _ in range(8)]
        ygd = [Dep() for _ in range(8)]
        cw_src = self.din["ssm_conv_w"][i]
        cb_src = self.din["ssm_conv_b"][i]
        P = lambda fn, w: Sx.op("pool", fn, writes=w)
        P(lambda e: e.memset(hst[:], 0.0), [d["hst"]])
        P(lambda e: e.memset(carx[:], 0.0), [d["carx"]])
        P(lambda e: e.memset(carbc[:], 0.0), [d["carbc"]])
        P(lambda e: e.memset(onesf[:], 1.0), [d["cst"]])
        P(lambda e: e.memset(trif[:], 1.0), [d["cst"]])
        P(lambda e: e.affine_select(out=trif[:], in_=trif[:], pattern=[[1, 128]], compare_op=ALU.is_ge, fill=0.0,
                                    base=0, channel_multiplier=-1), [d["cst"]])
        P(lambda e: e.memset(ntrif[:], -1.0), [d["cst"]])
        P(lambda e: e.affine_select(out=ntrif[:], in_=ntrif[:], pattern=[[1, 128]], compare_op=ALU.is_ge, fill=0.0,
                                    base=0, channel_multiplier=-1), [d["cst"]])
        P(lambda e: e.memset(sel127[:], 0.0), [d["cst"]])
        P(lambda e: e.affine_select(out=sel127[:], in_=sel127[:], pattern=[[0, 128]], compare_op=ALU.not_equal,
                                    fill=1.0, base=-127, channel_multiplier=1), [d["cst"]])
        Sx.op("dve", lambda e: e.tensor_scalar(out=negm4[:], in0=trif[:].unsqueeze(1).to_broadcast([128, 4, 128]),
                                               scalar1=-NEG, scalar2=NEG, op0=ALU.mult, op1=ALU.add),
              reads=[d["cst"]], writes=[d["cst"]])
        sd = lambda out, in_: self.small_dma("ssmp", out, in_, [d["par"]])
        for k in range(4):
            sd(cwx[0:64, :, k], cw_src[k, 0:2048].rearrange("(h p) -> p h", p=64))
            sd(cwbc[:, :, k], cw_src[k, 2048:3072].rearrange("(j p) -> p j", p=128))
        sd(cbx[0:64], cb_src[0:2048].rearrange("(h p) -> p h", p=64))
        sd(cbbc[:], cb_src[2048:3072].rearrange("(j p) -> p j", p=128))
        sd(dsk[:], self.din["ssm_d_skip"][i:i + 1, :].to_broadcast([128, 32]))
        sd(ngn[0:64], self.din["ssm_norm"][i].rearrange("(h p) -> p h", p=64))
        sd(ab[:], self.din["ssm_a_log"][i:i + 1, :].to_broadcast([128, 32]))
        sd(dtb[:], self.din["ssm_dt_bias"][i:i + 1, :].to_broadcast([128, 32]))
        Sx.op("act", lambda e: e.activation(out=ab[:], in_=ab[:], func=AF.Exp), reads=[d["par"]], writes=[d["par"]])
        Sx.op("dve", lambda e: e.tensor_scalar(out=ab[:], in0=ab[:], scalar1=-1.0, scalar2=None, op0=ALU.mult),
              reads=[d["par"]], writes=[d["par"]])
        Sx.dma("wdt", wdt, win[:, :, 5120:5152], writes=[d["wdt"]], qname="pool")
        B_PJ, B_D, B_Y, B_ST, B_SS, B_TR = (0, 1), (2, 3), (4, 5), 6, 7, 6
        B_MS = 7
        bk, bd = self.bank, self.bdep
        ps = bk[B_MS]
        for C in range(16):
            for kc in range(8):
                Sx.op("pe", lambda e, C=C, kc=kc: e.matmul(ps[:, C * 32:(C + 1) * 32], self.xn[:, kc, C * 128:(C + 1) * 128],
                                                           wdt[:, kc, :], start=(kc == 0), stop=(kc == 7)),
                      reads=[d["wdt"], self.xdep[C // 4]], writes=[bd[B_MS]], signal=(kc == 7 and C == 15))
        for C in range(16):
            Sx.op("dve", lambda e, C=C: e.tensor_tensor(out=dtt[:, C * 32:(C + 1) * 32], in0=ps[:, C * 32:(C + 1) * 32],
                                                        in1=dtb[:], op=ALU.add),
                  reads=[bd[B_MS], d["par"]], writes=[d["lay"]])
        Sx.op("act", lambda e: e.activation(out=dtt[:], in_=dtt[:], func=AF.Exp), reads=[d["lay"]], writes=[d["lay"]])
        Sx.op("act", lambda e: e.activation(out=dtt[:], in_=dtt[:], func=AF.Ln, bias=1.0, scale=1.0),
              reads=[d["lay"]], writes=[d["lay"]])
        for C in range(16):
            Sx.op("dve", lambda e, C=C: e.tensor_tensor(out=ad[:, C * 32:(C + 1) * 32], in0=dtt[:, C * 32:(C + 1) * 32],
                                                        in1=ab[:], op=ALU.mult),
                  reads=[d["lay"], d["par"]], writes=[d["lay"]])
        Sx.op("pe", lambda e: e.matmul(ps[:], trif[:], ad[:], start=True, stop=True),
              reads=[d["lay"], d["cst"]], writes=[bd[B_MS]])
        Sx.op("act", lambda e: e.activation(out=acs[:], in_=ps[:], func=AF.Copy), reads=[bd[B_MS]], writes=[d["lay"]])
        Sx.op("pe", lambda e: e.matmul(ps[:], sel127[:], acs[:], start=True, stop=True),
              reads=[d["lay"], d["cst"]], writes=[bd[B_MS]])
        Sx.op("act", lambda e: e.activation(out=cdb[:], in_=ps[:], func=AF.Exp), reads=[bd[B_MS]], writes=[d["lay"]])
        Sx.op("dve", lambda e: e.tensor_tensor(out=dtd[:], in0=ps[:], in1=acs[:], op=ALU.subtract),
              reads=[bd[B_MS], d["lay"]], writes=[d["lay"]])
        Sx.op("act", lambda e: e.activation(out=dtd[:], in_=dtd[:], func=AF.Exp), reads=[d["lay"]], writes=[d["lay"]])
        Sx.op("dve", lambda e: e.tensor_tensor(out=dtd[:], in0=dtd[:], in1=dtt[:], op=ALU.mult),
              reads=[d["lay"]], writes=[d["lay"]])
        Sx.barrier()
        cnt = {"pj": 0, "x": 0, "ep": 0, "wo": 0}

        def conv_silu(raw, rdep, np_, wsl, bsl, car, cdep, out, odep, pb_):
            ca = cacc[cnt["x"] % 2]
            cd_ = cad[cnt["x"] % 2]
            cnt["x"] += 1
            Sx.op("act", lambda e: e.activation(out=raw[0:np_, 4:516], in_=bk[pb_][0:np_, :], func=AF.Copy),
                  reads=[bd[pb_]], writes=[rdep])
            Sx.op("dve", lambda e: e.tensor_copy(out=raw[0:np_, 1:4], in_=car[:, 0:3]), reads=[cdep], writes=[rdep])
            Sx.op("act", lambda e: e.activation(out=ca[0:np_, :], in_=raw[0:np_, 1:513], func=AF.Identity,
                                                bias=bsl, scale=wsl[:, 0:1]),
                  reads=[rdep, d["par"]], writes=[cd_])
            for k in range(1, 4):
                Sx.op("dve", lambda e, k=k: e.scalar_tensor_tensor(out=ca[0:np_, :], in0=raw[0:np_, 1 + k:513 + k],
                                                                   scalar=wsl[:, k:k + 1], in1=ca[0:np_, :],
                                                                   op0=ALU.mult, op1=ALU.add),
                      reads=[rdep, cd_, d["par"]], writes=[cd_])
            Sx.op("dve", lambda e: e.tensor_copy(out=car[:, 0:3], in_=raw[0:np_, 513:516]), reads=[rdep], writes=[cdep])
            Sx.op("act", lambda e: e.activation(out=out, in_=ca[0:np_, :], func=AF.Silu), reads=[cd_], writes=[odep])

        def proj(wst, wdep, np_, tb):
            ts = slice(tb * 512, (tb + 1) * 512)
            pb_ = B_PJ[cnt["pj"] % 2]
            cnt["pj"] += 1
            for kc in range(8):
                Sx.op("pe", lambda e, kc=kc: e.matmul(bk[pb_][0:np_, :], wst[:, kc, :], self.xn[:, kc, ts],
                                                      start=(kc == 0), stop=(kc == 7)),
                      reads=[wdep, self.xdep[tb]], writes=[bd[pb_]], signal=(kc == 7))
            return pb_

        def head_prep(tb, g, r):
            h = g * 8 + r
            wi = h % 2
            Sx.dma("wz%d" % wi, wz[wi], win[:, :, h * 64:(h + 1) * 64], writes=[wzd[wi]], qname="pool")
            Sx.dma("wx%d" % wi, wx[wi], win[:, :, 2048 + h * 64:2048 + (h + 1) * 64], writes=[wxd[wi]], qname="pool")
            pb_ = proj(wz[wi], wzd[wi], 64, tb)
            Sx.op("act", lambda e, pb_=pb_: e.activation(out=zs8[0:64, r, :], in_=bk[pb_][0:64, :], func=AF.Silu),
                  reads=[bd[pb_]], writes=[zsd[r]])
            pb_ = proj(wx[wi], wxd[wi], 64, tb)
            conv_silu(xraw[wi], xrd[wi], 64, cwx[0:64, h, :], cbx[0:64, h:h + 1], carx[0:64, h, :], d["carx"],
                      xs8[0:64, r, :], xsd[r], pb_)

        DBG = int(os.environ.get('SSM_DBG', '99'))

        def chunk(tb, g, cl):
            C = tb * 4 + cl
            c0 = C * 32 + g * 8
            cs_ = slice(cl * 128, (cl + 1) * 128)
            for r in range(8):
                Sx.op("pe", lambda e, r=r: e.matmul(bk[B_TR][:, r * 64:(r + 1) * 64], xs8[0:64, r, cs_],
                                                    self.ident_b[0:64, 0:64], start=True, stop=True),
                      reads=[xsd[r], self.cdep], writes=[bd[B_TR]], signal=(r == 7))
            trv = bk[B_TR][:].rearrange("p (a b) -> p a b", a=8)
            Sx.op("dve", lambda e: e.tensor_tensor(out=xdt8[:, cl, :, :], in0=trv,
                                                   in1=dtt[:, c0:c0 + 8].unsqueeze(2).to_broadcast([128, 8, 64]),
                                                   op=ALU.mult), reads=[bd[B_TR], d["lay"]], writes=[d["xdt"]])
            Sx.op("dve", lambda e: e.tensor_tensor(out=xddt8[:, cl, :, :], in0=trv,
                                                   in1=dtd[:, c0:c0 + 8].unsqueeze(2).to_broadcast([128, 8, 64]),
                                                   op=ALU.mult), reads=[bd[B_TR], d["lay"]], writes=[d["xddt"]])
            adb = ad[:, c0:c0 + 8].unsqueeze(2).to_broadcast([128, 8, 128])
            Sx.op("dve", lambda e: e.tensor_tensor(out=R8[:], in0=trif[:].unsqueeze(1).to_broadcast([128, 8, 128]),
                                                   in1=adb, op=ALU.mult),
                  reads=[d["cst"], d["lay"]], writes=[d["R"]])
            Sx.op("act", lambda e: e.activation(out=A8[:], in_=adb, func=AF.Copy), reads=[d["lay"]], writes=[d["A"]])
            def decay_half(hf):
                db = B_D[hf]
                hs = slice(hf * 4, hf * 4 + 4)
                dec, dfs = dec2[hf], dfs2[hf]
                dv = bk[db][:].rearrange("p (a b) -> p a b", a=4)
                Sx.op("pe", lambda e: e.matmul(dv, onesf[:], R8[:, hs, :], start=True, stop=True),
                      reads=[d["R"], d["cst"]], writes=[bd[db]])
                Sx.op("act", lambda e: e.activation(out=dfs[:], in_=dv, func=AF.Exp), reads=[bd[db]], writes=[hd["dfs"][hf]])
                Sx.op("pe", lambda e: e.matmul(dv, ntrif[:], A8[:, hs, :], start=False, stop=False,
                                               skip_group_check=True),
                      reads=[d["A"], d["cst"]], writes=[bd[db]], signal=False)
                Sx.op("pe", lambda e: e.matmul(dv, self.ident_b[:], negm4[:], start=False, stop=True,
                                               skip_group_check=True),
                      reads=[d["cst"], self.cdep], writes=[bd[db]])
                Sx.op("act", lambda e: e.activation(out=dec[:], in_=dv, func=AF.Exp), reads=[bd[db]], writes=[hd["dec"][hf]])
                Sx.op("dve", lambda e: e.tensor_tensor(
                    out=MT8[:, hs, :], in0=dec[:], in1=CBT[:, cs_].unsqueeze(1).to_broadcast([128, 4, 128]),
                    op=ALU.mult), reads=[hd["dec"][hf], d["CBT"]], writes=[hd["MT"][hf]])
                Sx.op("dve", lambda e: e.tensor_tensor(out=Cs8[:, hs, :], in0=dfs[:],
                                                       in1=Cf[:, cs_].unsqueeze(1).to_broadcast([128, 4, 128]),
                                                       op=ALU.mult),
                      reads=[hd["dfs"][hf], d["Cf"]], writes=[hd["Cs"][hf]])
            for hf in range(2):
                decay_half(hf)

            def y_half(hf):
                yb = B_Y[hf]
                for rr in range(4):
                    r = hf * 4 + rr
                    Sx.op("pe", lambda e, r=r, rr=rr: e.matmul(bk[yb][0:64, rr * 128:(rr + 1) * 128], xdt8[:, cl, r, :],
                                                               MT8[:, r, :], start=True, stop=False),
                          reads=[d["xdt"], hd["MT"][hf]], writes=[bd[yb]], signal=False)
                    Sx.op("pe", lambda e, r=r, rr=rr: e.matmul(bk[yb][0:64, rr * 128:(rr + 1) * 128], hb8[:, r, :],
                                                               Cs8[:, r, :], start=False, stop=True),
                          reads=[d["hb8"], hd["Cs"][hf]], writes=[bd[yb]], signal=(rr == 3))
            for hf in range(2):
                y_half(hf)
            Sx.op("pe", lambda e: e.matmul(bk[B_ST][:], Btok[:, cl, :], xddt8[:, cl, :, :].rearrange("p a b -> p (a b)"),
                                           start=True, stop=True),
                  reads=[d["Btok"], d["xddt"]], writes=[bd[B_ST]])
            hv = hst[:, g * 8:(g + 1) * 8, :]
            Sx.op("dve", lambda e: e.tensor_tensor(out=hv, in0=hv,
                                                   in1=cdb[:, c0:c0 + 8].unsqueeze(2).to_broadcast([128, 8, 64]),
                                                   op=ALU.mult), reads=[d["lay"], d["hst"]], writes=[d["hst"]])
            Sx.op("dve", lambda e: e.tensor_tensor(out=hv, in0=bk[B_ST][:].rearrange("p (a b) -> p a b", a=8), in1=hv,
                                                   op=ALU.add), reads=[bd[B_ST], d["hst"]], writes=[d["hst"]])
            Sx.op("act", lambda e: e.activation(out=hb8[:], in_=hv, func=AF.Copy), reads=[d["hst"]], writes=[d["hb8"]])
            def epi_half(hf):
                yb = B_Y[hf]
                hs = slice(hf * 4, hf * 4 + 4)
                ei = cnt["ep"] % 2
                cnt["ep"] += 1
                t_, td_ = tt[ei], ttd[ei]
                q_, qd_ = sqs[ei], sqd[ei]
                h0 = g * 8 + hf * 4
                yv = bk[yb][0:64, :].rearrange("p (a b) -> p a b", a=4)
                Sx.op("dve", lambda e: e.tensor_tensor(out=t_[0:64], in0=xs8[0:64, hs, cs_],
                                                       in1=dsk[0:64, h0:h0 + 4].unsqueeze(2).to_broadcast([64, 4, 128]),
                                                       op=ALU.mult),
                      reads=[xsd[hf * 4 + k] for k in range(4)] + [d["par"]], writes=[td_])
                Sx.op("dve", lambda e: e.tensor_tensor(out=t_[0:64], in0=yv, in1=t_[0:64], op=ALU.add),
                      reads=[bd[yb], td_], writes=[td_])
                Sx.op("dve", lambda e: e.tensor_tensor(out=t_[0:64], in0=t_[0:64], in1=zs8[0:64, hs, cs_], op=ALU.mult),
                      reads=[td_] + [zsd[hf * 4 + k] for k in range(4)], writes=[td_])
                Sx.op("act", lambda e: e.activation(out=q_[0:64], in_=t_[0:64], func=AF.Square), reads=[td_], writes=[qd_])
                for rr in range(4):
                    first = (hf == 0 and rr == 0)
                    last = (hf == 1 and rr == 3)
                    Sx.op("pe", lambda e, rr=rr, first=first, last=last: e.matmul(
                        bk[B_SS][:, cs_], self.ones_b[0:64, :], q_[0:64, rr, :], start=first, stop=last,
                        skip_group_check=True),
                          reads=[qd_, self.cdep], writes=[bd[B_SS]], signal=(rr == 3))
                Sx.op("dve", lambda e: e.tensor_tensor(out=yg[0:64, hs, cs_], in0=t_[0:64],
                                                       in1=ngn[0:64, h0:h0 + 4].unsqueeze(2).to_broadcast([64, 4, 128]),
                                                       op=ALU.mult),
                      reads=[td_, d["par"]], writes=[ygd[hf * 4 + k] for k in range(4)])

            for hf in range(2):
                epi_half(hf)

        def group(tb, g):
            ts = slice(tb * 512, (tb + 1) * 512)
            Sx.dma("wB", wB, win[:, :, 4096 + g * 128:4096 + (g + 1) * 128], writes=[d["wB"]], qname="pool")
            Sx.dma("wC", wC, win[:, :, 4608 + g * 128:4608 + (g + 1) * 128], writes=[d["wC"]], qname="pool")
            pb_ = proj(wB, d["wB"], 128, tb)
            conv_silu(bcraw, xrd[0], 128, cwbc[:, g, :], cbbc[:, g:g + 1], carbc[:, g, :], d["carbc"], Bf[:], d["Bf"], pb_)
            pb_ = proj(wC, d["wC"], 128, tb)
            conv_silu(bcraw, xrd[0], 128, cwbc[:, 4 + g, :], cbbc[:, 4 + g:5 + g], carbc[:, 4 + g, :], d["carbc"],
                      Cf[:], d["Cf"], pb_)
            for cl in range(4):
                cs_ = slice(cl * 128, (cl + 1) * 128)
                Sx.op("pe", lambda e, cs_=cs_: e.matmul(bk[B_TR][:, cs_], Bf[:, cs_], self.ident_b[:], start=True, stop=True),
                      reads=[d["Bf"], self.cdep], writes=[bd[B_TR]], signal=(cl == 3))
            Sx.op("act", lambda e: e.activation(out=Btok[:], in_=bk[B_TR][:].rearrange("p (a b) -> p a b", a=4),
                                                func=AF.Copy), reads=[bd[B_TR]], writes=[d["Btok"]])
            for cl in range(4):
                cs_ = slice(cl * 128, (cl + 1) * 128)
                Sx.op("pe", lambda e, cs_=cs_: e.matmul(bk[B_TR][:, cs_], Bf[:, cs_], Cf[:, cs_], start=True, stop=True),
                      reads=[d["Bf"], d["Cf"]], writes=[bd[B_TR]], signal=(cl == 3))
            Sx.op("dve", lambda e: e.tensor_tensor(out=CBT[:].rearrange("p (a b) -> p a b", a=4),
                                                   in0=bk[B_TR][:].rearrange("p (a b) -> p a b", a=4),
                                                   in1=trif[:].unsqueeze(1).to_broadcast([128, 4, 128]), op=ALU.mult),
                  reads=[bd[B_TR], d["cst"]], writes=[d["CBT"]])
            Sx.op("act", lambda e: e.activation(out=hb8[:], in_=hst[:, g * 8:(g + 1) * 8, :], func=AF.Copy),
                  reads=[d["hst"]], writes=[d["hb8"]])
            for r in range(8):
                head_prep(tb, g, r)
            for cl in range(4):
                chunk(tb, g, cl)
            Sx.op("act", lambda e: e.activation(out=rsb[:], in_=bk[B_SS][:], func=AF.Sqrt, bias=EPS, scale=1.0 / 512),
                  reads=[bd[B_SS]], writes=[d["rsb"]])
            Sx.op("dve", lambda e: e.reciprocal(out=rsb[:], in_=rsb[:]), reads=[d["rsb"]], writes=[d["rsb"]])
            for dc in range(8):
                wi = cnt["wo"] % 2
                cnt["wo"] += 1
                Sx.dma("wo%d" % wi, wo[wi][0:64],
                       wout[g * 512:(g + 1) * 512, dc * 128:(dc + 1) * 128].rearrange("(h p) d -> p h d", p=64),
                       writes=[wod[wi]], qname="pool")
                ob = B_PJ[cnt["pj"] % 2]
                cnt["pj"] += 1
                for r in range(8):
                    Sx.op("pe", lambda e, r=r, wi=wi, ob=ob: e.matmul(bk[ob][:], wo[wi][0:64, r, :], yg[0:64, r, :],
                                                                      start=(r == 0), stop=(r == 7)),
                          reads=[wod[wi], ygd[r]], writes=[bd[ob]], signal=(r == 7))
                ei = cnt["ep"] % 2
                cnt["ep"] += 1
                t_ = tt[ei][:].rearrange("p a b -> p (a b)")
                td_ = ttd[ei]
                Sx.op("dve", lambda e, ob=ob, t_=t_: e.tensor_tensor(out=t_, in0=bk[ob][:], in1=rsb[:], op=ALU.mult),
                      reads=[bd[ob], d["rsb"]], writes=[td_])
                Sx.op("dve", lambda e, dc=dc, t_=t_: e.tensor_tensor(out=self.hT[:, dc, ts], in0=t_, in1=self.hT[:, dc, ts],
                                                                      op=ALU.add),
                      reads=[td_, self.hdep[dc][tb]], writes=[self.hdep[dc][tb]])

        for tb in range(4 if DBG >= 99 else 1):
            for g in range(4 if DBG >= 99 else 1):
                group(tb, g)


ALL_PHASES = []
for _l in range(4):
    ALL_PHASES.append(("ffn1", _l))
    ALL_PHASES.append(("attn" if _l % 2 == 0 else "ssm", _l))
    ALL_PHASES.append(("ffn2", _l))


def run(inputs, phases, n_cores=8, trace=False):
    nc = Builder(phases).build()
    x = np.ascontiguousarray(inputs["x"], dtype=np.float32)
    in_maps = []
    for c in range(n_cores):
        m = {"x": x[c]}
        for k in INPUT_SHAPES:
            m[k] = np.ascontiguousarray(inputs[k], dtype=np.float32)
        in_maps.append(m)
    res = run_bass_kernel_spmd(nc, in_maps, core_ids=list(range(n_cores)), trace=trace)
    out = np.stack([r["y"] for r in res.results], axis=0)
    return out, res


def kernel(**inputs):
    out, _ = run(inputs, ALL_PHASES, 8)
    return out.astype(np.float32)
```

```python
from contextlib import ExitStack
import os
import numpy as np
import concourse.bass as bass
import concourse.mybir as mybir
from concourse.bass_utils import run_bass_kernel_spmd

F32 = mybir.dt.float32
BF16 = mybir.dt.bfloat16
AF = mybir.ActivationFunctionType
ALU = mybir.AluOpType
AX = mybir.AxisListType

D = 1024
S = 2048
DFF = 2816
NF = DFF // 128
EPS = 1e-6
ATT_W = 2312
SSM_W = 5152
NEG = -30000.0

INPUT_SHAPES = {
    "ffn1_norm": [4, 1024], "ffn1_w_gate": [4, 1024, 2816], "ffn1_w_up": [4, 1024, 2816],
    "ffn1_w_down": [4, 2816, 1024], "mix_norm": [4, 1024], "ffn2_norm": [4, 1024],
    "ffn2_w_gate": [4, 1024, 2816], "ffn2_w_up": [4, 1024, 2816], "ffn2_w_down": [4, 2816, 1024],
    "attn_w_in": [2, 1024, 2312], "attn_forget_bias": [2, 8], "swa_q_norm": [2, 64],
    "swa_k_norm": [2, 64], "swa_sinks": [2, 8], "fox_q_norm": [2, 64], "fox_k_norm": [2, 64],
    "attn_w_out": [2, 1024, 1024], "ssm_w_in": [2, 1024, 5152], "ssm_conv_w": [2, 4, 3072],
    "ssm_conv_b": [2, 3072], "ssm_dt_bias": [2, 32], "ssm_a_log": [2, 32], "ssm_d_skip": [2, 32],
    "ssm_norm": [2, 2048], "ssm_w_out": [2, 2048, 1024],
}


class Dep:
    __slots__ = ("w", "r")

    def __init__(self):
        self.w = None
        self.r = []


class Queue:
    def __init__(self, name, sem, is_pe=False):
        self.name = name
        self.sem = sem
        self.count = 0
        self.ops = []
        self.waited = {}
        self.is_pe = is_pe


class Sched:
    def __init__(self, nc, stack):
        self.nc = nc
        self.stack = stack
        self.q = {}
        for name in ("pe", "act", "dve", "pool", "sp"):
            sem = stack.enter_context(nc.semaphore("s_" + name))
            self.q[name] = Queue(name, sem, name == "pe")
        self.dma_sems = {}

    def dma_sem(self, name):
        if name not in self.dma_sems:
            sem = self.stack.enter_context(self.nc.semaphore("d_" + name))
            self.dma_sems[name] = [sem, 0]
        return self.dma_sems[name]

    def _collect(self, q, reads, writes, dma_group=None):
        need = {}

        def add(tok):
            if tok is None:
                return
            sem, val, owner, grp = tok
            if owner == q.name and q.is_pe:
                return
            if dma_group is not None and grp is dma_group:
                return
            k = id(sem)
            if q.waited.get(k, 0) >= val:
                return
            if k not in need or need[k][1] < val:
                need[k] = (sem, val)

        for d in reads:
            add(d.w)
        for d in writes:
            add(d.w)
            for t in d.r:
                add(t)
        waits = list(need.values())
        for sem, val in waits:
            q.waited[id(sem)] = val
        return waits

    def op(self, qname, fn, reads=(), writes=(), signal=True):
        q = self.q[qname]
        waits = self._collect(q, reads, writes)
        tok = (q.sem, q.count + 1, q.name, None)
        if signal:
            q.count += 1
        q.ops.append((waits, fn, (q.sem, 1) if signal else None))
        for d in reads:
            d.r.append(tok)
        for d in writes:
            d.w = tok
            d.r = []
        return tok

    def dma(self, semname, out, in_, reads=(), writes=(), qname="sp"):
        q = self.q[qname]
        ent = self.dma_sem(semname)
        waits = self._collect(q, reads, writes, dma_group=ent)
        ent[1] += 16
        tok = (ent[0], ent[1], None, ent)
        q.ops.append((waits, lambda e: e.dma_start(out=out, in_=in_), (ent[0], 16)))
        for d in reads:
            d.r.append(tok)
        for d in writes:
            d.w = tok
            d.r = []
        return tok

    def barrier(self):
        toks = [(q.sem, q.count) for q in self.q.values() if q.count > 0]
        toks += [(ent[0], ent[1]) for ent in self.dma_sems.values() if ent[1] > 0]
        for q in self.q.values():
            waits = []
            for sem, val in toks:
                if sem is q.sem and q.is_pe:
                    continue
                if q.waited.get(id(sem), 0) >= val:
                    continue
                q.waited[id(sem)] = val
                waits.append((sem, val))
            q.ops.append((waits, None, None))

    def finish(self, final_tokens):
        nc = self.nc
        fin = {}
        for sem, val, owner, grp in final_tokens:
            k = id(sem)
            if k not in fin or fin[k][1] < val:
                fin[k] = (sem, val)
        engs = {"pe": "tensor", "act": "scalar", "dve": "vector", "pool": "gpsimd", "sp": "sync"}
        with nc.Block() as block:
            for name, attr in engs.items():
                q = self.q[name]

                def body(eng, q=q, name=name):
                    for waits, fn, inc in q.ops:
                        for sem, val in waits:
                            eng.wait_ge(sem, val)
                        if fn is None:
                            continue
                        ins = fn(eng)
                        if inc is not None:
                            ins.then_inc(inc[0], inc[1])
                    if name == "sp":
                        for sem, val in fin.values():
                            eng.wait_ge(sem, val)

                getattr(block, attr)(body)


class Builder:
    def __init__(self, phases):
        self.phases = phases
        self.nc = bass.Bass("TRN2", target_bir_lowering=False)

    def build(self):
        nc = self.nc
        self.din = {}
        self.x = nc.dram_tensor("x", [S, D], F32, kind="ExternalInput").ap()
        for k, shp in INPUT_SHAPES.items():
            self.din[k] = nc.dram_tensor(k, shp, F32, kind="ExternalInput").ap()
        self.y = nc.dram_tensor("y", [S, D], F32, kind="ExternalOutput").ap()
        with ExitStack() as st:
            self.st = st
            self.S = Sched(nc, st)
            self.alloc()
            self.consts()
            self.load_x()
            for ph in self.phases:
                kind, layer = ph
                self.S.barrier()
                if kind == "ffn1":
                    self.ffn(layer, "ffn1")
                elif kind == "ffn2":
                    self.ffn(layer, "ffn2")
                elif kind == "attn":
                    self.attn(layer // 2, layer)
                elif kind == "ssm":
                    self.ssm(layer // 2, layer)
            self.S.barrier()
            toks = self.store_y()
            self.S.finish(toks)
        return nc

    def sb(self, name, shape, dtype):
        return self.st.enter_context(self.nc.sbuf_tensor(name, shape, dtype))

    def alloc(self):
        nc = self.nc
        self.hT = self.sb("hT", [128, 8, S], F32)
        self.hdep = [[Dep() for _ in range(4)] for _ in range(8)]
        self.xn = self.sb("xn", [128, 8, S], BF16)
        self.xdep = [Dep() for _ in range(4)]
        self.bank = [self.st.enter_context(nc.psum_tensor("pb%d" % i, [128, 512], F32)) for i in range(8)]
        self.bdep = [Dep() for _ in range(8)]
        self.SCR = 23 * 1024 + 512
        self.scr = self.sb("scr", [128, self.SCR], F32)
        self.ident_f = self.sb("ident_f", [128, 128], F32)
        self.ident_b = self.sb("ident_b", [128, 128], BF16)
        self.ones_b = self.sb("ones_b", [128, 128], BF16)
        self.cdep = Dep()
        self.sq = self.sb("sq", [128, 8, 512], BF16)
        self.sqdep = Dep()
        self.rstd = self.sb("rstd", [128, 512], F32)
        self.rstdep = Dep()
        self.gains = self.sb("gains", [128, 12, 8], F32)
        self.gdep = Dep()
        self.sg = [self.sb("sg%d" % i, [128, 512], F32) for i in range(2)]
        self.sgdep = [Dep(), Dep()]
        self.io = [self.scr[:, i * 1024:(i + 1) * 1024] for i in range(2)]
        self.iodep = [Dep(), Dep()]

    def view(self, off_words, shape, dtype):
        n = 1
        for s in shape[1:]:
            n *= s
        if dtype == BF16:
            words = (n + 1) // 2
            ap = self.scr[:, off_words:off_words + words].bitcast(BF16)
        else:
            words = n
            ap = self.scr[:, off_words:off_words + words]
        assert off_words + words <= self.SCR, (off_words, words)
        if len(shape) == 3:
            ap = ap.rearrange("p (a b) -> p a b", a=shape[1])
        elif len(shape) == 4:
            ap = ap.rearrange("p (a b c) -> p a b c", a=shape[1], b=shape[2])
        if shape[0] != 128:
            ap = ap[0:shape[0]]
        return ap, off_words + words

    def consts(self):
        Sx = self.S
        nc = self.nc
        idf, idb, ones = self.ident_f, self.ident_b, self.ones_b
        Sx.op("pool", lambda e: e.memset(idf[:], 0.0), writes=[self.cdep])
        Sx.op("pool", lambda e: e.affine_select(out=idf[:], in_=idf[:], pattern=[[-1, 128]],
                                                compare_op=ALU.not_equal, fill=1.0, base=0,
                                                channel_multiplier=1), writes=[self.cdep])
        Sx.op("dve", lambda e: e.tensor_copy(out=idb[:], in_=idf[:]), reads=[self.cdep], writes=[self.cdep])
        Sx.op("dve", lambda e: e.memset(ones[:], 1.0), writes=[self.cdep])

    def _small_dma(self, out, in_):
        nc = self.nc

        def fn(e):
            with nc.allow_non_contiguous_dma(reason="tiny vectors"):
                return e.dma_start(out=out, in_=in_)
        return fn

    def small_dma(self, semname, out, in_, writes):
        Sx = self.S
        q = Sx.q["sp"]
        ent = Sx.dma_sem(semname)
        waits = Sx._collect(q, (), writes, dma_group=ent)
        ent[1] += 16
        tok = (ent[0], ent[1], None, ent)
        q.ops.append((waits, self._small_dma(out, in_), (ent[0], 16)))
        for d in writes:
            d.w = tok
            d.r = []
        return tok

    def load_x(self):
        Sx = self.S
        for n, name in enumerate(("ffn1_norm", "mix_norm", "ffn2_norm")):
            src = self.din[name].rearrange("l (c p) -> p l c", p=128)
            self.small_dma("gains", self.gains[:, n * 4:(n + 1) * 4, :], src, [self.gdep])
        for tt in range(16):
            io, iod = self.io[tt % 2], self.iodep[tt % 2]
            Sx.dma("xin%d" % (tt % 2), io[:], self.x[tt * 128:(tt + 1) * 128, :], writes=[iod])
            for half in range(2):
                b = 6 + half
                ps, pd = self.bank[b], self.bdep[b]
                for c4 in range(4):
                    c = half * 4 + c4
                    Sx.op("pe", lambda e, ps=ps, io=io, c=c, c4=c4: e.transpose(
                        out=ps[:, c4 * 128:(c4 + 1) * 128], in_=io[:, c * 128:(c + 1) * 128],
                        identity=self.ident_f[:]), reads=[iod, self.cdep], writes=[pd], signal=(c4 == 3))
                tb = tt // 4
                dst = self.hT[:, half * 4:half * 4 + 4, tt * 128:(tt + 1) * 128]
                src = ps[:].rearrange("p (a b) -> p a b", a=4)
                eng = "act" if half == 0 else "dve"
                if eng == "act":
                    fn = lambda e, dst=dst, src=src: e.activation(out=dst, in_=src, func=AF.Copy)
                else:
                    fn = lambda e, dst=dst, src=src: e.tensor_copy(out=dst, in_=src)
                Sx.op(eng, fn, reads=[pd], writes=[self.hdep[half * 4 + c4][tb] for c4 in range(4)])

    def store_y(self):
        Sx = self.S
        toks = []
        for tt in range(16):
            io, iod = self.io[tt % 2], self.iodep[tt % 2]
            tb = tt // 4
            for half in range(2):
                b = 6 + half
                ps, pd = self.bank[b], self.bdep[b]
                for c4 in range(4):
                    c = half * 4 + c4
                    Sx.op("pe", lambda e, ps=ps, c=c, c4=c4, tt=tt: e.transpose(
                        out=ps[:, c4 * 128:(c4 + 1) * 128], in_=self.hT[:, c, tt * 128:(tt + 1) * 128],
                        identity=self.ident_f[:]), reads=[self.hdep[c][tb], self.cdep], writes=[pd],
                        signal=(c4 == 3))
                dst = io[:, half * 512:(half + 1) * 512]
                if half == 0:
                    fn = lambda e, dst=dst, ps=ps: e.activation(out=dst, in_=ps[:], func=AF.Copy)
                    Sx.op("act", fn, reads=[pd], writes=[iod])
                else:
                    fn = lambda e, dst=dst, ps=ps: e.tensor_copy(out=dst, in_=ps[:])
                    Sx.op("dve", fn, reads=[pd], writes=[iod])
            toks.append(Sx.dma("yout%d" % (tt % 2), self.y[tt * 128:(tt + 1) * 128, :], io[:], reads=[iod]))
        return toks

    def rmsnorm(self, gidx):
        Sx = self.S
        nb = 5
        for tb in range(4):
            ts = slice(tb * 512, (tb + 1) * 512)
            for c in range(8):
                Sx.op("act", lambda e, c=c, ts=ts: e.activation(out=self.sq[:, c, :], in_=self.hT[:, c, ts],
                                                                func=AF.Square),
                      reads=[self.hdep[c][tb]], writes=[self.sqdep])
            ps, pd = self.bank[nb], self.bdep[nb]
            for c in range(8):
                Sx.op("pe", lambda e, c=c, ps=ps: e.matmul(ps[:], self.ones_b[:], self.sq[:, c, :],
                                                            start=(c == 0), stop=(c == 7)),
                      reads=[self.sqdep, self.cdep], writes=[pd], signal=(c == 7))
            Sx.op("act", lambda e, ps=ps: e.activation(out=self.rstd[:], in_=ps[:], func=AF.Sqrt,
                                                        bias=EPS, scale=1.0 / D),
                  reads=[pd], writes=[self.rstdep])
            Sx.op("dve", lambda e: e.reciprocal(out=self.rstd[:], in_=self.rstd[:]),
                  reads=[self.rstdep], writes=[self.rstdep])
            for c in range(8):
                Sx.op("dve", lambda e, c=c, ts=ts: e.scalar_tensor_tensor(
                    out=self.xn[:, c, ts], in0=self.hT[:, c, ts], scalar=self.gains[:, gidx, c:c + 1],
                    in1=self.rstd[:], op0=ALU.mult, op1=ALU.mult),
                    reads=[self.hdep[c][tb], self.rstdep, self.gdep], writes=[self.xdep[tb]])

    def ffn(self, layer, which):
        Sx = self.S
        gidx = (0 if which == "ffn1" else 2) * 4 + layer
        self.rmsnorm(gidx)
        wg = self.din[which + "_w_gate"][layer].rearrange("(kc p) f -> p kc f", p=128)
        wu = self.din[which + "_w_up"][layer].rearrange("(kc p) f -> p kc f", p=128)
        wd = self.din[which + "_w_down"][layer].rearrange("(fc p) d -> p fc d", p=128)
        off = 0
        hff, off = self.view(off, [128, NF, 1024], BF16)
        hfdep = [[Dep() for _ in range(2)] for _ in range(NF)]
        gst, ust, gud = [], [], []
        for i in range(3):
            a, off = self.view(off, [128, 8, 256], BF16)
            b, off = self.view(off, [128, 8, 256], BF16)
            gst.append(a)
            ust.append(b)
            gud.append(Dep())
        dst, ddd = [], []
        for i in range(2):
            a, off = self.view(off, [128, NF, 256], BF16)
            dst.append(a)
            ddd.append(Dep())
        step = 0
        for blk in range(2):
            for grp in range(NF // 2):
                si = grp % 3
                cs = slice(grp * 256, (grp + 1) * 256)
                Sx.dma("wgu%d" % si, gst[si], wg[:, :, cs], writes=[gud[si]], qname="pool")
                Sx.dma("wgu%d" % si, ust[si], wu[:, :, cs], writes=[gud[si]], qname="pool")
                for cc in range(2):
                    j = grp * 2 + cc
                    for half in range(2):
                        tb = blk * 2 + half
                        ts = slice(tb * 512, (tb + 1) * 512)
                        gb, ub = step % 2, 2 + step % 2
                        gps, ups = self.bank[gb], self.bank[ub]
                        for kc in range(8):
                            Sx.op("pe", lambda e, gps=gps, si=si, kc=kc, cc=cc, ts=ts: e.matmul(
                                gps[:], gst[si][:, kc, cc * 128:(cc + 1) * 128], self.xn[:, kc, ts],
                                start=(kc == 0), stop=(kc == 7)),
                                reads=[gud[si], self.xdep[tb]], writes=[self.bdep[gb]], signal=(kc == 7))
                        for kc in range(8):
                            Sx.op("pe", lambda e, ups=ups, si=si, kc=kc, cc=cc, ts=ts: e.matmul(
                                ups[:], ust[si][:, kc, cc * 128:(cc + 1) * 128], self.xn[:, kc, ts],
                                start=(kc == 0), stop=(kc == 7)),
                                reads=[gud[si], self.xdep[tb]], writes=[self.bdep[ub]], signal=(kc == 7))
                        sg, sgd = self.sg[step % 2], self.sgdep[step % 2]
                        Sx.op("act", lambda e, sg=sg, gps=gps: e.activation(out=sg[:], in_=gps[:], func=AF.Silu),
                              reads=[self.bdep[gb]], writes=[sgd])
                        Sx.op("dve", lambda e, sg=sg, ups=ups, j=j, half=half: e.tensor_tensor(
                            out=hff[:, j, half * 512:(half + 1) * 512], in0=ups[:], in1=sg[:], op=ALU.mult),
                            reads=[self.bdep[ub], sgd], writes=[hfdep[j][half]])
                        step += 1
            for dg in range(4):
                si = dg % 2
                Sx.dma("wd%d" % si, dst[si], wd[:, :, dg * 256:(dg + 1) * 256], writes=[ddd[si]], qname="pool")
                for cc in range(2):
                    dc = dg * 2 + cc
                    for half in range(2):
                        tb = blk * 2 + half
                        ts = slice(tb * 512, (tb + 1) * 512)
                        b = 4 + step % 2
                        ps = self.bank[b]
                        for f in range(NF):
                            Sx.op("pe", lambda e, ps=ps, si=si, f=f, cc=cc, half=half: e.matmul(
                                ps[:], dst[si][:, f, cc * 128:(cc + 1) * 128],
                                hff[:, f, half * 512:(half + 1) * 512],
                                start=(f == 0), stop=(f == NF - 1)),
                                reads=[ddd[si], hfdep[f][half]], writes=[self.bdep[b]], signal=(f == NF - 1))
                        Sx.op("dve", lambda e, ps=ps, dc=dc, ts=ts: e.scalar_tensor_tensor(
                            out=self.hT[:, dc, ts], in0=ps[:], scalar=0.5, in1=self.hT[:, dc, ts],
                            op0=ALU.mult, op1=ALU.add),
                            reads=[self.bdep[b], self.hdep[dc][tb]], writes=[self.hdep[dc][tb]])
                        step += 1

    def attn(self, i, layer):
        Sx = self.S
        nc = self.nc
        self.rmsnorm(4 + layer)
        Sx.barrier()
        win = self.din["attn_w_in"][i].rearrange("(kc p) f -> p kc f", p=128)
        wout = self.din["attn_w_out"][i]
        off = 0
        og, off = self.view(off, [128, 4, S], BF16)
        ogd = [[Dep() for _ in range(4)] for _ in range(4)]
        qh, off = self.view(off, [128, S], BF16)
        kh, off = self.view(off, [128, S], BF16)
        qd, kd = Dep(), Dep()
        vaug, off = self.view(off, [128, 16, 66], BF16)
        vd = Dep()
        wq, off = self.view(off, [128, 8, 64], BF16)
        wk, off = self.view(off, [128, 8, 64], BF16)
        wv, off = self.view(off, [128, 8, 64], BF16)
        wfl, off = self.view(off, [128, 8, 8], BF16)
        wqd, wkd, wvd, wfd = Dep(), Dep(), Dep(), Dep()
        nl, off = self.view(off, [128, S], F32)
        onesrow, off = self.view(off, [128, S], F32)
        wfl128, off = self.view(off, [128, 8, 128], BF16)
        gtok, off = self.view(off, [128, 16], F32)
        nld, gtd = Dep(), Dep()
        pb, pbd = [], []
        for _ in range(4):
            a, off = self.view(off, [128, 512], BF16)
            pb.append(a)
            pbd.append(Dep())
        otsb, off = self.view(off, [128, 512], F32)
        rc, off = self.view(off, [128, 512], F32)
        otd, rcd = Dep(), Dep()
        mtmp, off = self.view(off, [128, 512], F32)
        mfox, off = self.view(off, [128, 4, 512], BF16)
        mswa, off = self.view(off, [128, 5, 512], BF16)
        md = Dep()
        wo, off = self.view(off, [128, 4, 1024], BF16)
        wod = Dep()
        sm, off = self.view(off, [128, 32], F32)
        smd = Dep()
        sel, off = self.view(off, [128, 64], F32)
        hg = sm[:, 0:4]
        esink = sm[:, 4:12]
        negfb = sm[:, 12:20]
        onef = sm[:, 20:21]

        Sx.op("pool", lambda e: e.memset(vaug[:, :, 64:66], 1.0), writes=[vd])
        Sx.op("pool", lambda e: e.memset(onesrow[:], 1.0), writes=[nld])
        Sx.op("pool", lambda e: e.memset(kh[64:128, :], 0.0), writes=[kd])
        Sx.op("pool", lambda e: e.memset(kh[64:65, :], 1.0), writes=[kd])
        Sx.op("pool", lambda e: e.memset(kh[96:97, :], 1.0), writes=[kd])
        Sx.op("pool", lambda e: e.memset(wfl128[:], 0.0), writes=[wfd])
        Sx.op("pool", lambda e: e.memset(sm[:, 20:21], 1.0), writes=[smd])
        Sx.op("pool", lambda e: e.memset(sel[:], 0.0), writes=[smd])
        Sx.op("pool", lambda e: e.memset(sel[64:65, :], 1.0), writes=[smd])
        for j in range(4):
            Sx.op("pool", lambda e: e.memset(mtmp[:], 0.0), writes=[md])
            Sx.op("pool", lambda e, j=j: e.affine_select(out=mtmp[:], in_=mtmp[:], pattern=[[1, 512]],
                                                         compare_op=ALU.is_ge, fill=NEG, base=-128 * j,
                                                         channel_multiplier=-1), writes=[md])
            Sx.op("pool", lambda e, j=j: e.tensor_copy(out=mfox[:, j, :], in_=mtmp[:]), writes=[md])
        for jj in range(5):
            j = jj - 1
            Sx.op("pool", lambda e: e.memset(mtmp[:], 0.0), writes=[md])
            Sx.op("pool", lambda e, j=j: e.affine_select(out=mtmp[:], in_=mtmp[:], pattern=[[1, 512]],
                                                         compare_op=ALU.is_ge, fill=NEG, base=-128 * j,
                                                         channel_multiplier=-1), writes=[md])
            Sx.op("pool", lambda e, j=j: e.affine_select(out=mtmp[:], in_=mtmp[:], pattern=[[-1, 512]],
                                                         compare_op=ALU.is_ge, fill=NEG, base=127 + 128 * j,
                                                         channel_multiplier=1), writes=[md])
            Sx.op("pool", lambda e, jj=jj: e.tensor_copy(out=mswa[:, jj, :], in_=mtmp[:]), writes=[md])
        for n, name in enumerate(("swa_q_norm", "swa_k_norm", "fox_q_norm", "fox_k_norm")):
            self.small_dma("asm", sm[0:64, n:n + 1], self.din[name][i].rearrange("(d o) -> d o", o=1), [smd])
        self.small_dma("asm", sm[0:64, 4:12], self.din["swa_sinks"][i:i + 1, :].to_broadcast([64, 8]), [smd])
        self.small_dma("asm", sm[:, 12:20], self.din["attn_forget_bias"][i:i + 1, :].to_broadcast([128, 8]), [smd])
        Sx.op("act", lambda e: e.activation(out=sm[0:64, 4:12], in_=sm[0:64, 4:12], func=AF.Exp),
              reads=[smd], writes=[smd])
        Sx.op("dve", lambda e: e.tensor_scalar(out=sm[:, 12:20], in0=sm[:, 12:20], scalar1=-1.0, scalar2=None,
                                               op0=ALU.mult), reads=[smd], writes=[smd])
        Sx.dma("wfl", wfl, win[:, :, 2304:2312], writes=[wfd], qname="pool")

        B_ST, B_OT, B_OP = (0, 1, 4, 6), (2, 3), 7
        RAW, SUM = (4, 6), (5, 7)
        cnt = {"st": 0, "ot": 0, "raw": 0, "sum": 0, "sc": 0}
        sqs_ = [self.sq[0:64, 0, :], self.sq[0:64, 1, :]]
        sqd_ = [Dep(), Dep()]
        rss_ = [self.rstd[0:64, :], self.sg[0][0:64, :]]
        rsd_ = [Dep(), Dep()]

        def raw_bank():
            b_ = RAW[cnt["raw"] % 2]
            cnt["raw"] += 1
            return b_

        def sum_bank():
            b_ = SUM[cnt["sum"] % 2]
            cnt["sum"] += 1
            return b_

        def proj_norm(wst, wdep, gcol, dst, ddep):
            for tb in range(4):
                norm_tb(wst, wdep, gcol, dst, ddep, tb)

        def norm_tb(wst, wdep, gcol, dst, ddep, tb):
            ts = slice(tb * 512, (tb + 1) * 512)
            pj = raw_bank()
            ps = self.bank[pj]
            si = cnt["sc"] % 2
            cnt["sc"] += 1
            sq_, sqdd = sqs_[si], sqd_[si]
            rs_, rsdd = rss_[si], rsd_[si]
            for kc in range(8):
                Sx.op("pe", lambda e, kc=kc: e.matmul(ps[0:64, :], wst[:, kc, :], self.xn[:, kc, ts],
                                                      start=(kc == 0), stop=(kc == 7)),
                      reads=[wdep, self.xdep[tb]], writes=[self.bdep[pj]], signal=(kc == 7))
            Sx.op("act", lambda e: e.activation(out=sq_, in_=ps[0:64, :], func=AF.Square),
                  reads=[self.bdep[pj]], writes=[sqdd])
            sb_ = sum_bank()
            p2 = self.bank[sb_]
            Sx.op("pe", lambda e: e.matmul(p2[0:64, :], self.ones_b[0:64, 0:64], sq_, start=True, stop=True),
                  reads=[sqdd, self.cdep], writes=[self.bdep[sb_]])
            Sx.op("act", lambda e: e.activation(out=rs_, in_=p2[0:64, :], func=AF.Sqrt, bias=EPS, scale=1.0 / 64),
                  reads=[self.bdep[sb_]], writes=[rsdd])
            Sx.op("dve", lambda e: e.reciprocal(out=rs_, in_=rs_), reads=[rsdd], writes=[rsdd])
            Sx.op("dve", lambda e: e.scalar_tensor_tensor(
                out=dst[0:64, ts], in0=ps[0:64, :], scalar=sm[0:64, gcol:gcol + 1], in1=rs_,
                op0=ALU.mult, op1=ALU.mult),
                reads=[self.bdep[pj], rsdd, smd], writes=[ddep])

        def proj_v():
            for t4 in range(4):
                v_t4(t4)

        def v_t4(t4):
            pj = raw_bank()
            ps = self.bank[pj]
            for tl in range(4):
                t = t4 * 4 + tl
                for kc in range(8):
                    Sx.op("pe", lambda e, kc=kc, t=t, tl=tl: e.matmul(
                        ps[:, tl * 64:(tl + 1) * 64], self.xn[:, kc, t * 128:(t + 1) * 128], wv[:, kc, :],
                        start=(kc == 0), stop=(kc == 7)),
                        reads=[wvd, self.xdep[t // 4]], writes=[self.bdep[pj]],
                        signal=(kc == 7 and tl == 3))
            Sx.op("act", lambda e: e.activation(
                out=vaug[:, t4 * 4:(t4 + 1) * 4, 0:64], in_=ps[:, 0:256].rearrange("p (a b) -> p a b", a=4),
                func=AF.Copy), reads=[self.bdep[pj]], writes=[vd])

        def fox_gates(h):
            G = slice(64, 128)
            for cpos in (64, 96):
                Sx.op("dve", lambda e, cpos=cpos: e.tensor_copy(out=wfl128[:, :, cpos:cpos + 1], in_=wfl[:, :, h:h + 1]),
                      reads=[wfd], writes=[wfd])
            for tb in range(4):
                ts = slice(tb * 512, (tb + 1) * 512)
                pj = raw_bank()
                ps = self.bank[pj]
                for kc in range(8):
                    Sx.op("pe", lambda e, kc=kc, ts=ts, ps=ps: e.matmul(ps[:], wfl128[:, kc, :], self.xn[:, kc, ts],
                                                                         start=(kc == 0), stop=(kc == 7)),
                          reads=[wfd, self.xdep[tb]], writes=[self.bdep[pj]], signal=(kc == 7))
                Sx.op("act", lambda e, ps=ps, ts=ts: e.activation(out=nl[G, ts], in_=ps[G, :], func=AF.Exp,
                                                                  bias=sm[G, 12 + h:13 + h], scale=-1.0),
                      reads=[self.bdep[pj], smd], writes=[nld])
            Sx.op("act", lambda e: e.activation(out=nl[G, :], in_=nl[G, :], func=AF.Ln, bias=1.0, scale=1.0),
                  reads=[nld], writes=[nld])
            for c4 in range(4):
                ts = slice(c4 * 512, (c4 + 1) * 512)
                init = 0.0 if c4 == 0 else nl[G, c4 * 512 - 1:c4 * 512]
                Sx.op("dve", lambda e, ts=ts, init=init: e.tensor_tensor_scan(
                    out=nl[G, ts], data0=onesrow[G, 0:512], data1=nl[G, ts], initial=init, op0=ALU.mult, op1=ALU.add),
                    reads=[nld], writes=[nld])
            Sx.op("dve", lambda e: e.tensor_scalar(out=qh[G, :], in0=nl[G, :], scalar1=-8.0, scalar2=None,
                                                   op0=ALU.mult), reads=[nld], writes=[qd])
            Sx.op("dve", lambda e: e.scalar_tensor_tensor(out=qh[96:97, :], in0=nl[96:97, :], scalar=-8.0,
                                                          in1=qh[96:97, :], op0=ALU.mult, op1=ALU.subtract),
                  reads=[nld, qd], writes=[qd])
            pj = raw_bank()
            ps = self.bank[pj]
            for t in range(16):
                Sx.op("pe", lambda e, t=t, ps=ps: e.matmul(ps[:, t:t + 1], nl[64:65, t * 128:(t + 1) * 128], sm[64:65, 20:21],
                                                           start=True, stop=True),
                      reads=[nld, smd], writes=[self.bdep[pj]], signal=(t == 15))
            Sx.op("dve", lambda e, ps=ps: e.tensor_copy(out=gtok[:], in_=ps[:, 0:16]),
                  reads=[self.bdep[pj]], writes=[gtd])

        def attention(pairs, fox, hh, grp, h):
            for qb in range(4):
                do_qb(pairs, fox, hh, grp, h, qb)

        def do_qb(pairs, fox, hh, grp, h, qb):
            if True:
                qs = slice(qb * 512, (qb + 1) * 512)
                plist = pairs[qb]
                ob = B_OT[cnt["ot"] % 2]
                cnt["ot"] += 1
                ot = self.bank[ob]
                pendq = []

                def emit_pv(p):
                    idx, kt, pi = p
                    Sx.op("pe", lambda e, kt=kt, pi=pi: e.matmul(ot[0:65, :], vaug[:, kt, 0:65], pb[pi][:],
                                                                  start=(idx == 0), stop=(idx == len(plist) - 1)),
                          reads=[vd, pbd[pi]], writes=[self.bdep[ob]], signal=True)

                for idx, (kt, mask) in enumerate(plist):
                    sb_ = B_ST[cnt["st"] % 4]
                    pi = cnt["st"] % 4
                    cnt["st"] += 1
                    st_ = self.bank[sb_]
                    last = "qk" if mask is None else "mask"
                    KR = 97 if fox else 64
                    Sx.op("pe", lambda e, kt=kt, st_=st_, last=last, KR=KR: e.matmul(
                        st_[:], kh[0:KR, kt * 128:(kt + 1) * 128], qh[0:KR, qs], start=True, stop=(last == "qk")),
                          reads=[kd, qd], writes=[self.bdep[sb_]], signal=(last == "qk"))
                    if mask is not None:
                        Sx.op("pe", lambda e, st_=st_, mask=mask: e.matmul(st_[:], self.ident_b[:], mask,
                                                                            start=False, stop=True),
                              reads=[md, self.cdep], writes=[self.bdep[sb_]], signal=True)
                    if fox:
                        Sx.op("act", lambda e, st_=st_, pi=pi, kt=kt: e.activation(
                            out=pb[pi][:], in_=st_[:], func=AF.Exp, bias=gtok[:, kt:kt + 1], scale=0.125),
                            reads=[self.bdep[sb_], gtd], writes=[pbd[pi]])
                    else:
                        Sx.op("act", lambda e, st_=st_, pi=pi: e.activation(
                            out=pb[pi][:], in_=st_[:], func=AF.Exp, scale=0.125),
                            reads=[self.bdep[sb_]], writes=[pbd[pi]])
                    pendq.append((idx, kt, pi))
                    if len(pendq) > 3:
                        emit_pv(pendq.pop(0))
                while pendq:
                    emit_pv(pendq.pop(0))
                Sx.op("act", lambda e: e.activation(out=otsb[0:65, :], in_=ot[0:65, :], func=AF.Copy),
                      reads=[self.bdep[ob]], writes=[otd])
                B_SS = sum_bank()
                p2 = self.bank[B_SS]
                Sx.op("pe", lambda e, p2=p2: e.matmul(p2[0:64, :], sel[0:65, 0:64], otsb[0:65, :], start=True, stop=True),
                      reads=[otd, smd], writes=[self.bdep[B_SS]])
                if fox:
                    Sx.op("dve", lambda e, p2=p2: e.reciprocal(out=rc[0:64, :], in_=p2[0:64, :]),
                          reads=[self.bdep[B_SS]], writes=[rcd])
                else:
                    Sx.op("dve", lambda e, p2=p2: e.tensor_scalar(out=rc[0:64, :], in0=p2[0:64, :],
                                                                   scalar1=sm[0:64, 4 + h:5 + h], scalar2=None,
                                                                   op0=ALU.add),
                          reads=[self.bdep[B_SS], smd], writes=[rcd])
                    Sx.op("dve", lambda e: e.reciprocal(out=rc[0:64, :], in_=rc[0:64, :]), reads=[rcd], writes=[rcd])
                Sx.op("dve", lambda e, hh=hh: e.tensor_tensor(out=og[0:64, hh, qs], in0=otsb[0:64, :], in1=rc[0:64, :],
                                                              op=ALU.mult),
                      reads=[otd, rcd], writes=[ogd[hh][qb]])

        fox_pairs, swa_pairs = [], []
        for qb in range(4):
            fp = [(kt, None) for kt in range(4 * qb)] + [(4 * qb + j, mfox[:, j, :]) for j in range(4)]
            fox_pairs.append(fp)
            sp_ = [(4 * qb + j, mswa[:, j + 1, :]) for j in range(-1, 4) if 4 * qb + j >= 0]
            swa_pairs.append(sp_)

        for grp in range(4):
            fox = grp >= 2
            Sx.dma("wo", wo[0:64], wout[grp * 256:(grp + 1) * 256, :].rearrange("(h p) d -> p h d", p=64),
                   writes=[wod], qname="pool")
            for hh in range(4):
                h = (grp % 2) * 4 + hh
                if fox:
                    qc, kc_, vc = 768 + h * 64, 1280 + h * 64, 1792 + h * 64
                else:
                    g = h // 4
                    qc, kc_, vc = h * 64, 512 + g * 64, 640 + g * 64
                new_kv = fox or hh == 0
                Sx.dma("wq", wq, win[:, :, qc:qc + 64], writes=[wqd], qname="pool")
                if new_kv:
                    Sx.dma("wk", wk, win[:, :, kc_:kc_ + 64], writes=[wkd], qname="pool")
                    Sx.dma("wv", wv, win[:, :, vc:vc + 64], writes=[wvd], qname="pool")
                proj_norm(wq, wqd, 2 if fox else 0, qh, qd)
                if new_kv:
                    proj_norm(wk, wkd, 3 if fox else 1, kh, kd)
                    proj_v()
                if fox:
                    fox_gates(h)
                attention(fox_pairs if fox else swa_pairs, fox, hh, grp, h)
            for dc in range(8):
                for tb in range(4):
                    ts = slice(tb * 512, (tb + 1) * 512)
                    ps = self.bank[B_OP]
                    for hh in range(4):
                        Sx.op("pe", lambda e, hh=hh, dc=dc, ts=ts, ps=ps: e.matmul(
                            ps[:], wo[0:64, hh, dc * 128:(dc + 1) * 128], og[0:64, hh, ts],
                            start=(hh == 0), stop=(hh == 3)),
                            reads=[wod, ogd[hh][tb]], writes=[self.bdep[B_OP]], signal=(hh == 3))
                    Sx.op("dve", lambda e, dc=dc, ts=ts, ps=ps: e.tensor_tensor(
                        out=self.hT[:, dc, ts], in0=ps[:], in1=self.hT[:, dc, ts], op=ALU.add),
                        reads=[self.bdep[B_OP], self.hdep[dc][tb]], writes=[self.hdep[dc][tb]])

    def ssm(self, i, layer):
        Sx = self.S
        self.rmsnorm(4 + layer)
        Sx.barrier()
        win = self.din["ssm_w_in"][i].rearrange("(kc p) f -> p kc f", p=128)
        wout = self.din["ssm_w_out"][i]
        V = self.view
        off = 0
        hst, off = V(off, [128, 32, 64], F32)
        hb8, off = V(off, [128, 8, 64], BF16)
        zs8, off = V(off, [128, 8, 512], BF16)
        yg, off = V(off, [128, 8, 512], BF16)
        xs8 = self.sq
        xraw = []
        for _ in range(2):
            a_, off = V(off, [128, 516], F32)
            xraw.append(a_)
        bcraw = xraw[0]
        cacc = self.sg
        tt = []
        for _ in range(2):
            a_, off = V(off, [128, 4, 128], F32)
            tt.append(a_)
        sqs = []
        for _ in range(2):
            a_, off = V(off, [128, 4, 128], BF16)
            sqs.append(a_)
        xdt8, off = V(off, [128, 4, 8, 64], BF16)
        xddt8, off = V(off, [128, 4, 8, 64], BF16)
        Bf, off = V(off, [128, 512], BF16)
        Cf, off = V(off, [128, 512], BF16)
        Btok, off = V(off, [128, 4, 128], BF16)
        CBT, off = V(off, [128, 512], F32)
        dtt, off = V(off, [128, 512], F32)
        ad, off = V(off, [128, 512], F32)
        dtd, off = V(off, [128, 512], F32)
        cdb, off = V(off, [128, 512], F32)
        wx, wz = [], []
        for _ in range(2):
            a_, off = V(off, [128, 8, 64], BF16)
            wx.append(a_)
            a_, off = V(off, [128, 8, 64], BF16)
            wz.append(a_)
        wB, off = V(off, [128, 8, 128], BF16)
        wC, off = V(off, [128, 8, 128], BF16)
        wdt, off = V(off, [128, 8, 32], BF16)
        wo = []
        for _ in range(2):
            a_, off = V(off, [128, 8, 128], BF16)
            wo.append(a_)
        R8, off = V(off, [128, 8, 128], F32)
        acs = R8[:, 0:4, :].rearrange("p a b -> p (a b)")
        A8, off = V(off, [128, 8, 128], F32)
        dec2, dfs2 = [], []
        for _ in range(2):
            a_, off = V(off, [128, 4, 128], F32)
            dec2.append(a_)
            a_, off = V(off, [128, 4, 128], F32)
            dfs2.append(a_)
        hd = {k: [Dep(), Dep()] for k in ("dec", "dfs", "MT", "Cs")}
        MT8, off = V(off, [128, 8, 128], BF16)
        Cs8, off = V(off, [128, 8, 128], BF16)
        trif, off = V(off, [128, 128], F32)
        ntrif, off = V(off, [128, 128], F32)
        onesf, off = V(off, [128, 128], F32)
        sel127, off = V(off, [128, 128], F32)
        negm4, off = V(off, [128, 4, 128], BF16)
        cwx, off = V(off, [128, 32, 4], F32)
        cbx, off = V(off, [128, 32], F32)
        cwbc, off = V(off, [128, 8, 4], F32)
        cbbc, off = V(off, [128, 8], F32)
        dsk, off = V(off, [128, 32], F32)
        ngn, off = V(off, [128, 32], F32)
        ab, off = V(off, [128, 32], F32)
        dtb, off = V(off, [128, 32], F32)
        carx, off = V(off, [128, 32, 4], F32)
        carbc, off = V(off, [128, 8, 4], F32)
        rsb = self.rstd
        self._ssm_off = off
        names = ("bcraw", "xdt", "xddt", "Bf", "Cf", "Btok", "CBT", "lay", "wB", "wC", "wdt", "R", "A", "dec", "dfs",
                 "MT", "Cs", "cst", "par", "carx", "carbc", "rsb", "hb8", "hst")
        d = {k: Dep() for k in names}
        xrd = [Dep(), Dep()]
        cad = [Dep(), Dep()]
        ttd = [Dep(), Dep()]
        sqd = [Dep(), Dep()]
        wxd = [Dep(), Dep()]
        wzd = [Dep(), Dep()]
        wod = [Dep(), Dep()]
        zsd = [Dep() for _ in range(8)]
        xsd = [Dep() for _ in range(8)]
        ygd = [Dep() for _ in range(8)]
        cw_src = self.din["ssm_conv_w"][i]
        cb_src = self.din["ssm_conv_b"][i]
        P = lambda fn, w: Sx.op("pool", fn, writes=w)
        P(lambda e: e.memset(hst[:], 0.0), [d["hst"]])
        P(lambda e: e.memset(carx[:], 0.0), [d["carx"]])
        P(lambda e: e.memset(carbc[:], 0.0), [d["carbc"]])
        P(lambda e: e.memset(onesf[:], 1.0), [d["cst"]])
        P(lambda e: e.memset(trif[:], 1.0), [d["cst"]])
        P(lambda e: e.affine_select(out=trif[:], in_=trif[:], pattern=[[1, 128]], compare_op=ALU.is_ge, fill=0.0,
                                    base=0, channel_multiplier=-1), [d["cst"]])
        P(lambda e: e.memset(ntrif[:], -1.0), [d["cst"]])
        P(lambda e: e.affine_select(out=ntrif[:], in_=ntrif[:], pattern=[[1, 128]], compare_op=ALU.is_ge, fill=0.0,
                                    base=0, channel_multiplier=-1), [d["cst"]])
        P(lambda e: e.memset(sel127[:], 0.0), [d["cst"]])
        P(lambda e: e.affine_select(out=sel127[:], in_=sel127[:], pattern=[[0, 128]], compare_op=ALU.not_equal,
                                    fill=1.0, base=-127, channel_multiplier=1), [d["cst"]])
        Sx.op("dve", lambda e: e.tensor_scalar(out=negm4[:], in0=trif[:].unsqueeze(1).to_broadcast([128, 4, 128]),
                                               scalar1=-NEG, scalar2=NEG, op0=ALU.mult, op1=ALU.add),
              reads=[d["cst"]], writes=[d["cst"]])
        sd = lambda out, in_: self.small_dma("ssmp", out, in_, [d["par"]])
        for k in range(4):
            sd(cwx[0:64, :, k], cw_src[k, 0:2048].rearrange("(h p) -> p h", p=64))
            sd(cwbc[:, :, k], cw_src[k, 2048:3072].rearrange("(j p) -> p j", p=128))
        sd(cbx[0:64], cb_src[0:2048].rearrange("(h p) -> p h", p=64))
        sd(cbbc[:], cb_src[2048:3072].rearrange("(j p) -> p j", p=128))
        sd(dsk[:], self.din["ssm_d_skip"][i:i + 1, :].to_broadcast([128, 32]))
        sd(ngn[0:64], self.din["ssm_norm"][i].rearrange("(h p) -> p h", p=64))
        sd(ab[:], self.din["ssm_a_log"][i:i + 1, :].to_broadcast([128, 32]))
        sd(dtb[:], self.din["ssm_dt_bias"][i:i + 1, :].to_broadcast([128, 32]))
        Sx.op("act", lambda e: e.activation(out=ab[:], in_=ab[:], func=AF.Exp), reads=[d["par"]], writes=[d["par"]])
        Sx.op("dve", lambda e: e.tensor_scalar(out=ab[:], in0=ab[:], scalar1=-1.0, scalar2=None, op0=ALU.mult),
              reads=[d["par"]], writes=[d["par"]])
        Sx.dma("wdt", wdt, win[:, :, 5120:5152], writes=[d["wdt"]], qname="pool")
        B_PJ, B_D, B_Y, B_ST, B_SS, B_TR = (0, 1), (2, 3), (4, 5), 6, 7, 6
        B_MS = 7
        bk, bd = self.bank, self.bdep
        ps = bk[B_MS]
        for C in range(16):
            for kc in range(8):
                Sx.op("pe", lambda e, C=C, kc=kc: e.matmul(ps[:, C * 32:(C + 1) * 32], self.xn[:, kc, C * 128:(C + 1) * 128],
                                                           wdt[:, kc, :], start=(kc == 0), stop=(kc == 7)),
                      reads=[d["wdt"], self.xdep[C // 4]], writes=[bd[B_MS]], signal=(kc == 7 and C == 15))
        for C in range(16):
            Sx.op("dve", lambda e, C=C: e.tensor_tensor(out=dtt[:, C * 32:(C + 1) * 32], in0=ps[:, C * 32:(C + 1) * 32],
                                                        in1=dtb[:], op=ALU.add),
                  reads=[bd[B_MS], d["par"]], writes=[d["lay"]])
        Sx.op("act", lambda e: e.activation(out=dtt[:], in_=dtt[:], func=AF.Exp), reads=[d["lay"]], writes=[d["lay"]])
        Sx.op("act", lambda e: e.activation(out=dtt[:], in_=dtt[:], func=AF.Ln, bias=1.0, scale=1.0),
              reads=[d["lay"]], writes=[d["lay"]])
        for C in range(16):
            Sx.op("dve", lambda e, C=C: e.tensor_tensor(out=ad[:, C * 32:(C + 1) * 32], in0=dtt[:, C * 32:(C + 1) * 32],
                                                        in1=ab[:], op=ALU.mult),
                  reads=[d["lay"], d["par"]], writes=[d["lay"]])
        Sx.op("pe", lambda e: e.matmul(ps[:], trif[:], ad[:], start=True, stop=True),
              reads=[d["lay"], d["cst"]], writes=[bd[B_MS]])
        Sx.op("act", lambda e: e.activation(out=acs[:], in_=ps[:], func=AF.Copy), reads=[bd[B_MS]], writes=[d["lay"]])
        Sx.op("pe", lambda e: e.matmul(ps[:], sel127[:], acs[:], start=True, stop=True),
              reads=[d["lay"], d["cst"]], writes=[bd[B_MS]])
        Sx.op("act", lambda e: e.activation(out=cdb[:], in_=ps[:], func=AF.Exp), reads=[bd[B_MS]], writes=[d["lay"]])
        Sx.op("dve", lambda e: e.tensor_tensor(out=dtd[:], in0=ps[:], in1=acs[:], op=ALU.subtract),
              reads=[bd[B_MS], d["lay"]], writes=[d["lay"]])
        Sx.op("act", lambda e: e.activation(out=dtd[:], in_=dtd[:], func=AF.Exp), reads=[d["lay"]], writes=[d["lay"]])
        Sx.op("dve", lambda e: e.tensor_tensor(out=dtd[:], in0=dtd[:], in1=dtt[:], op=ALU.mult),
              reads=[d["lay"]], writes=[d["lay"]])
        Sx.barrier()
        cnt = {"pj": 0, "x": 0, "ep": 0, "wo": 0}

        def conv_silu(raw, rdep, np_, wsl, bsl, car, cdep, out, odep, pb_):
            ca = cacc[cnt["x"] % 2]
            cd_ = cad[cnt["x"] % 2]
            cnt["x"] += 1
            Sx.op("act", lambda e: e.activation(out=raw[0:np_, 4:516], in_=bk[pb_][0:np_, :], func=AF.Copy),
                  reads=[bd[pb_]], writes=[rdep])
            Sx.op("dve", lambda e: e.tensor_copy(out=raw[0:np_, 1:4], in_=car[:, 0:3]), reads=[cdep], writes=[rdep])
            Sx.op("act", lambda e: e.activation(out=ca[0:np_, :], in_=raw[0:np_, 1:513], func=AF.Identity,
                                                bias=bsl, scale=wsl[:, 0:1]),
                  reads=[rdep, d["par"]], writes=[cd_])
            for k in range(1, 4):
                Sx.op("dve", lambda e, k=k: e.scalar_tensor_tensor(out=ca[0:np_, :], in0=raw[0:np_, 1 + k:513 + k],
                                                                   scalar=wsl[:, k:k + 1], in1=ca[0:np_, :],
                                                                   op0=ALU.mult, op1=ALU.add),
                      reads=[rdep, cd_, d["par"]], writes=[cd_])
            Sx.op("dve", lambda e: e.tensor_copy(out=car[:, 0:3], in_=raw[0:np_, 513:516]), reads=[rdep], writes=[cdep])
            Sx.op("act", lambda e: e.activation(out=out, in_=ca[0:np_, :], func=AF.Silu), reads=[cd_], writes=[odep])

        def proj(wst, wdep, np_, tb):
            ts = slice(tb * 512, (tb + 1) * 512)
            pb_ = B_PJ[cnt["pj"] % 2]
            cnt["pj"] += 1
            for kc in range(8):
                Sx.op("pe", lambda e, kc=kc: e.matmul(bk[pb_][0:np_, :], wst[:, kc, :], self.xn[:, kc, ts],
                                                      start=(kc == 0), stop=(kc == 7)),
                      reads=[wdep, self.xdep[tb]], writes=[bd[pb_]], signal=(kc == 7))
            return pb_

        def head_prep(tb, g, r):
            h = g * 8 + r
            wi = h % 2
            Sx.dma("wz%d" % wi, wz[wi], win[:, :, h * 64:(h + 1) * 64], writes=[wzd[wi]], qname="pool")
            Sx.dma("wx%d" % wi, wx[wi], win[:, :, 2048 + h * 64:2048 + (h + 1) * 64], writes=[wxd[wi]], qname="pool")
            pb_ = proj(wz[wi], wzd[wi], 64, tb)
            Sx.op("act", lambda e, pb_=pb_: e.activation(out=zs8[0:64, r, :], in_=bk[pb_][0:64, :], func=AF.Silu),
                  reads=[bd[pb_]], writes=[zsd[r]])
            pb_ = proj(wx[wi], wxd[wi], 64, tb)
            conv_silu(xraw[wi], xrd[wi], 64, cwx[0:64, h, :], cbx[0:64, h:h + 1], carx[0:64, h, :], d["carx"],
                      xs8[0:64, r, :], xsd[r], pb_)

        DBG = int(os.environ.get('SSM_DBG', '99'))

        def chunk(tb, g, cl):
            C = tb * 4 + cl
            c0 = C * 32 + g * 8
            cs_ = slice(cl * 128, (cl + 1) * 128)
            for r in range(8):
                Sx.op("pe", lambda e, r=r: e.matmul(bk[B_TR][:, r * 64:(r + 1) * 64], xs8[0:64, r, cs_],
                                                    self.ident_b[0:64, 0:64], start=True, stop=True),
                      reads=[xsd[r], self.cdep], writes=[bd[B_TR]], signal=(r == 7))
            trv = bk[B_TR][:].rearrange("p (a b) -> p a b", a=8)
            Sx.op("dve", lambda e: e.tensor_tensor(out=xdt8[:, cl, :, :], in0=trv,
                                                   in1=dtt[:, c0:c0 + 8].unsqueeze(2).to_broadcast([128, 8, 64]),
                                                   op=ALU.mult), reads=[bd[B_TR], d["lay"]], writes=[d["xdt"]])
            Sx.op("dve", lambda e: e.tensor_tensor(out=xddt8[:, cl, :, :], in0=trv,
                                                   in1=dtd[:, c0:c0 + 8].unsqueeze(2).to_broadcast([128, 8, 64]),
                                                   op=ALU.mult), reads=[bd[B_TR], d["lay"]], writes=[d["xddt"]])
            adb = ad[:, c0:c0 + 8].unsqueeze(2).to_broadcast([128, 8, 128])
            Sx.op("dve", lambda e: e.tensor_tensor(out=R8[:], in0=trif[:].unsqueeze(1).to_broadcast([128, 8, 128]),
                                                   in1=adb, op=ALU.mult),
                  reads=[d["cst"], d["lay"]], writes=[d["R"]])
            Sx.op("act", lambda e: e.activation(out=A8[:], in_=adb, func=AF.Copy), reads=[d["lay"]], writes=[d["A"]])
            def decay_half(hf):
                db = B_D[hf]
                hs = slice(hf * 4, hf * 4 + 4)
                dec, dfs = dec2[hf], dfs2[hf]
                dv = bk[db][:].rearrange("p (a b) -> p a b", a=4)
                Sx.op("pe", lambda e: e.matmul(dv, onesf[:], R8[:, hs, :], start=True, stop=True),
                      reads=[d["R"], d["cst"]], writes=[bd[db]])
                Sx.op("act", lambda e: e.activation(out=dfs[:], in_=dv, func=AF.Exp), reads=[bd[db]], writes=[hd["dfs"][hf]])
                Sx.op("pe", lambda e: e.matmul(dv, ntrif[:], A8[:, hs, :], start=False, stop=False,
                                               skip_group_check=True),
                      reads=[d["A"], d["cst"]], writes=[bd[db]], signal=False)
                Sx.op("pe", lambda e: e.matmul(dv, self.ident_b[:], negm4[:], start=False, stop=True,
                                               skip_group_check=True),
                      reads=[d["cst"], self.cdep], writes=[bd[db]])
                Sx.op("act", lambda e: e.activation(out=dec[:], in_=dv, func=AF.Exp), reads=[bd[db]], writes=[hd["dec"][hf]])
                Sx.op("dve", lambda e: e.tensor_tensor(
                    out=MT8[:, hs, :], in0=dec[:], in1=CBT[:, cs_].unsqueeze(1).to_broadcast([128, 4, 128]),
                    op=ALU.mult), reads=[hd["dec"][hf], d["CBT"]], writes=[hd["MT"][hf]])
                Sx.op("dve", lambda e: e.tensor_tensor(out=Cs8[:, hs, :], in0=dfs[:],
                                                       in1=Cf[:, cs_].unsqueeze(1).to_broadcast([128, 4, 128]),
                                                       op=ALU.mult),
                      reads=[hd["dfs"][hf], d["Cf"]], writes=[hd["Cs"][hf]])
            for hf in range(2):
                decay_half(hf)

            def y_half(hf):
                yb = B_Y[hf]
                for rr in range(4):
                    r = hf * 4 + rr
                    Sx.op("pe", lambda e, r=r, rr=rr: e.matmul(bk[yb][0:64, rr * 128:(rr + 1) * 128], xdt8[:, cl, r, :],
                                                               MT8[:, r, :], start=True, stop=False),
                          reads=[d["xdt"], hd["MT"][hf]], writes=[bd[yb]], signal=False)
                    Sx.op("pe", lambda e, r=r, rr=rr: e.matmul(bk[yb][0:64, rr * 128:(rr + 1) * 128], hb8[:, r, :],
                                                               Cs8[:, r, :], start=False, stop=True),
                          reads=[d["hb8"], hd["Cs"][hf]], writes=[bd[yb]], signal=(rr == 3))
            for hf in range(2):
                y_half(hf)
            Sx.op("pe", lambda e: e.matmul(bk[B_ST][:], Btok[:, cl, :], xddt8[:, cl, :, :].rearrange("p a b -> p (a b)"),
                                           start=True, stop=True),
                  reads=[d["Btok"], d["xddt"]], writes=[bd[B_ST]])
            hv = hst[:, g * 8:(g + 1) * 8, :]
            Sx.op("dve", lambda e: e.tensor_tensor(out=hv, in0=hv,
                                                   in1=cdb[:, c0:c0 + 8].unsqueeze(2).to_broadcast([128, 8, 64]),
                                                   op=ALU.mult), reads=[d["lay"], d["hst"]], writes=[d["hst"]])
            Sx.op("dve", lambda e: e.tensor_tensor(out=hv, in0=bk[B_ST][:].rearrange("p (a b) -> p a b", a=8), in1=hv,
                                                   op=ALU.add), reads=[bd[B_ST], d["hst"]], writes=[d["hst"]])
            Sx.op("act", lambda e: e.activation(out=hb8[:], in_=hv, func=AF.Copy), reads=[d["hst"]], writes=[d["hb8"]])
            def epi_half(hf):
                yb = B_Y[hf]
                hs = slice(hf * 4, hf * 4 + 4)
                ei = cnt["ep"] % 2
                cnt["ep"] += 1
                t_, td_ = tt[ei], ttd[ei]
                q_, qd_ = sqs[ei], sqd[ei]
                h0 = g * 8 + hf * 4
                yv = bk[yb][0:64, :].rearrange("p (a b) -> p a b", a=4)
                Sx.op("dve", lambda e: e.tensor_tensor(out=t_[0:64], in0=xs8[0:64, hs, cs_],
                                                       in1=dsk[0:64, h0:h0 + 4].unsqueeze(2).to_broadcast([64, 4, 128]),
                                                       op=ALU.mult),
                      reads=[xsd[hf * 4 + k] for k in range(4)] + [d["par"]], writes=[td_])
                Sx.op("dve", lambda e: e.tensor_tensor(out=t_[0:64], in0=yv, in1=t_[0:64], op=ALU.add),
                      reads=[bd[yb], td_], writes=[td_])
                Sx.op("dve", lambda e: e.tensor_tensor(out=t_[0:64], in0=t_[0:64], in1=zs8[0:64, hs, cs_], op=ALU.mult),
                      reads=[td_] + [zsd[hf * 4 + k] for k in range(4)], writes=[td_])
                Sx.op("act", lambda e: e.activation(out=q_[0:64], in_=t_[0:64], func=AF.Square), reads=[td_], writes=[qd_])
                for rr in range(4):
                    first = (hf == 0 and rr == 0)
                    last = (hf == 1 and rr == 3)
                    Sx.op("pe", lambda e, rr=rr, first=first, last=last: e.matmul(
                        bk[B_SS][:, cs_], self.ones_b[0:64, :], q_[0:64, rr, :], start=first, stop=last,
                        skip_group_check=True),
                          reads=[qd_, self.cdep], writes=[bd[B_SS]], signal=(rr == 3))
                Sx.op("dve", lambda e: e.tensor_tensor(out=yg[0:64, hs, cs_], in0=t_[0:64],
                                                       in1=ngn[0:64, h0:h0 + 4].unsqueeze(2).to_broadcast([64, 4, 128]),
                                                       op=ALU.mult),
                      reads=[td_, d["par"]], writes=[ygd[hf * 4 + k] for k in range(4)])

            for hf in range(2):
                epi_half(hf)

        def group(tb, g):
            ts = slice(tb * 512, (tb + 1) * 512)
            Sx.dma("wB", wB, win[:, :, 4096 + g * 128:4096 + (g + 1) * 128], writes=[d["wB"]], qname="pool")
            Sx.dma("wC", wC, win[:, :, 4608 + g * 128:4608 + (g + 1) * 128], writes=[d["wC"]], qname="pool")
            pb_ = proj(wB, d["wB"], 128, tb)
            conv_silu(bcraw, xrd[0], 128, cwbc[:, g, :], cbbc[:, g:g + 1], carbc[:, g, :], d["carbc"], Bf[:], d["Bf"], pb_)
            pb_ = proj(wC, d["wC"], 128, tb)
            conv_silu(bcraw, xrd[0], 128, cwbc[:, 4 + g, :], cbbc[:, 4 + g:5 + g], carbc[:, 4 + g, :], d["carbc"],
                      Cf[:], d["Cf"], pb_)
            for cl in range(4):
                cs_ = slice(cl * 128, (cl + 1) * 128)
                Sx.op("pe", lambda e, cs_=cs_: e.matmul(bk[B_TR][:, cs_], Bf[:, cs_], self.ident_b[:], start=True, stop=True),
                      reads=[d["Bf"], self.cdep], writes=[bd[B_TR]], signal=(cl == 3))
            Sx.op("act", lambda e: e.activation(out=Btok[:], in_=bk[B_TR][:].rearrange("p (a b) -> p a b", a=4),
                                                func=AF.Copy), reads=[bd[B_TR]], writes=[d["Btok"]])
            for cl in range(4):
                cs_ = slice(cl * 128, (cl + 1) * 128)
                Sx.op("pe", lambda e, cs_=cs_: e.matmul(bk[B_TR][:, cs_], Bf[:, cs_], Cf[:, cs_], start=True, stop=True),
                      reads=[d["Bf"], d["Cf"]], writes=[bd[B_TR]], signal=(cl == 3))
            Sx.op("dve", lambda e: e.tensor_tensor(out=CBT[:].rearrange("p (a b) -> p a b", a=4),
                                                   in0=bk[B_TR][:].rearrange("p (a b) -> p a b", a=4),
                                                   in1=trif[:].unsqueeze(1).to_broadcast([128, 4, 128]), op=ALU.mult),
                  reads=[bd[B_TR], d["cst"]], writes=[d["CBT"]])
            Sx.op("act", lambda e: e.activation(out=hb8[:], in_=hst[:, g * 8:(g + 1) * 8, :], func=AF.Copy),
                  reads=[d["hst"]], writes=[d["hb8"]])
            for r in range(8):
                head_prep(tb, g, r)
            for cl in range(4):
                chunk(tb, g, cl)
            Sx.op("act", lambda e: e.activation(out=rsb[:], in_=bk[B_SS][:], func=AF.Sqrt, bias=EPS, scale=1.0 / 512),
                  reads=[bd[B_SS]], writes=[d["rsb"]])
            Sx.op("dve", lambda e: e.reciprocal(out=rsb[:], in_=rsb[:]), reads=[d["rsb"]], writes=[d["rsb"]])
            for dc in range(8):
                wi = cnt["wo"] % 2
                cnt["wo"] += 1
                Sx.dma("wo%d" % wi, wo[wi][0:64],
                       wout[g * 512:(g + 1) * 512, dc * 128:(dc + 1) * 128].rearrange("(h p) d -> p h d", p=64),
                       writes=[wod[wi]], qname="pool")
                ob = B_PJ[cnt["pj"] % 2]
                cnt["pj"] += 1
                for r in range(8):
                    Sx.op("pe", lambda e, r=r, wi=wi, ob=ob: e.matmul(bk[ob][:], wo[wi][0:64, r, :], yg[0:64, r, :],
                                                                      start=(r == 0), stop=(r == 7)),
                          reads=[wod[wi], ygd[r]], writes=[bd[ob]], signal=(r == 7))
                ei = cnt["ep"] % 2
                cnt["ep"] += 1
                t_ = tt[ei][:].rearrange("p a b -> p (a b)")
                td_ = ttd[ei]
                Sx.op("dve", lambda e, ob=ob, t_=t_: e.tensor_tensor(out=t_, in0=bk[ob][:], in1=rsb[:], op=ALU.mult),
                      reads=[bd[ob], d["rsb"]], writes=[td_])
                Sx.op("dve", lambda e, dc=dc, t_=t_: e.tensor_tensor(out=self.hT[:, dc, ts], in0=t_, in1=self.hT[:, dc, ts],
                                                                      op=ALU.add),
                      reads=[td_, self.hdep[dc][tb]], writes=[self.hdep[dc][tb]])

        for tb in range(4 if DBG >= 99 else 1):
            for g in range(4 if DBG >= 99 else 1):
                group(tb, g)


ALL_PHASES = []
for _l in range(4):
    ALL_PHASES.append(("ffn1", _l))
    ALL_PHASES.append(("attn" if _l % 2 == 0 else "ssm", _l))
    ALL_PHASES.append(("ffn2", _l))


def run(inputs, phases, n_cores=8, trace=False):
    nc = Builder(phases).build()
    x = np.ascontiguousarray(inputs["x"], dtype=np.float32)
    in_maps = []
    for c in range(n_cores):
        m = {"x": x[c]}
        for k in INPUT_SHAPES:
            m[k] = np.ascontiguousarray(inputs[k], dtype=np.float32)
        in_maps.append(m)
    res = run_bass_kernel_spmd(nc, in_maps, core_ids=list(range(n_cores)), trace=trace)
    out = np.stack([r["y"] for r in res.results], axis=0)
    return out, res


def kernel(**inputs):
    out, _ = run(inputs, ALL_PHASES, 8)
    return out.astype(np.float32)
```
